# Optimizing a Trainium2 kernel written in Bass

```python
import jax, jax.numpy as jnp
from jax import lax
import numpy as np

D_MODEL = 2048
BATCH = 2
SEQ = 8192
DEPTH = 2

N_MIXERS = 2
N_RET_LAYERS = (DEPTH + N_MIXERS - 1) // N_MIXERS
N_ATT_LAYERS = DEPTH // N_MIXERS

RET_HEADS = 8
RET_DK = D_MODEL // RET_HEADS
RET_DV = 2 * RET_DK
RET_QK_WIDTH = RET_HEADS * RET_DK
RET_V_WIDTH = RET_HEADS * RET_DV
RET_IN_WIDTH = 2 * RET_QK_WIDTH + 2 * RET_V_WIDTH
RET_CHUNK = 128
RET_ROT_BASE = 10000.0

ATT_GROUPS = ((128, 1), (512, 4), (2048, 16))
N_ATT_GROUPS = len(ATT_GROUPS)
ATT_HEAD_DIM = 128
ATT_HEADS = D_MODEL // ATT_HEAD_DIM
ATT_WIDTH = ATT_HEADS * ATT_HEAD_DIM
ATT_IN_WIDTH = 3 * N_ATT_GROUPS * ATT_WIDTH
ATT_BLOCK = 128
ROPE_DIM = ATT_HEAD_DIM // 4
ROPE_THETA = 500000.0

D_FF = 4 * D_MODEL
EPS = 1e-6

kernel_name = "hybrid_retention_dilated_attention_block"


def rms_norm(x, gain):
    x32 = x.astype(jnp.float32)
    y = x32 * lax.rsqrt(jnp.mean(x32 * x32, axis=-1, keepdims=True) + EPS)
    return (y * gain.astype(jnp.float32)).astype(x.dtype)


def head_rms(x):
    return x * lax.rsqrt(jnp.mean(x * x, axis=-1, keepdims=True) + EPS)


def rotate(x, pos, inv_freq):
    half = inv_freq.shape[0]
    ang = pos[:, None] * inv_freq[None, :]
    cos = jnp.cos(ang)[None, :, None, :]
    sin = jnp.sin(ang)[None, :, None, :]
    x1 = x[..., :half]
    x2 = x[..., half:2 * half]
    rest = x[..., 2 * half:]
    return jnp.concatenate([x1 * cos - x2 * sin, x2 * cos + x1 * sin, rest], axis=-1)


def retention_mixer(h, w_in, w_out, pos):
    B, S, _ = h.shape
    n_chunks = S // RET_CHUNK
    proj = (h @ w_in).astype(jnp.float32)
    q, k, v, g = jnp.split(proj, [RET_QK_WIDTH, 2 * RET_QK_WIDTH, 2 * RET_QK_WIDTH + RET_V_WIDTH], axis=-1)
    inv_freq = 1.0 / (RET_ROT_BASE ** jnp.linspace(0.0, 1.0, RET_DK // 2, dtype=jnp.float32))
    q = rotate(q.reshape(B, S, RET_HEADS, RET_DK), pos, inv_freq)
    k = rotate(k.reshape(B, S, RET_HEADS, RET_DK), pos, inv_freq) * (RET_DK ** -0.5)
    v = v.reshape(B, S, RET_HEADS, RET_DV)

    log_gamma = jnp.log(1.0 - 2.0 ** (-5.0 - jnp.arange(RET_HEADS, dtype=jnp.float32)))
    idx = jnp.arange(RET_CHUNK, dtype=jnp.float32)
    diff = idx[:, None] - idx[None, :]
    inner_decay = jnp.where(diff >= 0, jnp.exp(log_gamma[:, None, None] * jnp.maximum(diff, 0.0)), 0.0)
    query_decay = jnp.exp(log_gamma[None, :] * (idx[:, None] + 1.0))
    key_decay = jnp.exp(log_gamma[None, :] * (RET_CHUNK - 1.0 - idx[:, None]))
    chunk_decay = jnp.exp(log_gamma * RET_CHUNK)

    def to_chunks(t):
        return t.reshape(B, n_chunks, RET_CHUNK, *t.shape[2:]).swapaxes(0, 1)

    def step(state, qkv):
        qc, kc, vc = qkv
        scores = jnp.einsum('bihd,bjhd->bhij', qc, kc) * inner_decay
        inner = jnp.einsum('bhij,bjhv->bihv', scores, vc)
        cross = jnp.einsum('bihd,bhdv->bihv', qc, state) * query_decay[None, :, :, None]
        state = state * chunk_decay[None, :, None, None] + jnp.einsum(
            'bjhd,bjhv->bhdv', kc * key_decay[None, :, :, None], vc)
        return state, inner + cross

    state0 = jnp.zeros((B, RET_HEADS, RET_DK, RET_DV), jnp.float32)
    _, out = lax.scan(step, state0, (to_chunks(q), to_chunks(k), to_chunks(v)))
    out = out.swapaxes(0, 1).reshape(B, S, RET_HEADS, RET_DV)
    out = head_rms(out).reshape(B, S, RET_V_WIDTH) * jax.nn.silu(g)
    return out.astype(h.dtype) @ w_out


def dilated_window_attention(q, k, v, dilation, steps):
    B, S, H, Dh = q.shape
    L = S // dilation
    n_blk = -(-L // ATT_BLOCK)
    Lp = n_blk * ATT_BLOCK

    def to_blocks(t):
        t = t.reshape(B, L, dilation, H, Dh).swapaxes(1, 2).reshape(B * dilation, L, H, Dh)
        t = jnp.pad(t, ((0, 0), (0, Lp - L), (0, 0), (0, 0)))
        return t.reshape(B * dilation, n_blk, ATT_BLOCK, H, Dh)

    def with_prev(t):
        prev = jnp.pad(t, ((0, 0), (1, 0), (0, 0), (0, 0), (0, 0)))[:, :-1]
        return jnp.concatenate([prev, t], axis=2)

    qb = to_blocks(q)
    kw = with_prev(to_blocks(k))
    vw = with_prev(to_blocks(v))

    qi = jnp.arange(ATT_BLOCK)[:, None]
    kj = jnp.arange(2 * ATT_BLOCK)[None, :]
    dist = ATT_BLOCK + qi - kj
    blk = jnp.arange(n_blk)[:, None, None]
    valid = (dist >= 0) & (dist <= steps) & (blk * ATT_BLOCK - ATT_BLOCK + kj >= 0)

    s = jnp.einsum('znqhd,znkhd->znhqk', qb, kw)
    s = jnp.where(valid[None, :, None], s, -jnp.inf)
    m = jnp.max(s, axis=-1, keepdims=True)
    p = jnp.exp(s - m)
    denom = jnp.sum(p, axis=-1, keepdims=True)
    o = jnp.einsum('znhqk,znkhd->znqhd', p, vw) / jnp.swapaxes(denom, 2, 3)
    lse = jnp.swapaxes((m + jnp.log(denom))[..., 0], 2, 3)

    o = o.reshape(B, dilation, Lp, H, Dh)[:, :, :L].swapaxes(1, 2).reshape(B, S, H, Dh)
    lse = lse.reshape(B, dilation, Lp, H)[:, :, :L].swapaxes(1, 2).reshape(B, S, H)
    return o, lse


def dilated_attention_mixer(h, w_in, q_gain, k_gain, w_out, pos):
    B, S, _ = h.shape
    proj = (h @ w_in).astype(jnp.float32).reshape(B, S, 3, N_ATT_GROUPS, ATT_HEADS, ATT_HEAD_DIM)
    inv_freq = ROPE_THETA ** (-jnp.arange(0, ROPE_DIM, 2, dtype=jnp.float32) / ROPE_DIM)
    all_heads = N_ATT_GROUPS * ATT_HEADS
    q = proj[:, :, 0].reshape(B, S, all_heads, ATT_HEAD_DIM)
    k = proj[:, :, 1].reshape(B, S, all_heads, ATT_HEAD_DIM)
    v = proj[:, :, 2]
    q = rotate(head_rms(q) * q_gain.astype(jnp.float32), pos, inv_freq) * (ATT_HEAD_DIM ** -0.5)
    k = rotate(head_rms(k) * k_gain.astype(jnp.float32), pos, inv_freq)
    q = q.reshape(B, S, N_ATT_GROUPS, ATT_HEADS, ATT_HEAD_DIM)
    k = k.reshape(B, S, N_ATT_GROUPS, ATT_HEADS, ATT_HEAD_DIM)

    outs, lses = [], []
    for g, (window, dilation) in enumerate(ATT_GROUPS):
        o, l = dilated_window_attention(q[:, :, g], k[:, :, g], v[:, :, g], dilation, window // dilation)
        outs.append(o)
        lses.append(l)
    weights = jax.nn.softmax(jnp.stack(lses, axis=0), axis=0)
    out = jnp.sum(weights[..., None] * jnp.stack(outs, axis=0), axis=0)
    return out.reshape(B, S, ATT_WIDTH).astype(h.dtype) @ w_out


def squared_relu_mlp(h, w_in, w_out):
    return jnp.square(jax.nn.relu(h @ w_in)) @ w_out


def setup_inputs(seed: int = 0) -> dict:
    key = jax.random.key(seed)
    ks = jax.random.split(key, 11)
    f32 = jnp.float32
    x = jax.random.normal(ks[0], (BATCH, SEQ, D_MODEL), f32)
    norm_mix_gain = 1.0 + 0.02 * jax.random.normal(ks[1], (DEPTH, D_MODEL), f32)
    norm_mlp_gain = 1.0 + 0.02 * jax.random.normal(ks[2], (DEPTH, D_MODEL), f32)
    ret_w_in = jax.random.normal(ks[3], (N_RET_LAYERS, D_MODEL, RET_IN_WIDTH), f32) * D_MODEL ** -0.5
    ret_w_out = jax.random.normal(ks[4], (N_RET_LAYERS, RET_V_WIDTH, D_MODEL), f32) * RET_V_WIDTH ** -0.5
    att_w_in = jax.random.normal(ks[5], (N_ATT_LAYERS, D_MODEL, ATT_IN_WIDTH), f32) * D_MODEL ** -0.5
    att_q_gain = 1.0 + 0.02 * jax.random.normal(ks[6], (N_ATT_LAYERS, ATT_HEAD_DIM), f32)
    att_k_gain = 1.0 + 0.02 * jax.random.normal(ks[7], (N_ATT_LAYERS, ATT_HEAD_DIM), f32)
    att_w_out = jax.random.normal(ks[8], (N_ATT_LAYERS, ATT_WIDTH, D_MODEL), f32) * ATT_WIDTH ** -0.5
    mlp_w_in = jax.random.normal(ks[9], (DEPTH, D_MODEL, D_FF), f32) * D_MODEL ** -0.5
    mlp_w_out = jax.random.normal(ks[10], (DEPTH, D_FF, D_MODEL), f32) * D_FF ** -0.5
    return {"x": x, "norm_mix_gain": norm_mix_gain, "norm_mlp_gain": norm_mlp_gain,
            "ret_w_in": ret_w_in, "ret_w_out": ret_w_out,
            "att_w_in": att_w_in, "att_q_gain": att_q_gain, "att_k_gain": att_k_gain,
            "att_w_out": att_w_out, "mlp_w_in": mlp_w_in, "mlp_w_out": mlp_w_out}


def reference(x, norm_mix_gain, norm_mlp_gain, ret_w_in, ret_w_out, att_w_in, att_q_gain,
              att_k_gain, att_w_out, mlp_w_in, mlp_w_out):
    S = x.shape[1]
    pos = jnp.arange(S, dtype=jnp.float32)
    h = x
    for i in range(DEPTH):
        hn = rms_norm(h, norm_mix_gain[i])
        j = i // N_MIXERS
        if i % N_MIXERS == 0:
            mix = retention_mixer(hn, ret_w_in[j], ret_w_out[j], pos)
        else:
            mix = dilated_attention_mixer(hn, att_w_in[j], att_q_gain[j], att_k_gain[j], att_w_out[j], pos)
        h = h + mix
        hn = rms_norm(h, norm_mlp_gain[i])
        h = h + squared_relu_mlp(hn, mlp_w_in[i], mlp_w_out[i])
    return h
```

```python
import numpy as np
import ml_dtypes
import concourse.bass as bass
import concourse.mybir as mybir
from concourse.bass_utils import run_bass_kernel_spmd

F32 = mybir.dt.float32
BF16 = mybir.dt.bfloat16
AF = mybir.ActivationFunctionType
ALU = mybir.AluOpType
AX = mybir.AxisListType
NPBF = ml_dtypes.bfloat16

D = 2048
S = 8192
B = 2
NCORE = 8
T = 2048
EPS = 1e-6
RET_H, RET_DK, RET_DV = 8, 256, 512
ATT_G, ATT_H, ATT_DH = 3, 16, 128
DILS = (1, 4, 16)
DFF = 8192

ENGS = ("pe", "act", "dve", "pool", "sp")
BLK = {"pe": "tensor", "act": "scalar", "dve": "vector", "pool": "gpsimd", "sp": "sync"}
NDMA = 8
STRICT_SAME_ENGINE = True


class Buf:
    __slots__ = ("w", "re", "rd")

    def __init__(self):
        self.w = []
        self.re = {}
        self.rd = []


class Sched:
    def __init__(self):
        self.streams = {e: [] for e in ENGS}
        self.n_inst = {e: 0 for e in ENGS}
        self.needed = {e: set() for e in ENGS}
        self.seen = {e: {} for e in ENGS}
        self.dma_cnt = {}
        self.dma_rr = {e: 0 for e in ENGS}
        self.dma_last = {}

    def _wait(self, eng, t):
        if t[0] == "e":
            _, src, idx = t
            if src == eng and (eng == "pe" or not STRICT_SAME_ENGINE):
                return
            if self.seen[eng].get(src, -1) >= idx:
                return
            self.seen[eng][src] = idx
            self.needed[src].add(idx)
            self.streams[eng].append(("we", src, idx))
        else:
            _, q, i, val = t
            key = (q, i)
            if self.seen[eng].get(key, 0) >= val:
                return
            self.seen[eng][key] = val
            self.streams[eng].append(("wd", q, i, val))

    def op(self, eng, name, kw, reads=(), writes=(), dma=False, part=0, partial=None):
        fn = (name, kw)
        if name == "dma_start":
            dma = True
        deps = []
        for b in reads:
            deps.extend(b.w)
        for b in writes:
            if not (part and (partial is None or b in partial)):
                deps.extend(b.w)
            for src, idx in b.re.items():
                deps.append(("e", src, idx))
            deps.extend(b.rd)
        if dma:
            i = self.dma_rr[eng]
            self.dma_rr[eng] = (i + 1) % NDMA
            prev = self.dma_last.get((eng, i))
            if prev is not None:
                deps.append(prev)
        for t in deps:
            self._wait(eng, t)
        if dma:
            c = self.dma_cnt.get((eng, i), 0) + 16
            self.dma_cnt[(eng, i)] = c
            tk = ("d", eng, i, c)
            self.dma_last[(eng, i)] = tk
            self.streams[eng].append(("dma", fn, i))
        else:
            idx = self.n_inst[eng]
            self.n_inst[eng] = idx + 1
            tk = ("e", eng, idx)
            self.streams[eng].append(("inst", fn, idx))
        for b in writes:
            if part and (partial is None or b in partial):
                b.w = b.w + [tk]
            else:
                b.w = [tk]
            b.re = {}
            b.rd = []
        for b in reads:
            if tk[0] == "e":
                b.re[tk[1]] = tk[2]
            else:
                b.rd.append(tk)
        return tk

    def barrier(self):
        last = {e: self.n_inst[e] - 1 for e in ENGS if self.n_inst[e] > 0}
        for e in ENGS:
            for e2, idx in last.items():
                if e2 != e:
                    self._wait(e, ("e", e2, idx))
            for key in sorted(self.dma_last):
                self._wait(e, self.dma_last[key])

    def finish(self):
        for (q, i), c in sorted(self.dma_cnt.items()):
            self.streams[q].append(("wd", q, i, c))

    def emit(self, nc):
        from contextlib import ExitStack
        self.finish()
        val = {}
        for e in ENGS:
            val[e] = {idx: r + 1 for r, idx in enumerate(sorted(self.needed[e]))}
        with ExitStack() as st:
            esem = {e: st.enter_context(nc.semaphore("se_" + e)) for e in ENGS}
            dsem = {}
            for (q, i) in sorted(self.dma_cnt):
                dsem[(q, i)] = st.enter_context(nc.semaphore("sd_%s%d" % (q, i)))
            block = st.enter_context(nc.Block())
            for e in ENGS:
                items = self.streams[e]
                if not items:
                    continue

                def body(engine, e=e, items=items):
                    for it in items:
                        k = it[0]
                        if k == "inst":
                            ins = getattr(engine, it[1][0])(**it[1][1])
                            v = val[e].get(it[2])
                            if v is not None:
                                ins.then_inc(esem[e], 1)
                        elif k == "dma":
                            getattr(engine, it[1][0])(**it[1][1]).then_inc(dsem[(e, it[2])], 16)
                        elif k == "we":
                            engine.wait_ge(esem[it[1]], val[it[1]][it[2]])
                        else:
                            engine.wait_ge(dsem[(it[1], it[2])], it[3])

                getattr(block, BLK[e])(body)


class Ctx:
    def __init__(self, nc):
        self.nc = nc
        self.s = Sched()
        self.n = 0
        self.stack = None

    def sb(self, shape, dt, name=None):
        self.n += 1
        nm = name or ("t%d" % self.n)
        if self.stack is not None:
            return self.stack.enter_context(self.nc.sbuf_tensor(nm, list(shape), dt))
        return self.nc.alloc_sbuf_tensor(nm, list(shape), dt)

    def ps(self, shape, dt=F32, name=None):
        self.n += 1
        nm = name or ("p%d" % self.n)
        if self.stack is not None:
            return self.stack.enter_context(self.nc.psum_tensor(nm, list(shape), dt))
        return self.nc.alloc_psum_tensor(nm, list(shape), dt)

    def phase(self):
        return _Phase(self)

    def scratch(self, name, shape, dt):
        return self.nc.dram_tensor(name, list(shape), dt).ap()

    def din(self, name, shape, dt):
        return self.nc.dram_tensor(name, list(shape), dt, kind="ExternalInput").ap()

    def dout(self, name, shape, dt):
        return self.nc.dram_tensor(name, list(shape), dt, kind="ExternalOutput").ap()


class _Phase:
    def __init__(self, cx):
        self.cx = cx

    def __enter__(self):
        from contextlib import ExitStack
        self.prev = self.cx.stack
        self.cx.stack = ExitStack()
        return self

    def __exit__(self, *a):
        self.cx.s.barrier()
        self.cx.stack.close()
        self.cx.stack = self.prev
        return False


def load_const(cx, dram_ap, shape, dt, eng="sp"):
    t = cx.sb(shape, dt)
    b = Buf()
    out = t[:, :] if len(shape) == 2 else t[:, :, :]
    cx.s.op(eng, "dma_start", dict(out=out, in_=dram_ap), writes=[b])
    return t, b


def rsqrt_small(s, dst, src, b, scale, eps):
    s.op("dve", "tensor_scalar", dict(out=dst, in0=src, scalar1=scale, scalar2=eps,
                                      op0=ALU.mult, op1=ALU.add), reads=[b], writes=[b])
    s.op("act", "activation", dict(out=dst, in_=dst, func=AF.Sqrt), reads=[b], writes=[b])
    s.op("dve", "reciprocal", dict(out=dst, in_=dst), reads=[b], writes=[b])


class NormT:
    def __init__(self, cx, gainT_dram, ident, ident_b):
        self.cx = cx
        self.gain, self.gain_b = load_const(cx, gainT_dram, [128, 16], F32)
        self.ident, self.ident_b = ident, ident_b
        self.sq = cx.sb([128, D], BF16)
        self.sq_b = Buf()
        self.st = [cx.sb([128, 2], F32) for _ in range(2)]
        self.st_b = [Buf() for _ in range(2)]
        self.xn = [cx.sb([128, D], BF16) for _ in range(2)]
        self.xn_b = [Buf() for _ in range(2)]
        self.pt = [cx.ps([128, 1024], BF16) for _ in range(2)]
        self.pt_b = [Buf() for _ in range(2)]
        self.k = 0

    def run(self, src, src_b, XT, XT_b, t0):
        s = self.cx.s
        j = self.k % 2
        self.k += 1
        st, xn = self.st[j], self.xn[j]
        s.op("act", "activation", dict(out=self.sq[:, :], in_=src, func=AF.Square, accum_out=st[:, 0:1]),
             reads=[src_b], writes=[self.sq_b, self.st_b[j]])
        rsqrt_small(s, st[:, 1:2], st[:, 0:1], self.st_b[j], 1.0 / D, EPS)
        s.op("act", "activation", dict(out=xn[:, :], in_=src, func=AF.Copy, scale=st[:, 1:2]),
             reads=[src_b, self.st_b[j]], writes=[self.xn_b[j]])
        for half in range(2):
            pt, pt_b = self.pt[half], self.pt_b[half]
            for c in range(8):
                kc = half * 8 + c
                s.op("pe", "transpose", dict(out=pt[:, c * 128:(c + 1) * 128],
                                             in_=xn[:, kc * 128:(kc + 1) * 128], identity=self.ident[:, :]),
                     reads=[self.xn_b[j], self.ident_b], writes=[pt_b])
            s.op("dve", "tensor_tensor", dict(
                out=XT[:, half * 8:(half + 1) * 8, t0:t0 + 128],
                in0=pt[:, :].rearrange("p (c t) -> p c t", t=128),
                in1=self.gain[:, half * 8:(half + 1) * 8].unsqueeze(2).broadcast_to([128, 8, 128]),
                op=ALU.mult), reads=[pt_b, self.gain_b], writes=[XT_b], part=(t0 + half))


class WStream:
    def __init__(self, cx, nbuf=3, elems=8192):
        self.cx = cx
        self.elems = elems
        self.t = [cx.sb([128, elems], BF16) for _ in range(nbuf)]
        self.b = [Buf() for _ in range(nbuf)]
        self.k = 0

    def load(self, w_dram, r0, kc, c0, ncols):
        assert kc * ncols <= self.elems
        j = self.k % len(self.t)
        self.k += 1
        b = self.b[j]
        t = self.t[j][:, 0:kc * ncols].rearrange("p (kc n) -> p kc n", n=ncols)
        src = w_dram[r0:r0 + kc * 128, c0:c0 + ncols].rearrange("(kc p) n -> p kc n", p=128)
        step = max(1, kc // 4)
        for k0 in range(0, kc, step):
            self.cx.s.op("pool", "dma_start", dict(out=t[:, k0:k0 + step, :], in_=src[:, k0:k0 + step, :]),
                         writes=[b], part=k0)
        return t, b


def build_A(Tn=T):
    nc = bass.Bass("TRN2", target_bir_lowering=False)
    cx = Ctx(nc)
    s = cx.s
    x = cx.din("x", [Tn, D], F32)
    gain = cx.din("gain", [128, 16], F32)
    w = cx.din("w", [D, 12288], F32)
    identd = cx.din("ident", [128, 128], BF16)
    tabs = cx.din("tabs", [4, 128, Tn], F32)
    qT = cx.dout("qT", [2048, Tn], BF16)
    kT = cx.dout("kT", [2048, Tn], BF16)
    v = cx.dout("v", [Tn, 4096], BF16)
    sg = cx.dout("sg", [Tn, 4096], BF16)

    ident, ident_b = load_const(cx, identd, [128, 128], BF16)
    tab = [load_const(cx, tabs[i], [128, Tn], F32) for i in range(4)]
    XT = cx.sb([128, 16, Tn], BF16)
    XT_b = Buf()
    nt = NormT(cx, gain, ident, ident_b)
    xin = [cx.sb([128, D], F32) for _ in range(2)]
    xin_b = [Buf() for _ in range(2)]
    for tt in range(Tn // 128):
        j = tt % 2
        s.op("sp", "dma_start", dict(out=xin[j][:, :], in_=x[tt * 128:(tt + 1) * 128, :]), writes=[xin_b[j]])
        nt.run(xin[j][:, :], xin_b[j], XT, XT_b, tt * 128)

    ws = WStream(cx)
    pg = [cx.ps([128, 512], F32) for _ in range(4)]
    pg_b = [Buf() for _ in range(4)]
    tmp = [cx.sb([128, 512], F32) for _ in range(4)]
    tmp_b = [Buf() for _ in range(4)]
    ost = [cx.sb([128, 4, 512], BF16) for _ in range(2)]
    ost_b = [Buf() for _ in range(2)]
    ov = [cx.sb([128, 512], BF16) for _ in range(3)]
    ov_b = [Buf() for _ in range(3)]
    NTB = Tn // 512
    oc = vc = pk = 0
    for nb in range(24):
        wt, wb = ws.load(w, 0, 16, nb * 512, 512)
        if nb < 8:
            dst = qT if nb < 4 else kT
            (cst, cstb), (snt, sntb) = (tab[0], tab[1]) if nb < 4 else (tab[2], tab[3])
            row0 = (nb % 4) * 512
            for tb in range(NTB):
                o, o_b = ost[oc % 2], ost_b[oc % 2]
                oc += 1
                tsl = slice(tb * 512, (tb + 1) * 512)
                for pr in range(2):
                    pp = []
                    for half in range(2):
                        c = pr * 2 + half
                        p, p_b = pg[pk % 4], pg_b[pk % 4]
                        pk += 1
                        pp.append((p, p_b))
                        for kc in range(16):
                            s.op("pe", "matmul", dict(out=p[:, :], lhsT=wt[:, kc, c * 128:(c + 1) * 128],
                                                      rhs=XT[:, kc, tsl], start=(kc == 0), stop=(kc == 15)),
                                 reads=[wb, XT_b], writes=[p_b])
                    (x1, x1b), (x2, x2b) = pp
                    s.op("dve", "tensor_tensor", dict(out=tmp[0][:, :], in0=x1[:, :], in1=cst[:, tsl], op=ALU.mult),
                         reads=[x1b, cstb], writes=[tmp_b[0]])
                    s.op("dve", "tensor_tensor", dict(out=tmp[1][:, :], in0=x2[:, :], in1=snt[:, tsl], op=ALU.mult),
                         reads=[x2b, sntb], writes=[tmp_b[1]])
                    s.op("dve", "tensor_tensor", dict(out=tmp[2][:, :], in0=x2[:, :], in1=cst[:, tsl], op=ALU.mult),
                         reads=[x2b, cstb], writes=[tmp_b[2]])
                    s.op("dve", "tensor_tensor", dict(out=tmp[3][:, :], in0=x1[:, :], in1=snt[:, tsl], op=ALU.mult),
                         reads=[x1b, sntb], writes=[tmp_b[3]])
                    s.op("pool", "tensor_tensor", dict(out=o[:, pr * 2, :], in0=tmp[0][:, :], in1=tmp[1][:, :],
                                                       op=ALU.subtract),
                         reads=[tmp_b[0], tmp_b[1]], writes=[o_b], part=pr)
                    s.op("pool", "tensor_tensor", dict(out=o[:, pr * 2 + 1, :], in0=tmp[2][:, :], in1=tmp[3][:, :],
                                                       op=ALU.add),
                         reads=[tmp_b[2], tmp_b[3]], writes=[o_b], part=1)
                s.op("sp", "dma_start", dict(out=dst[row0:row0 + 512, tsl].rearrange("(c p) t -> p c t", p=128),
                                             in_=o[:, :, :]), reads=[o_b])
        else:
            isv = nb < 16
            dst = v if isv else sg
            col0 = ((nb - 8) % 8) * 512
            for tt in range(Tn // 128):
                p, p_b = pg[pk % 4], pg_b[pk % 4]
                pk += 1
                for kc in range(16):
                    s.op("pe", "matmul", dict(out=p[:, :], lhsT=XT[:, kc, tt * 128:(tt + 1) * 128], rhs=wt[:, kc, :],
                                              start=(kc == 0), stop=(kc == 15)),
                         reads=[wb, XT_b], writes=[p_b])
                o, o_b = ov[vc % 3], ov_b[vc % 3]
                vc += 1
                s.op("act", "activation", dict(out=o[:, :], in_=p[:, :], func=(AF.Copy if isv else AF.Silu)),
                     reads=[p_b], writes=[o_b])
                s.op("sp", "dma_start", dict(out=dst[tt * 128:(tt + 1) * 128, col0:col0 + 512], in_=o[:, :]),
                     reads=[o_b])
    s.emit(nc)
    return nc


def ret_tables(pos):
    inv = (1.0 / (np.float32(10000.0) ** np.linspace(0.0, 1.0, 128, dtype=np.float32))).astype(np.float32)
    ang = (pos[:, None].astype(np.float32) * inv[None, :]).astype(np.float32)
    c = np.cos(ang).astype(np.float32).T
    sn = np.sin(ang).astype(np.float32).T
    sc = np.float32(RET_DK ** -0.5)
    return np.ascontiguousarray(np.stack([c, sn, c * sc, sn * sc]).astype(np.float32))


IDENT = np.eye(128, dtype=np.float32).astype(NPBF)


def gainT(vec):
    return np.ascontiguousarray(np.asarray(vec, np.float32).reshape(16, 128).T)


def build_B(Sn=S, NU=2):
    nc = bass.Bass("TRN2", target_bir_lowering=False)
    cx = Ctx(nc)
    s = cx.s
    NCH = Sn // 128
    G = min(8, NCH)
    qT = cx.din("qT", [NU, 256, Sn], BF16)
    kT = cx.din("kT", [NU, 256, Sn], BF16)
    v = cx.din("v", [NU, Sn, 512], BF16)
    cm = cx.din("cmask", [NU, 2, 128, 128], F32)
    cc = cx.din("ccol", [NU, 128, 2], F32)
    identd = cx.din("ident", [128, 128], BF16)
    o = cx.dout("o", [NU, Sn, 512], BF16)

    ident, ident_b = load_const(cx, identd, [128, 128], BF16)
    U = []
    for u in range(NU):
        d = {}
        d["mask"], d["mask_b"] = load_const(cx, cm[u, 0], [128, 128], F32)
        d["qd"], d["qd_b"] = load_const(cx, cm[u, 1], [128, 128], F32)
        d["col"], d["col_b"] = load_const(cx, cc[u], [128, 2], F32)
        d["S"] = cx.sb([128, 2, 512], F32)
        d["S_b"] = Buf()
        d["Sbf"] = [cx.sb([128, 2, 512], BF16) for _ in range(2)]
        d["Sbf_b"] = [Buf() for _ in range(2)]
        s.op("pool", "memset", dict(ap=d["S"][:, :, :], constant=0.0), writes=[d["S_b"]])
        s.op("pool", "memset", dict(ap=d["Sbf"][0][:, :, :], constant=0.0), writes=[d["Sbf_b"][0]])
        d["q"] = [cx.sb([128, 2, G * 128], BF16) for _ in range(2)]
        d["k"] = [cx.sb([128, 2, G * 128], BF16) for _ in range(2)]
        d["v"] = [cx.sb([128, G, 512], BF16) for _ in range(2)]
        d["in_b"] = [Buf() for _ in range(2)]
        U.append(d)
    ps_s = [cx.ps([128, 512], F32) for _ in range(2)]
    ps_s_b = [Buf() for _ in range(2)]
    ps_o = [cx.ps([128, 512], F32) for _ in range(2)]
    ps_o_b = [Buf() for _ in range(2)]
    ps_k = cx.ps([128, 1024], BF16)
    ps_k_b = Buf()
    ps_st = [cx.ps([128, 512], F32) for _ in range(2)]
    ps_st_b = [Buf() for _ in range(2)]
    PT = [cx.sb([128, 128], BF16) for _ in range(2)]
    PT_b = [Buf() for _ in range(2)]
    qdT = [cx.sb([128, 2, 128], BF16) for _ in range(2)]
    qdT_b = [Buf() for _ in range(2)]
    kd = [cx.sb([128, 256], BF16) for _ in range(2)]
    kd_b = [Buf() for _ in range(2)]
    sq = cx.sb([128, 512], BF16)
    sq_b = Buf()
    st = [cx.sb([128, 2], F32) for _ in range(2)]
    st_b = [Buf() for _ in range(2)]
    ob = [cx.sb([128, 512], BF16) for _ in range(3)]
    ob_b = [Buf() for _ in range(3)]
    it = 0
    for n in range(NCH):
        for u in range(NU):
            d = U[u]
            g, gi = n // G, n % G
            gb = g % 2
            if gi == 0:
                tsl = slice(g * G * 128, (g + 1) * G * 128)
                s.op("sp", "dma_start", dict(out=d["q"][gb][:, :, :],
                                             in_=qT[u][:, tsl].rearrange("(c p) t -> p c t", p=128)),
                     writes=[d["in_b"][gb]])
                s.op("sp", "dma_start", dict(out=d["k"][gb][:, :, :],
                                             in_=kT[u][:, tsl].rearrange("(c p) t -> p c t", p=128)),
                     writes=[d["in_b"][gb]], part=1)
                s.op("sp", "dma_start", dict(out=d["v"][gb][:, :, :],
                                             in_=v[u][tsl, :].rearrange("(g p) f -> p g f", p=128)),
                     writes=[d["in_b"][gb]], part=1)
            j = it % 2
            it += 1
            inb = d["in_b"][gb]
            csl = slice(gi * 128, (gi + 1) * 128)
            qc, kc_, vc = d["q"][gb], d["k"][gb], d["v"][gb]
            sb_old, sb_old_b = d["Sbf"][n % 2], d["Sbf_b"][n % 2]
            sb_new, sb_new_b = d["Sbf"][(n + 1) % 2], d["Sbf_b"][(n + 1) % 2]
            for dc in range(2):
                s.op("pe", "matmul", dict(out=ps_s[j][:, 0:128], lhsT=kc_[:, dc, csl], rhs=qc[:, dc, csl],
                                          start=(dc == 0), stop=(dc == 1)), reads=[inb], writes=[ps_s_b[j]])
            s.op("dve", "tensor_tensor", dict(out=PT[j][:, :], in0=ps_s[j][:, 0:128], in1=d["mask"][:, :],
                                              op=ALU.mult), reads=[ps_s_b[j], d["mask_b"]], writes=[PT_b[j]])
            s.op("pool", "tensor_tensor", dict(out=qdT[j][:, :, :], in0=qc[:, :, csl],
                                               in1=d["qd"][:, :].unsqueeze(1).broadcast_to([128, 2, 128]),
                                               op=ALU.mult), reads=[inb, d["qd_b"]], writes=[qdT_b[j]])
            s.op("pe", "matmul", dict(out=ps_o[j][:, :], lhsT=PT[j][:, :], rhs=vc[:, gi, :], start=True, stop=False),
                 reads=[PT_b[j], inb], writes=[ps_o_b[j]])
            for dc in range(2):
                s.op("pe", "matmul", dict(out=ps_o[j][:, :], lhsT=qdT[j][:, dc, :], rhs=sb_old[:, dc, :],
                                          start=False, stop=(dc == 1)),
                     reads=[qdT_b[j], sb_old_b], writes=[ps_o_b[j]])
            s.op("act", "activation", dict(out=sq[:, :], in_=ps_o[j][:, :], func=AF.Square, accum_out=st[j][:, 0:1]),
                 reads=[ps_o_b[j]], writes=[sq_b, st_b[j]])
            rsqrt_small(s, st[j][:, 1:2], st[j][:, 0:1], st_b[j], 1.0 / RET_DV, EPS)
            oo, oo_b = ob[it % 3], ob_b[it % 3]
            s.op("act", "activation", dict(out=oo[:, :], in_=ps_o[j][:, :], func=AF.Copy, scale=st[j][:, 1:2]),
                 reads=[ps_o_b[j], st_b[j]], writes=[oo_b])
            s.op("sp", "dma_start", dict(out=o[u][n * 128:(n + 1) * 128, :], in_=oo[:, :]), reads=[oo_b])
            if n == NCH - 1:
                continue
            for dc in range(2):
                s.op("pe", "transpose", dict(out=ps_k[:, dc * 128:(dc + 1) * 128], in_=kc_[:, dc, csl],
                                             identity=ident[:, :]), reads=[inb, ident_b], writes=[ps_k_b])
            s.op("dve", "tensor_scalar", dict(out=kd[j][:, :], in0=ps_k[:, 0:256], scalar1=d["col"][:, 0:1],
                                              scalar2=None, op0=ALU.mult), reads=[ps_k_b, d["col_b"]],
                 writes=[kd_b[j]])
            for dc in range(2):
                s.op("pe", "matmul", dict(out=ps_st[dc][:, :], lhsT=kd[j][:, dc * 128:(dc + 1) * 128],
                                          rhs=vc[:, gi, :], start=True, stop=True),
                     reads=[kd_b[j], inb], writes=[ps_st_b[dc]])
                s.op("dve", "scalar_tensor_tensor", dict(out=d["S"][:, dc, :], in0=d["S"][:, dc, :],
                                                         scalar=d["col"][:, 1:2], in1=ps_st[dc][:, :],
                                                         op0=ALU.mult, op1=ALU.add),
                     reads=[ps_st_b[dc], d["col_b"], d["S_b"]], writes=[d["S_b"]])
                s.op("act", "copy", dict(out=sb_new[:, dc, :], in_=d["S"][:, dc, :]),
                     reads=[d["S_b"]], writes=[sb_new_b], part=dc)
    s.emit(nc)
    return nc


def ret_consts(h):
    lg = np.log(np.float32(1.0) - np.float32(2.0) ** (np.float32(-5.0) - np.float32(h))).astype(np.float32)
    idx = np.arange(128, dtype=np.float32)
    diff = idx[:, None] - idx[None, :]
    inner = np.where(diff >= 0, np.exp(lg * np.maximum(diff, 0.0)), 0.0).astype(np.float32)
    maskT = np.ascontiguousarray(inner.T)
    qd = np.exp(lg * (idx + 1.0)).astype(np.float32)
    qdb = np.broadcast_to(qd[None, :], (128, 128))
    kdec = np.exp(lg * (127.0 - idx)).astype(np.float32)
    cd = np.full(128, np.exp(lg * 128.0), np.float32)
    return (np.ascontiguousarray(np.stack([maskT, qdb]).astype(np.float32)),
            np.ascontiguousarray(np.stack([kdec, cd], axis=1).astype(np.float32)))


def build_CE(mode, Tn=T, TB=1024):
    nc = bass.Bass("TRN2", target_bir_lowering=False)
    cx = Ctx(nc)
    s = cx.s
    KM = 4096 if mode == "ret" else 2048
    resid = cx.din("resid", [Tn, D], F32)
    if mode == "ret":
        o_in = cx.din("o", [Tn, KM], BF16)
        sg_in = cx.din("sg", [Tn, KM], BF16)
    else:
        aT = cx.din("aT", [KM, Tn], BF16)
    wo_d = cx.din("wo", [KM, D], F32)
    gain = cx.din("gain", [128, 16], F32)
    w1_d = cx.din("w1", [D, DFF], F32)
    w2_d = cx.din("w2", [DFF, D], F32)
    identd = cx.din("ident", [128, 128], BF16)
    out = cx.dout("out", [Tn, D], F32)

    NT = TB // 128
    ident, ident_b = load_const(cx, identd, [128, 128], BF16)
    Y = cx.sb([128, NT, D], F32)
    Y_b = [Buf() for _ in range(NT)]
    XT = cx.sb([128, 16, TB], BF16)
    XT_b = Buf()
    nt = NormT(cx, gain, ident, ident_b)
    ws = WStream(cx)
    GT = [cx.sb([128, 4, TB], BF16) for _ in range(2)]
    GT_b = [Buf() for _ in range(2)]
    pg = [cx.ps([128, 512], F32) for _ in range(4)]
    pg_b = [Buf() for _ in range(4)]
    pk = 0
    gk = 0
    if mode == "ret":
        oin = [cx.sb([128, 512], BF16) for _ in range(2)]
        sgin = [cx.sb([128, 512], BF16) for _ in range(2)]
        in_b = [Buf() for _ in range(2)]
        prod = [cx.sb([128, 512], BF16) for _ in range(2)]
        prod_b = [Buf() for _ in range(2)]
        ptr = [cx.ps([128, 1024], BF16) for _ in range(2)]
        ptr_b = [Buf() for _ in range(2)]
        ik = 0
    rl = [cx.sb([128, 512], F32) for _ in range(2)]
    rl_b = [Buf() for _ in range(2)]
    rk = 0

    def acc_gemm(A, A_b, W, W_b, nkc):
        nonlocal pk
        for tt in range(NT):
            for nb in range(4):
                p, p_b = pg[pk % 4], pg_b[pk % 4]
                pk += 1
                for kc in range(nkc):
                    s.op("pe", "matmul", dict(out=p[:, :], lhsT=A[:, kc, tt * 128:(tt + 1) * 128],
                                              rhs=W[:, kc, nb * 512:(nb + 1) * 512],
                                              start=(kc == 0), stop=(kc == nkc - 1)),
                         reads=[A_b, W_b], writes=[p_b])
                ysl = Y[:, tt, nb * 512:(nb + 1) * 512]
                s.op("dve", "tensor_tensor", dict(out=ysl, in0=ysl, in1=p[:, :], op=ALU.add),
                     reads=[p_b, Y_b[tt]], writes=[Y_b[tt]])

    for blk in range(Tn // TB):
        tok0 = blk * TB
        for tt in range(NT):
            s.op("sp", "dma_start", dict(out=Y[:, tt, :], in_=resid[tok0 + tt * 128:tok0 + (tt + 1) * 128, :]),
                 writes=[Y_b[tt]])
        for kb in range(KM // 512):
            W, W_b = ws.load(wo_d, kb * 512, 4, 0, D)
            g, g_b = GT[gk % 2], GT_b[gk % 2]
            gk += 1
            if mode == "att":
                s.op("sp", "dma_start", dict(out=g[:, :, :], in_=aT[kb * 512:(kb + 1) * 512, tok0:tok0 + TB]
                                             .rearrange("(c p) t -> p c t", p=128)), writes=[g_b])
            else:
                for tt in range(NT):
                    j = ik % 2
                    ik += 1
                    rows = slice(tok0 + tt * 128, tok0 + (tt + 1) * 128)
                    cols = slice(kb * 512, (kb + 1) * 512)
                    s.op("sp", "dma_start", dict(out=oin[j][:, :], in_=o_in[rows, cols]), writes=[in_b[j]])
                    s.op("sp", "dma_start", dict(out=sgin[j][:, :], in_=sg_in[rows, cols]), writes=[in_b[j]], part=1)
                    s.op("pool", "tensor_tensor", dict(out=prod[j][:, :], in0=oin[j][:, :], in1=sgin[j][:, :],
                                                       op=ALU.mult), reads=[in_b[j]], writes=[prod_b[j]])
                    for c in range(4):
                        s.op("pe", "transpose", dict(out=ptr[j][:, c * 128:(c + 1) * 128],
                                                     in_=prod[j][:, c * 128:(c + 1) * 128], identity=ident[:, :]),
                             reads=[prod_b[j], ident_b], writes=[ptr_b[j]])
                    s.op("act", "copy", dict(out=g[:, :, tt * 128:(tt + 1) * 128],
                                             in_=ptr[j][:, 0:512].rearrange("p (c t) -> p c t", t=128)),
                         reads=[ptr_b[j]], writes=[g_b], part=tt)
            acc_gemm(g, g_b, W, W_b, 4)
        for tt in range(NT):
            nt.run(Y[:, tt, :], Y_b[tt], XT, XT_b, tt * 128)
        for hb in range(DFF // 512):
            W1, W1_b = ws.load(w1_d, 0, 16, hb * 512, 512)
            W2, W2_b = ws.load(w2_d, hb * 512, 4, 0, D)
            u, u_b = GT[gk % 2], GT_b[gk % 2]
            gk += 1
            first = True
            for hc in range(4):
                for t2 in range(TB // 512):
                    p, p_b = pg[pk % 4], pg_b[pk % 4]
                    pk += 1
                    for kc in range(16):
                        s.op("pe", "matmul", dict(out=p[:, :], lhsT=W1[:, kc, hc * 128:(hc + 1) * 128],
                                                  rhs=XT[:, kc, t2 * 512:(t2 + 1) * 512],
                                                  start=(kc == 0), stop=(kc == 15)),
                             reads=[W1_b, XT_b], writes=[p_b])
                    r, r_b = rl[rk % 2], rl_b[rk % 2]
                    rk += 1
                    s.op("act", "activation", dict(out=r[:, :], in_=p[:, :], func=AF.Relu), reads=[p_b], writes=[r_b])
                    s.op("pool", "tensor_tensor", dict(out=u[:, hc, t2 * 512:(t2 + 1) * 512], in0=r[:, :], in1=r[:, :],
                                                       op=ALU.mult), reads=[r_b], writes=[u_b], part=(0 if first else 1))
                    first = False
            acc_gemm(u, u_b, W2, W2_b, 4)
        for tt in range(NT):
            s.op("sp", "dma_start", dict(out=out[tok0 + tt * 128:tok0 + (tt + 1) * 128, :], in_=Y[:, tt, :]),
                 reads=[Y_b[tt]])
    s.emit(nc)
    return nc


def build_C2(Tn=T):
    nc = bass.Bass("TRN2", target_bir_lowering=False)
    cx = Ctx(nc)
    s = cx.s
    NQ = ATT_G * ATT_H * ATT_DH
    x = cx.din("x", [Tn, D], F32)
    gain = cx.din("gain", [128, 16], F32)
    w = cx.din("w", [D, 3 * NQ], F32)
    qkg = cx.din("qkg", [128, 2], F32)
    identd = cx.din("ident", [128, 128], BF16)
    cm = cx.din("cmat", [2, 128, 128], BF16)
    tabs = cx.din("tabs", [2, 32, Tn], F32)
    qT = cx.dout("qT", [NQ, Tn], BF16)
    kT = cx.dout("kT", [NQ, Tn], BF16)
    v = cx.dout("v", [Tn, NQ], BF16)

    ident, ident_b = load_const(cx, identd, [128, 128], BF16)
    onesm, onesm_b = load_const(cx, cm[0], [128, 128], BF16)
    swp, swp_b = load_const(cx, cm[1], [128, 128], BF16)
    cos32, cos_b = load_const(cx, tabs[0], [32, Tn], F32)
    sin32, sin_b = load_const(cx, tabs[1], [32, Tn], F32)
    gk, gk_b = load_const(cx, qkg, [128, 2], F32)
    s.op("dve", "tensor_scalar", dict(out=gk[:, 0:1], in0=gk[:, 0:1], scalar1=float(ATT_DH ** -0.5), scalar2=None,
                                      op0=ALU.mult), reads=[gk_b], writes=[gk_b])
    XT = cx.sb([128, 16, Tn], BF16)
    XT_b = Buf()
    nt = NormT(cx, gain, ident, ident_b)
    xin = [cx.sb([128, D], F32) for _ in range(2)]
    xin_b = [Buf() for _ in range(2)]
    for tt in range(Tn // 128):
        j = tt % 2
        s.op("sp", "dma_start", dict(out=xin[j][:, :], in_=x[tt * 128:(tt + 1) * 128, :]), writes=[xin_b[j]])
        nt.run(xin[j][:, :], xin_b[j], XT, XT_b, tt * 128)

    ws = WStream(cx)
    pg = [cx.ps([128, 512], F32) for _ in range(4)]
    pg_b = [Buf() for _ in range(4)]
    ps2 = cx.ps([128, 512], F32)
    ps2_b = Buf()
    ps3 = cx.ps([128, 512], F32)
    ps3_b = Buf()
    sqb = cx.sb([128, 4, 512], BF16)
    sqb_b = [Buf() for _ in range(4)]
    ms = cx.sb([128, 4, 512], F32)
    ms_b = Buf()
    ob = [cx.sb([128, 4, 512], BF16) for _ in range(2)]
    ob_b = [[Buf() for _ in range(4)] for _ in range(2)]
    t1 = cx.sb([32, 512], F32)
    t1_b = Buf()
    t2 = cx.sb([32, 512], F32)
    t2_b = Buf()
    ov = [cx.sb([128, 512], BF16) for _ in range(3)]
    ov_b = [Buf() for _ in range(3)]
    oc = vc = pk = 0
    for nb in range(3 * NQ // 512):
        wt, wb = ws.load(w, 0, 16, nb * 512, 512)
        if nb < 24:
            isq = nb < 12
            dst = qT if isq else kT
            gcol = gk[:, 0:1] if isq else gk[:, 1:2]
            row0 = (nb % 12) * 512
            for tb in range(Tn // 512):
                tsl = slice(tb * 512, (tb + 1) * 512)
                o, o_b = ob[oc % 2], ob_b[oc % 2]
                oc += 1
                for hd in range(4):
                    for kc in range(16):
                        s.op("pe", "matmul", dict(out=pg[hd][:, :], lhsT=wt[:, kc, hd * 128:(hd + 1) * 128],
                                                  rhs=XT[:, kc, tsl], start=(kc == 0), stop=(kc == 15)),
                             reads=[wb, XT_b], writes=[pg_b[hd]])
                for hd in range(4):
                    s.op("act", "activation", dict(out=sqb[:, hd, :], in_=pg[hd][:, :], func=AF.Square),
                         reads=[pg_b[hd]], writes=[sqb_b[hd]])
                for hd in range(4):
                    s.op("pe", "matmul", dict(out=ps2[:, :], lhsT=onesm[:, :], rhs=sqb[:, hd, :], start=True, stop=True),
                         reads=[onesm_b, sqb_b[hd]], writes=[ps2_b])
                    s.op("dve", "tensor_scalar", dict(out=ms[:, hd, :], in0=ps2[:, :], scalar1=EPS, scalar2=None,
                                                      op0=ALU.add), reads=[ps2_b], writes=[ms_b], part=hd)
                s.op("act", "activation", dict(out=ms[:, :, :], in_=ms[:, :, :], func=AF.Sqrt), reads=[ms_b], writes=[ms_b])
                s.op("dve", "reciprocal", dict(out=ms[:, :, :], in_=ms[:, :, :]), reads=[ms_b], writes=[ms_b])
                for hd in range(4):
                    s.op("dve", "scalar_tensor_tensor", dict(out=o[:, hd, :], in0=pg[hd][:, :], scalar=gcol,
                                                             in1=ms[:, hd, :], op0=ALU.mult, op1=ALU.mult),
                         reads=[pg_b[hd], ms_b, gk_b], writes=[o_b[hd]])
                for hd in range(4):
                    s.op("pe", "matmul", dict(out=ps3[0:32, :], lhsT=swp[:, 0:32], rhs=o[:, hd, :], start=True, stop=True),
                         reads=[swp_b, o_b[hd]], writes=[ps3_b])
                    s.op("dve", "tensor_tensor", dict(out=t2[:, :], in0=ps3[0:32, :], in1=sin32[:, tsl], op=ALU.mult),
                         reads=[ps3_b, sin_b], writes=[t2_b])
                    s.op("pool", "tensor_tensor", dict(out=t1[:, :], in0=o[0:32, hd, :], in1=cos32[:, tsl], op=ALU.mult),
                         reads=[o_b[hd], cos_b], writes=[t1_b])
                    s.op("pool", "tensor_tensor", dict(out=o[0:32, hd, :], in0=t1[:, :], in1=t2[:, :], op=ALU.add),
                         reads=[t1_b, t2_b], writes=[o_b[hd]])
                s.op("sp", "dma_start", dict(out=dst[row0:row0 + 512, tsl].rearrange("(c p) t -> p c t", p=128),
                                             in_=o[:, :, :]), reads=o_b)
        else:
            col0 = (nb - 24) * 512
            for tt in range(Tn // 128):
                p, p_b = pg[pk % 4], pg_b[pk % 4]
                pk += 1
                for kc in range(16):
                    s.op("pe", "matmul", dict(out=p[:, :], lhsT=XT[:, kc, tt * 128:(tt + 1) * 128], rhs=wt[:, kc, :],
                                              start=(kc == 0), stop=(kc == 15)),
                         reads=[wb, XT_b], writes=[p_b])
                oo, oo_b = ov[vc % 3], ov_b[vc % 3]
                vc += 1
                s.op("act", "activation", dict(out=oo[:, :], in_=p[:, :], func=AF.Copy), reads=[p_b], writes=[oo_b])
                s.op("sp", "dma_start", dict(out=v[tt * 128:(tt + 1) * 128, col0:col0 + 512], in_=oo[:, :]),
                     reads=[oo_b])
    s.emit(nc)
    return nc


def att_tables(pos):
    inv = (np.float32(500000.0) ** (-np.arange(0, 32, 2, dtype=np.float32) / np.float32(32))).astype(np.float32)
    ang = (pos[:, None].astype(np.float32) * inv[None, :]).astype(np.float32)
    c = np.cos(ang).astype(np.float32).T
    sn = np.sin(ang).astype(np.float32).T
    return np.ascontiguousarray(np.stack([np.concatenate([c, c]), np.concatenate([-sn, sn])]).astype(np.float32))


def att_cmat():
    ones = np.full((128, 128), 1.0 / 128.0, np.float32)
    sw = np.zeros((128, 128), np.float32)
    for i in range(16):
        sw[i + 16, i] = 1.0
        sw[i, i + 16] = 1.0
    return np.stack([ones, sw]).astype(NPBF)


def build_D(Sn=S, NU=4):
    nc = bass.Bass("TRN2", target_bir_lowering=False)
    cx = Ctx(nc)
    s = cx.s
    SB = 2048
    qp = cx.din("qp", [NU, 3, 128, Sn], BF16)
    kp = cx.din("kp", [NU, 3, 128, Sn], BF16)
    vp = cx.din("vp", [NU, 3, Sn, 128], BF16)
    mk = cx.din("mask", [128, 256], BF16)
    on = cx.din("ones", [128, 128], BF16)
    aT = cx.dout("aT", [NU, 128, Sn], BF16)

    mask, mask_b = load_const(cx, mk, [128, 256], BF16)
    ones, ones_b = load_const(cx, on, [128, 128], BF16)
    Q = [cx.sb([128, Sn], BF16) for _ in range(3)]
    Kt = [cx.sb([128, Sn], BF16) for _ in range(3)]
    V = [cx.sb([128, Sn // 128, 128], BF16) for _ in range(3)]
    in_b = [Buf() for _ in range(3)]
    ND = cx.sb([128, 2, SB], F32)
    ND_b = Buf()
    ps = [cx.ps([128, 512], F32) for _ in range(3)]
    ps_b = [Buf() for _ in range(3)]
    pnd = [cx.ps([128, 512], F32) for _ in range(3)]
    pnd_b = [Buf() for _ in range(3)]
    E = [cx.sb([128, 256], BF16) for _ in range(3)]
    E_b = [Buf() for _ in range(3)]
    PT = [cx.sb([128, 256], BF16) for _ in range(3)]
    PT_b = [Buf() for _ in range(3)]
    rec = cx.sb([128, SB], F32)
    rec_b = Buf()
    ao = [cx.sb([128, SB], BF16) for _ in range(2)]
    ao_b = [Buf() for _ in range(2)]
    it = 0
    ak = 0
    for u in range(NU):
        for g in range(3):
            s.op("sp", "dma_start", dict(out=Q[g][:, :], in_=qp[u, g]), writes=[in_b[g]])
            s.op("sp", "dma_start", dict(out=Kt[g][:, :], in_=kp[u, g]), writes=[in_b[g]], part=1)
            s.op("sp", "dma_start", dict(out=V[g][:, :, :], in_=vp[u, g].rearrange("(n p) d -> p n d", p=128)),
                 writes=[in_b[g]], part=1)
        for sb in range(Sn // SB):
            s.op("pool", "memset", dict(ap=ND[:, :, :], constant=0.0), writes=[ND_b])
            for g, dl in enumerate(DILS):
                L = Sn // dl
                nbk = (SB // dl) // 128
                for c in range(dl):
                    for n in range(nbk):
                        m0 = sb * (SB // dl) + n * 128
                        col = c * L + m0
                        has_prev = m0 >= 128
                        j = it % 3
                        it += 1
                        qs = Q[g][:, col:col + 128]
                        if has_prev:
                            s.op("pe", "matmul", dict(out=ps[j][:, 0:128], lhsT=Kt[g][:, col - 128:col], rhs=qs,
                                                      start=True, stop=True), reads=[in_b[g]], writes=[ps_b[j]])
                        s.op("pe", "matmul", dict(out=ps[j][:, 128:256], lhsT=Kt[g][:, col:col + 128], rhs=qs,
                                                  start=True, stop=True), reads=[in_b[g]], writes=[ps_b[j]])
                        lo = 0 if has_prev else 128
                        s.op("act", "activation", dict(out=E[j][:, lo:256], in_=ps[j][:, lo:256], func=AF.Exp),
                             reads=[ps_b[j]], writes=[E_b[j]])
                        s.op("pool", "tensor_tensor", dict(out=PT[j][:, lo:256], in0=E[j][:, lo:256],
                                                           in1=mask[:, lo:256], op=ALU.mult),
                             reads=[E_b[j], mask_b], writes=[PT_b[j]])
                        bi = col // 128
                        if has_prev:
                            s.op("pe", "matmul", dict(out=pnd[j][:, 0:128], lhsT=V[g][:, bi - 1, :], rhs=PT[j][:, 0:128],
                                                      start=True, stop=False), reads=[in_b[g], PT_b[j]],
                                 writes=[pnd_b[j]])
                        s.op("pe", "matmul", dict(out=pnd[j][:, 0:128], lhsT=V[g][:, bi, :], rhs=PT[j][:, 128:256],
                                                  start=(not has_prev), stop=True), reads=[in_b[g], PT_b[j]],
                             writes=[pnd_b[j]])
                        if has_prev:
                            s.op("pe", "matmul", dict(out=pnd[j][:, 128:256], lhsT=ones[:, :], rhs=PT[j][:, 0:128],
                                                      start=True, stop=False), reads=[ones_b, PT_b[j]],
                                 writes=[pnd_b[j]])
                        s.op("pe", "matmul", dict(out=pnd[j][:, 128:256], lhsT=ones[:, :], rhs=PT[j][:, 128:256],
                                                  start=(not has_prev), stop=True), reads=[ones_b, PT_b[j]],
                             writes=[pnd_b[j]])
                        t0 = n * 128 * dl + c
                        dst = ND[:, :, t0:t0 + 127 * dl + 1:dl]
                        s.op("dve", "tensor_tensor", dict(out=dst, in0=dst,
                                                          in1=pnd[j][:, 0:256].rearrange("p (a t) -> p a t", a=2),
                                                          op=ALU.add), reads=[pnd_b[j], ND_b], writes=[ND_b])
            s.op("dve", "reciprocal", dict(out=rec[:, :], in_=ND[:, 1, :]), reads=[ND_b], writes=[rec_b])
            a, a_b = ao[ak % 2], ao_b[ak % 2]
            ak += 1
            s.op("dve", "tensor_tensor", dict(out=a[:, :], in0=ND[:, 0, :], in1=rec[:, :], op=ALU.mult),
                 reads=[ND_b, rec_b], writes=[a_b])
            s.op("sp", "dma_start", dict(out=aT[u][:, sb * SB:(sb + 1) * SB], in_=a[:, :]), reads=[a_b])
    s.emit(nc)
    return nc


def att_masks():
    kk = np.arange(128)[:, None]
    qi = np.arange(128)[None, :]
    prev = (kk >= qi).astype(np.float32)
    own = (kk <= qi).astype(np.float32)
    return np.ascontiguousarray(np.concatenate([prev, own], axis=1).astype(NPBF))


def to_subseq(a, dl, axis):
    a = np.moveaxis(a, axis, 0)
    n = a.shape[0]
    a = a.reshape(n // dl, dl, *a.shape[1:]).swapaxes(0, 1).reshape(n, *a.shape[1:])
    return np.ascontiguousarray(np.moveaxis(a, 0, axis))


def _run(nc, in_maps):
    res = run_bass_kernel_spmd(nc, in_maps, core_ids=list(range(NCORE)))
    return res.results


def kernel_unfused(x, norm_mix_gain, norm_mlp_gain, ret_w_in, ret_w_out, att_w_in, att_q_gain, att_k_gain,
           att_w_out, mlp_w_in, mlp_w_out):
    f = lambda a: np.ascontiguousarray(np.asarray(a, np.float32))
    x = f(x)
    norm_mix_gain, norm_mlp_gain = f(norm_mix_gain), f(norm_mlp_gain)
    cores = list(range(NCORE))
    R = S // T
    xs = [np.ascontiguousarray(x[c // R, (c % R) * T:(c % R + 1) * T]) for c in cores]
    pos = [np.arange((c % R) * T, (c % R + 1) * T, dtype=np.float32) for c in cores]

    wA = f(ret_w_in[0])
    gA = gainT(norm_mix_gain[0])
    rA = _run(build_A(), [{"x": xs[c], "gain": gA, "w": wA, "ident": IDENT, "tabs": ret_tables(pos[c])}
                          for c in cores])
    qTf = [np.concatenate([rA[b * R + r]["qT"] for r in range(R)], axis=1) for b in range(B)]
    kTf = [np.concatenate([rA[b * R + r]["kT"] for r in range(R)], axis=1) for b in range(B)]
    vf = [np.concatenate([rA[b * R + r]["v"] for r in range(R)], axis=0) for b in range(B)]

    insB = []
    for c in cores:
        us = [(2 * c + i) for i in range(2)]
        bh = [(u // RET_H, u % RET_H) for u in us]
        cms, ccs = zip(*[ret_consts(h) for _, h in bh])
        insB.append({
            "qT": np.ascontiguousarray(np.stack([qTf[b][h * 256:(h + 1) * 256] for b, h in bh])),
            "kT": np.ascontiguousarray(np.stack([kTf[b][h * 256:(h + 1) * 256] for b, h in bh])),
            "v": np.ascontiguousarray(np.stack([vf[b][:, h * 512:(h + 1) * 512] for b, h in bh])),
            "cmask": np.stack(cms), "ccol": np.stack(ccs), "ident": IDENT})
    rB = _run(build_B(), insB)

    insC = []
    for c in cores:
        b, r = c // R, c % R
        o = np.concatenate([rB[(b * RET_H + h) // 2]["o"][(b * RET_H + h) % 2][r * T:(r + 1) * T]
                            for h in range(RET_H)], axis=1)
        insC.append({"resid": xs[c], "o": np.ascontiguousarray(o), "sg": rA[c]["sg"], "wo": f(ret_w_out[0]),
                     "gain": gainT(norm_mlp_gain[0]), "w1": f(mlp_w_in[0]), "w2": f(mlp_w_out[0]), "ident": IDENT})
    rC = _run(build_CE("ret"), insC)
    h1 = [rC[c]["out"] for c in cores]

    qkg = np.ascontiguousarray(np.stack([f(att_q_gain[0]), f(att_k_gain[0])], axis=1))
    wC2 = f(att_w_in[0])
    cmat = att_cmat()
    rC2 = _run(build_C2(), [{"x": h1[c], "gain": gainT(norm_mix_gain[1]), "w": wC2, "qkg": qkg, "ident": IDENT,
                             "cmat": cmat, "tabs": att_tables(pos[c])} for c in cores])
    aq = [np.concatenate([rC2[b * R + r]["qT"] for r in range(R)], axis=1) for b in range(B)]
    ak = [np.concatenate([rC2[b * R + r]["kT"] for r in range(R)], axis=1) for b in range(B)]
    av = [np.concatenate([rC2[b * R + r]["v"] for r in range(R)], axis=0) for b in range(B)]

    insD = []
    masks = att_masks()
    ones = np.ones((128, 128), NPBF)
    for c in cores:
        b = c // R
        hs = [4 * (c % R) + i for i in range(4)]
        qp = np.stack([np.stack([to_subseq(aq[b][(g * ATT_H + h) * 128:(g * ATT_H + h + 1) * 128], dl, 1)
                                 for g, dl in enumerate(DILS)]) for h in hs])
        kp = np.stack([np.stack([to_subseq(ak[b][(g * ATT_H + h) * 128:(g * ATT_H + h + 1) * 128], dl, 1)
                                 for g, dl in enumerate(DILS)]) for h in hs])
        vp = np.stack([np.stack([to_subseq(av[b][:, (g * ATT_H + h) * 128:(g * ATT_H + h + 1) * 128], dl, 0)
                                 for g, dl in enumerate(DILS)]) for h in hs])
        insD.append({"qp": np.ascontiguousarray(qp), "kp": np.ascontiguousarray(kp), "vp": np.ascontiguousarray(vp),
                     "mask": masks, "ones": ones})
    rD = _run(build_D(), insD)

    insE = []
    for c in cores:
        b, r = c // R, c % R
        a = np.concatenate([rD[b * R + h // 4]["aT"][h % 4][:, r * T:(r + 1) * T] for h in range(ATT_H)], axis=0)
        insE.append({"resid": h1[c], "aT": np.ascontiguousarray(a), "wo": f(att_w_out[0]),
                     "gain": gainT(norm_mlp_gain[1]), "w1": f(mlp_w_in[1]), "w2": f(mlp_w_out[1]), "ident": IDENT})
    rE = _run(build_CE("att"), insE)
    out = np.empty((B, S, D), np.float32)
    for c in cores:
        out[c // R, (c % R) * T:(c % R + 1) * T] = rE[c]["out"]
    return out


SEG = 2048
NQ = ATT_G * ATT_H * ATT_DH
DBG = {}


def phase_A(cx, E, seg, full):
    s = cx.s
    w0 = seg * SEG
    with cx.phase():
        tab = [load_const(cx, E["rtabs"][i][:, w0:w0 + SEG], [128, SEG], F32) for i in range(4)]
        XT = cx.sb([128, 16, SEG], BF16)
        XT_b = Buf()
        ws = WStream(cx)
        w = E["w_ret_in"]
        nb_list = list(range(24) if full else range(4, 16))
        pre_w = ws.load(w, 0, 16, nb_list[0] * 512, 512)
        with cx.phase():
            nt = NormT(cx, E["gains"][0], E["ident"], E["ident_b"])
            xin = [cx.sb([128, D], F32) for _ in range(4)]
            xin_b = [Buf() for _ in range(4)]
            for tt in range(SEG // 128):
                j = tt % 4
                s.op("sp", "dma_start", dict(out=xin[j][:, :], in_=E["xw"][w0 + tt * 128:w0 + (tt + 1) * 128, :]),
                     writes=[xin_b[j]])
                nt.run(xin[j][:, :], xin_b[j], XT, XT_b, tt * 128)
        pg = [cx.ps([128, 512], F32) for _ in range(4)]
        pg_b = [Buf() for _ in range(4)]
        tmp8 = [cx.sb([128, 512], F32) for _ in range(8)]
        tmp8_b = [Buf() for _ in range(8)]
        tq = 0
        ost = [cx.sb([128, 4, 512], BF16) for _ in range(2)]
        ost_b = [Buf() for _ in range(2)]
        ov = [cx.sb([128, 512], BF16) for _ in range(3)]
        ov_b = [Buf() for _ in range(3)]
        oc = vc = pk = 0
        for nb in nb_list:
            wt, wb = pre_w if nb == nb_list[0] else ws.load(w, 0, 16, nb * 512, 512)
            if nb < 8:
                isq = nb < 4
                (cst, cstb), (snt, sntb) = (tab[0], tab[1]) if isq else (tab[2], tab[3])
                row0 = (nb % 4) * 512
                for tb in range(SEG // 512):
                    o, o_b = ost[oc % 2], ost_b[oc % 2]
                    oc += 1
                    tsl = slice(tb * 512, (tb + 1) * 512)
                    for pr in range(2):
                        pp = []
                        for half in range(2):
                            c = pr * 2 + half
                            p, p_b = pg[pk % 4], pg_b[pk % 4]
                            pk += 1
                            pp.append((p, p_b))
                            for kc in range(16):
                                s.op("pe", "matmul", dict(out=p[:, :], lhsT=wt[:, kc, c * 128:(c + 1) * 128],
                                                          rhs=XT[:, kc, tsl], start=(kc == 0), stop=(kc == 15)),
                                     reads=[wb, XT_b], writes=[p_b])
                        (x1, x1b), (x2, x2b) = pp
                        tmp = tmp8[(tq % 2) * 4:(tq % 2) * 4 + 4]
                        tmp_b = tmp8_b[(tq % 2) * 4:(tq % 2) * 4 + 4]
                        tq += 1
                        s.op("dve", "tensor_tensor", dict(out=tmp[0][:, :], in0=x1[:, :], in1=cst[:, tsl], op=ALU.mult),
                             reads=[x1b, cstb], writes=[tmp_b[0]])
                        s.op("dve", "tensor_tensor", dict(out=tmp[1][:, :], in0=x2[:, :], in1=snt[:, tsl], op=ALU.mult),
                             reads=[x2b, sntb], writes=[tmp_b[1]])
                        s.op("dve", "tensor_tensor", dict(out=tmp[2][:, :], in0=x2[:, :], in1=cst[:, tsl], op=ALU.mult),
                             reads=[x2b, cstb], writes=[tmp_b[2]])
                        s.op("dve", "tensor_tensor", dict(out=tmp[3][:, :], in0=x1[:, :], in1=snt[:, tsl], op=ALU.mult),
                             reads=[x1b, sntb], writes=[tmp_b[3]])
                        s.op("dve", "tensor_tensor", dict(out=o[:, pr * 2, :], in0=tmp[0][:, :], in1=tmp[1][:, :],
                                                          op=ALU.subtract),
                             reads=[tmp_b[0], tmp_b[1]], writes=[o_b], part=pr)
                        s.op("dve", "tensor_tensor", dict(out=o[:, pr * 2 + 1, :], in0=tmp[2][:, :], in1=tmp[3][:, :],
                                                          op=ALU.add),
                             reads=[tmp_b[2], tmp_b[3]], writes=[o_b], part=1)
                    if isq:
                        dst = E["qT_s"][row0:row0 + 512, (seg - 2) * SEG + tb * 512:(seg - 2) * SEG + (tb + 1) * 512]
                    else:
                        dst = E["kT_s"][row0:row0 + 512, w0 + tb * 512:w0 + (tb + 1) * 512]
                    s.op("sp", "dma_start", dict(out=dst.rearrange("(c p) t -> p c t", p=128), in_=o[:, :, :]),
                         reads=[o_b])
            else:
                isv = nb < 16
                col0 = ((nb - 8) % 8) * 512
                for tt in range(SEG // 128):
                    p, p_b = pg[pk % 4], pg_b[pk % 4]
                    pk += 1
                    for kc in range(16):
                        s.op("pe", "matmul", dict(out=p[:, :], lhsT=XT[:, kc, tt * 128:(tt + 1) * 128],
                                                  rhs=wt[:, kc, :], start=(kc == 0), stop=(kc == 15)),
                             reads=[wb, XT_b], writes=[p_b])
                    o, o_b = ov[vc % 3], ov_b[vc % 3]
                    vc += 1
                    s.op("act", "activation", dict(out=o[:, :], in_=p[:, :], func=(AF.Copy if isv else AF.Silu)),
                         reads=[p_b], writes=[o_b])
                    if isv:
                        dst = E["v_s"][w0 + tt * 128:w0 + (tt + 1) * 128, col0:col0 + 512]
                    else:
                        r0 = (seg - 2) * SEG + tt * 128
                        dst = E["sg_s"][r0:r0 + 128, col0:col0 + 512]
                    s.op("sp", "dma_start", dict(out=dst, in_=o[:, :]), reads=[o_b])


def phase_B(cx, E, NH=4, G=4):
    s = cx.s
    NCH = 4 * SEG // 128
    NPRE = 2 * SEG // 128
    with cx.phase():
        S_ = [cx.sb([128, 2, 512], F32) for _ in range(RET_H)]
        S_b = [Buf() for _ in range(RET_H)]
        Sbf = [[cx.sb([128, 2, 512], BF16) for _ in range(2)] for _ in range(RET_H)]
        Sbf_b = [[Buf() for _ in range(2)] for _ in range(RET_H)]
        with cx.phase():
            GP = 8
            if DBG.get("skip_pre"):
                raise_skip = True
            else:
                raise_skip = False
            kp = [cx.sb([128, 2, GP * 128], BF16) for _ in range(2)]
            vp = [cx.sb([128, GP, 512], BF16) for _ in range(2)]
            in_b = [Buf() for _ in range(2)]
            pre = [cx.sb([128, NPRE], F32) for _ in range(2)]
            pre_b = [Buf() for _ in range(2)]
            psS = [[cx.ps([128, 512], F32) for _ in range(2)] for _ in range(2)]
            psS_b = [[Buf() for _ in range(2)] for _ in range(2)]
            psk = [cx.ps([128, 1024], BF16) for _ in range(2)]
            psk_b = [Buf() for _ in range(2)]
            kd = [cx.sb([128, 256], BF16) for _ in range(3)]
            kd_b = [Buf() for _ in range(3)]
            steps = [(h, n) for h in range(RET_H) for n in range(NPRE)]
            lk = 0

            def pre_front(i):
                nonlocal lk
                h, n = steps[i]
                g, gi = n // GP, n % GP
                if n == 0:
                    s.op("sp", "dma_start", dict(out=pre[h % 2][:, :], in_=E["rpre"][h]), writes=[pre_b[h % 2]])
                if gi == 0:
                    gb = lk % 2
                    lk += 1
                    tsl = slice(g * GP * 128, (g + 1) * GP * 128)
                    s.op("sp", "dma_start", dict(out=kp[gb][:, :, :], in_=E["kT_s"][h * 256:(h + 1) * 256, tsl]
                                                 .rearrange("(c p) t -> p c t", p=128)), writes=[in_b[gb]])
                    s.op("sp", "dma_start", dict(out=vp[gb][:, :, :], in_=E["v_s"][tsl, h * 512:(h + 1) * 512]
                                                 .rearrange("(g p) f -> p g f", p=128)), writes=[in_b[gb]], part=1)
                gb = (h * (NPRE // GP) + g) % 2
                csl = slice(gi * 128, (gi + 1) * 128)
                for dc in range(2):
                    s.op("pe", "transpose", dict(out=psk[i % 2][:, dc * 128:(dc + 1) * 128], in_=kp[gb][:, dc, csl],
                                                 identity=E["ident"][:, :]),
                         reads=[in_b[gb], E["ident_b"]], writes=[psk_b[i % 2]])
                s.op("dve", "tensor_scalar", dict(out=kd[i % 3][:, :], in0=psk[i % 2][:, 0:256],
                                                  scalar1=(pre[h % 2][:, 0:1] if DBG.get("pre_col0") else pre[h % 2][:, n:n + 1]),
                                                  scalar2=None, op0=ALU.mult),
                     reads=[psk_b[i % 2], pre_b[h % 2]], writes=[kd_b[i % 3]])

            def pre_back(i):
                h, n = steps[i]
                g, gi = n // GP, n % GP
                gb = (h * (NPRE // GP) + g) % 2
                for dc in range(0 if DBG.get("pre_nomm") else 2):
                    s.op("pe", "matmul", dict(out=psS[h % 2][dc][:, :], lhsT=kd[i % 3][:, dc * 128:(dc + 1) * 128],
                                              rhs=vp[gb][:, gi, :], start=(n == 0), stop=(n == NPRE - 1)),
                         reads=[kd_b[i % 3], in_b[gb]], writes=[psS_b[h % 2][dc]])
                if n == NPRE - 1 and not DBG.get("pre_noevac"):
                    for dc in range(2):
                        if not DBG.get("evac_noact"):
                            s.op("act", "copy", dict(out=S_[h][:, dc, :], in_=psS[h % 2][dc][:, :]),
                                 reads=[psS_b[h % 2][dc]], writes=[S_b[h]], part=dc)
                        s.op("pool", "tensor_copy", dict(out=Sbf[h][0][:, dc, :], in_=S_[h][:, dc, :]),
                             reads=[S_b[h]], writes=[Sbf_b[h][0]], part=dc)

            for i in range(0 if raise_skip else len(steps) + 1):
                if i < len(steps):
                    pre_front(i)
                if i >= 1:
                    pre_back(i - 1)
        with cx.phase():
            _ps_s = cx.ps([128, 512], F32)
            ps_s = [_ps_s, _ps_s]
            _ps_s_b = Buf()
            ps_s_b = [_ps_s_b, _ps_s_b]
            ps_o = [cx.ps([128, 512], F32) for _ in range(NH)]
            ps_o_b = [Buf() for _ in range(NH)]
            _ps_k = cx.ps([128, 1024], BF16)
            ps_k = [_ps_k, _ps_k]
            _ps_k_b = Buf()
            ps_k_b = [_ps_k_b, _ps_k_b]
            ps_st = [cx.ps([128, 512], F32) for _ in range(2)]
            ps_st_b = [Buf() for _ in range(2)]
            stA = [cx.sb([128, 2 * NH], F32) for _ in range(3)]
            stA_b = [Buf() for _ in range(3)]
            po_sb = [[cx.sb([128, 512], F32) for _ in range(NH)] for _ in range(4)]
            po_sb_b = [[Buf() for _ in range(NH)] for _ in range(4)]
            PT = [[cx.sb([128, 128], BF16) for _ in range(NH)] for _ in range(2)]
            PT_b = [[Buf() for _ in range(NH)] for _ in range(2)]
            qdT = [[cx.sb([128, 2, 128], BF16) for _ in range(NH)] for _ in range(2)]
            qdT_b = [[Buf() for _ in range(NH)] for _ in range(2)]
            kd = [[cx.sb([128, 256], BF16) for _ in range(NH)] for _ in range(2)]
            kd_b = [[Buf() for _ in range(NH)] for _ in range(2)]
            sq = cx.sb([128, 512], BF16)
            sq_b = Buf()
            st = [cx.sb([128, 2], F32) for _ in range(NH)]
            st_b = [Buf() for _ in range(NH)]
            ob = [cx.sb([128, 512], BF16) for _ in range(4)]
            ob_b = [Buf() for _ in range(4)]
            C = []
            for h in range(RET_H):
                c = {}
                c["mask"] = cx.sb([128, 128], F32)
                c["qd"] = cx.sb([128, 128], F32)
                c["col"] = cx.sb([128, 2], F32)
                c["b"] = Buf()
                s.op("sp", "dma_start", dict(out=c["mask"][:, :], in_=E["rmask"][h, 0]), writes=[c["b"]])
                s.op("sp", "dma_start", dict(out=c["qd"][:, :], in_=E["rmask"][h, 1]), writes=[c["b"]], part=1)
                s.op("sp", "dma_start", dict(out=c["col"][:, :], in_=E["rcol"][h]), writes=[c["b"]], part=1)
                C.append(c)
            U = []
            for u in range(NH):
                d = {}
                d["q"] = [cx.sb([128, 2, G * 128], BF16) for _ in range(2)]
                d["k"] = [cx.sb([128, 2, G * 128], BF16) for _ in range(2)]
                d["v"] = [cx.sb([128, G, 512], BF16) for _ in range(2)]
                d["in_b"] = [Buf() for _ in range(2)]
                U.append(d)
            cnt = dict(oc=0, sk=0, ok=0)
            stepl = [(hp, n) for hp in range(0 if DBG.get("skip_full") else RET_H // NH) for n in range(NPRE, NCH)]

            def geom(k):
                hp, n = stepl[k]
                m = n - NPRE
                g, gi = m // G, m % G
                return hp, n, m, g, gi, g % 2, slice(gi * 128, (gi + 1) * 128), n == NCH - 1

            def stage_ab(k):
                hp, n, m, g, gi, gb, csl, last = geom(k)
                kk = k % 2
                if gi == 0:
                    tsl = slice(n * 128, (n + G) * 128)
                    qsl = slice(m * 128, (m + G) * 128)
                    for u in range(NH):
                        d = U[u]
                        h = hp * NH + u
                        s.op("sp", "dma_start", dict(out=d["k"][gb][:, :, :],
                                                     in_=E["kT_s"][h * 256:(h + 1) * 256, tsl]
                                                     .rearrange("(c p) t -> p c t", p=128)), writes=[d["in_b"][gb]])
                        s.op("sp", "dma_start", dict(out=d["v"][gb][:, :, :],
                                                     in_=E["v_s"][tsl, h * 512:(h + 1) * 512]
                                                     .rearrange("(g p) f -> p g f", p=128)),
                             writes=[d["in_b"][gb]], part=1)
                        s.op("sp", "dma_start", dict(out=d["q"][gb][:, :, :],
                                                     in_=E["qT_s"][h * 256:(h + 1) * 256, qsl]
                                                     .rearrange("(c p) t -> p c t", p=128)),
                             writes=[d["in_b"][gb]], part=1)
                for u in range(NH):
                    d = U[u]
                    inb = d["in_b"][gb]
                    qc, kc_ = d["q"][gb], d["k"][gb]
                    for dc in range(2):
                        s.op("pe", "matmul", dict(out=ps_s[kk][:, u * 128:(u + 1) * 128], lhsT=kc_[:, dc, csl],
                                                  rhs=qc[:, dc, csl], start=(dc == 0), stop=(dc == 1)),
                             reads=[inb], writes=[ps_s_b[kk]])
                    if not last:
                        for dc in range(2):
                            s.op("pe", "transpose", dict(
                                out=ps_k[kk][:, u * 256 + dc * 128:u * 256 + (dc + 1) * 128],
                                in_=kc_[:, dc, csl], identity=E["ident"][:, :]),
                                reads=[inb, E["ident_b"]], writes=[ps_k_b[kk]])
                for u in range(NH):
                    d = U[u]
                    c = C[hp * NH + u]
                    inb = d["in_b"][gb]
                    qc = d["q"][gb]
                    s.op("dve", "tensor_tensor", dict(out=PT[kk][u][:, :], in0=ps_s[kk][:, u * 128:(u + 1) * 128],
                                                      in1=c["mask"][:, :], op=ALU.mult),
                         reads=[ps_s_b[kk], c["b"]], writes=[PT_b[kk][u]])
                    s.op("pool", "tensor_tensor", dict(out=qdT[kk][u][:, :, :], in0=qc[:, :, csl],
                                                       in1=c["qd"][:, :].unsqueeze(1).broadcast_to([128, 2, 128]),
                                                       op=ALU.mult), reads=[inb, c["b"]], writes=[qdT_b[kk][u]])
                    if not last:
                        s.op("dve", "tensor_scalar", dict(out=kd[kk][u][:, :], in0=ps_k[kk][:, u * 256:(u + 1) * 256],
                                                          scalar1=c["col"][:, 0:1], scalar2=None, op0=ALU.mult),
                             reads=[ps_k_b[kk], c["b"]], writes=[kd_b[kk][u]])

            def stage_cd(k):
                hp, n, m, g, gi, gb, csl, last = geom(k)
                kk = k % 2
                for u in range(NH):
                    d = U[u]
                    h = hp * NH + u
                    c = C[h]
                    inb = d["in_b"][gb]
                    vc = d["v"][gb]
                    sb_old, sb_old_b = Sbf[h][m % 2], Sbf_b[h][m % 2]
                    sb_new, sb_new_b = Sbf[h][(m + 1) % 2], Sbf_b[h][(m + 1) % 2]
                    po, po_b = ps_o[u], ps_o_b[u]
                    s.op("pe", "matmul", dict(out=po[:, :], lhsT=PT[kk][u][:, :], rhs=vc[:, gi, :],
                                              start=True, stop=False), reads=[PT_b[kk][u], inb], writes=[po_b])
                    for dc in range(2):
                        s.op("pe", "matmul", dict(out=po[:, :], lhsT=qdT[kk][u][:, dc, :], rhs=sb_old[:, dc, :],
                                                  start=False, stop=(dc == 1)),
                             reads=[qdT_b[kk][u], sb_old_b], writes=[po_b])
                    s.op("act", "copy", dict(out=po_sb[k % 4][u][:, :], in_=po[:, :]), reads=[po_b],
                         writes=[po_sb_b[k % 4][u]])
                    if not last:
                        for dc in range(2):
                            pst, pst_b = ps_st[cnt["sk"] % 2], ps_st_b[cnt["sk"] % 2]
                            cnt["sk"] += 1
                            s.op("pe", "matmul", dict(out=pst[:, :], lhsT=kd[kk][u][:, dc * 128:(dc + 1) * 128],
                                                      rhs=vc[:, gi, :], start=True, stop=True),
                                 reads=[kd_b[kk][u], inb], writes=[pst_b])
                            s.op("dve", "scalar_tensor_tensor", dict(out=S_[h][:, dc, :], in0=S_[h][:, dc, :],
                                                                     scalar=c["col"][:, 1:2], in1=pst[:, :],
                                                                     op0=ALU.mult, op1=ALU.add),
                                 reads=[pst_b, c["b"], S_b[h]], writes=[S_b[h]])
                            s.op("act", "copy", dict(out=sb_new[:, dc, :], in_=S_[h][:, dc, :]),
                                 reads=[S_b[h]], writes=[sb_new_b], part=dc)
            def rms_act(k):
                for u in range(NH):
                    s.op("act", "activation", dict(out=sq[:, :], in_=po_sb[k % 4][u][:, :], func=AF.Square,
                                                   accum_out=stA[k % 3][:, u:u + 1]),
                         reads=[po_sb_b[k % 4][u]], writes=[sq_b, stA_b[k % 3]], part=u, partial=[stA_b[k % 3]])

            def rms_fin(k):
                hp, n, m, g, gi, gb, csl, last = geom(k)
                a, a_b = stA[k % 3], stA_b[k % 3]
                rsqrt_small(s, a[:, NH:2 * NH], a[:, 0:NH], a_b, 1.0 / RET_DV, EPS)
                for u in range(NH):
                    h = hp * NH + u
                    oo, oo_b = ob[cnt["oc"] % 4], ob_b[cnt["oc"] % 4]
                    cnt["oc"] += 1
                    s.op("dve", "tensor_scalar", dict(out=oo[:, :], in0=po_sb[k % 4][u][:, :],
                                                      scalar1=a[:, NH + u:NH + u + 1], scalar2=None, op0=ALU.mult),
                         reads=[po_sb_b[k % 4][u], a_b], writes=[oo_b])
                    s.op("sp", "dma_start", dict(out=E["o_s"][m * 128:(m + 1) * 128, h * 512:(h + 1) * 512],
                                                 in_=oo[:, :]), reads=[oo_b])

            NS_ = len(stepl)
            for k in range(NS_ + 3):
                if k < NS_:
                    stage_ab(k)
                if 0 <= k - 1 < NS_:
                    stage_cd(k - 1)
                if 0 <= k - 3 < NS_:
                    rms_fin(k - 3)
                if 0 <= k - 2 < NS_:
                    rms_act(k - 2)


def phase_CE(cx, E, mode, blocks, wo_d, gain, w1_d, w2_d, TB=1024):
    s = cx.s
    KM = 4096 if mode == "ret" else 2048
    NT = TB // 128
    with cx.phase():
        Y = cx.sb([128, NT, D], F32)
        Y_b = [Buf() for _ in range(NT)]
        XT = cx.sb([128, 16, TB], BF16)
        XT_b = Buf()
        nt = NormT(cx, gain, E["ident"], E["ident_b"])
        ws = WStream(cx)
        GT = [cx.sb([128, 4, TB], BF16) for _ in range(2)]
        GT_b = [Buf() for _ in range(2)]
        pg = [cx.ps([128, 512], F32) for _ in range(4)]
        pg_b = [Buf() for _ in range(4)]
        pk = [0]
        gk = 0
        if mode == "ret":
            oin = [cx.sb([128, 512], BF16) for _ in range(2)]
            sgin = [cx.sb([128, 512], BF16) for _ in range(2)]
            in_b = [Buf() for _ in range(2)]
            prod = [cx.sb([128, 512], BF16) for _ in range(2)]
            prod_b = [Buf() for _ in range(2)]
            ptr = [cx.ps([128, 1024], BF16) for _ in range(2)]
            ptr_b = [Buf() for _ in range(2)]
            ik = 0
        rl = [cx.sb([128, 512], F32) for _ in range(2)]
        rl_b = [Buf() for _ in range(2)]
        rk = 0

        def acc_gemm(A, A_b, W, W_b, nkc):
            for tt in range(NT):
                for nb in range(4):
                    p, p_b = pg[pk[0] % 4], pg_b[pk[0] % 4]
                    pk[0] += 1
                    for kc in range(nkc):
                        s.op("pe", "matmul", dict(out=p[:, :], lhsT=A[:, kc, tt * 128:(tt + 1) * 128],
                                                  rhs=W[:, kc, nb * 512:(nb + 1) * 512],
                                                  start=(kc == 0), stop=(kc == nkc - 1)),
                             reads=[A_b, W_b], writes=[p_b])
                    ysl = Y[:, tt, nb * 512:(nb + 1) * 512]
                    s.op("dve", "tensor_tensor", dict(out=ysl, in0=ysl, in1=p[:, :], op=ALU.add),
                         reads=[p_b, Y_b[tt]], writes=[Y_b[tt]])

        for resid, mix, out in blocks:
            for tt in range(NT):
                s.op("sp", "dma_start", dict(out=Y[:, tt, :], in_=resid[tt * 128:(tt + 1) * 128, :]), writes=[Y_b[tt]])
            for kb in range(KM // 512):
                W, W_b = ws.load(wo_d, kb * 512, 4, 0, D)
                g, g_b = GT[gk % 2], GT_b[gk % 2]
                gk += 1
                if mode == "att":
                    s.op("sp", "dma_start", dict(out=g[:, :, :], in_=mix[kb * 512:(kb + 1) * 512, :]
                                                 .rearrange("(c p) t -> p c t", p=128)), writes=[g_b])
                else:
                    o_in, sg_in = mix
                    for tt in range(NT):
                        j = ik % 2
                        ik += 1
                        rows = slice(tt * 128, (tt + 1) * 128)
                        cols = slice(kb * 512, (kb + 1) * 512)
                        s.op("sp", "dma_start", dict(out=oin[j][:, :], in_=o_in[rows, cols]), writes=[in_b[j]])
                        s.op("sp", "dma_start", dict(out=sgin[j][:, :], in_=sg_in[rows, cols]), writes=[in_b[j]], part=1)
                        s.op("pool", "tensor_tensor", dict(out=prod[j][:, :], in0=oin[j][:, :], in1=sgin[j][:, :],
                                                           op=ALU.mult), reads=[in_b[j]], writes=[prod_b[j]])
                        for c in range(4):
                            s.op("pe", "transpose", dict(out=ptr[j][:, c * 128:(c + 1) * 128],
                                                         in_=prod[j][:, c * 128:(c + 1) * 128], identity=E["ident"][:, :]),
                                 reads=[prod_b[j], E["ident_b"]], writes=[ptr_b[j]])
                        s.op("act", "copy", dict(out=g[:, :, tt * 128:(tt + 1) * 128],
                                                 in_=ptr[j][:, 0:512].rearrange("p (c t) -> p c t", t=128)),
                             reads=[ptr_b[j]], writes=[g_b], part=tt)
                acc_gemm(g, g_b, W, W_b, 4)
            for tt in range(NT):
                nt.run(Y[:, tt, :], Y_b[tt], XT, XT_b, tt * 128)
            for hb in range(DFF // 512):
                W1, W1_b = ws.load(w1_d, 0, 16, hb * 512, 512)
                W2, W2_b = ws.load(w2_d, hb * 512, 4, 0, D)
                u, u_b = GT[gk % 2], GT_b[gk % 2]
                gk += 1
                first = True
                for hc in range(4):
                    for t2 in range(TB // 512):
                        p, p_b = pg[pk[0] % 4], pg_b[pk[0] % 4]
                        pk[0] += 1
                        for kc in range(16):
                            s.op("pe", "matmul", dict(out=p[:, :], lhsT=W1[:, kc, hc * 128:(hc + 1) * 128],
                                                      rhs=XT[:, kc, t2 * 512:(t2 + 1) * 512],
                                                      start=(kc == 0), stop=(kc == 15)),
                                 reads=[W1_b, XT_b], writes=[p_b])
                        r, r_b = rl[rk % 2], rl_b[rk % 2]
                        rk += 1
                        s.op("act", "activation", dict(out=r[:, :], in_=p[:, :], func=AF.Relu), reads=[p_b], writes=[r_b])
                        s.op("pool", "tensor_tensor", dict(out=u[:, hc, t2 * 512:(t2 + 1) * 512], in0=r[:, :], in1=r[:, :],
                                                           op=ALU.mult), reads=[r_b], writes=[u_b],
                             part=(0 if first else 1))
                        first = False
                acc_gemm(u, u_b, W2, W2_b, 4)
            for tt in range(NT):
                s.op("sp", "dma_start", dict(out=out[tt * 128:(tt + 1) * 128, :], in_=Y[:, tt, :]), reads=[Y_b[tt]])


def phase_C2(cx, E, sidx):
    s = cx.s
    with cx.phase():
        cos32, cos_b = load_const(cx, E["atabs"][0][:, sidx * SEG:(sidx + 1) * SEG], [32, SEG], F32)
        sin32, sin_b = load_const(cx, E["atabs"][1][:, sidx * SEG:(sidx + 1) * SEG], [32, SEG], F32)
        gk, gk_b = load_const(cx, E["qkg"], [128, 2], F32)
        s.op("dve", "tensor_scalar", dict(out=gk[:, 0:1], in0=gk[:, 0:1], scalar1=float(ATT_DH ** -0.5),
                                          scalar2=None, op0=ALU.mult), reads=[gk_b], writes=[gk_b])
        XT = cx.sb([128, 16, SEG], BF16)
        XT_b = Buf()
        ws = WStream(cx, nbuf=2)
        w = E["w_att_in"]
        nbs = list(range(36) if sidx == 1 else range(12, 36))
        qk_nbs = [nb for nb in nbs if nb < 24]
        v_nbs = [nb for nb in nbs if nb >= 24]
        wtab = {0: ws.load(w, 0, 16, qk_nbs[0] * 512, 512)}
        with cx.phase():
            nt = NormT(cx, E["gains"][2], E["ident"], E["ident_b"])
            xin = [cx.sb([128, D], F32) for _ in range(4)]
            xin_b = [Buf() for _ in range(4)]
            for tt in range(SEG // 128):
                j = tt % 4
                r0 = sidx * SEG + tt * 128
                s.op("sp", "dma_start", dict(out=xin[j][:, :], in_=E["h1_s"][r0:r0 + 128, :]), writes=[xin_b[j]])
                nt.run(xin[j][:, :], xin_b[j], XT, XT_b, tt * 128)
        NS = 3
        pg = [[cx.ps([128, 512], F32) for _ in range(2)] for _ in range(NS)]
        pg_b = [[Buf() for _ in range(2)] for _ in range(NS)]
        ps2 = cx.ps([128, 512], F32)
        ps2_b = Buf()
        ps3 = cx.ps([128, 512], F32)
        ps3_b = Buf()
        sqb = [cx.sb([128, 2, 512], BF16) for _ in range(2)]
        sqb_b = [[Buf() for _ in range(2)] for _ in range(2)]
        ms = [cx.sb([128, 2, 512], F32) for _ in range(2)]
        ms_b = [Buf() for _ in range(2)]
        ob = [cx.sb([128, 4, SEG], BF16) for _ in range(2)]
        ob_b = [[Buf() for _ in range(4)] for _ in range(2)]
        t1 = [cx.sb([32, 512], F32) for _ in range(2)]
        t1_b = [Buf() for _ in range(2)]
        t2 = [cx.sb([32, 512], F32) for _ in range(2)]
        t2_b = [Buf() for _ in range(2)]
        ov = [cx.sb([128, 512], BF16) for _ in range(3)]
        ov_b = [Buf() for _ in range(3)]
        pairs = []
        def tbs_of(grp):
            return [SEG // 512 - 1] if (sidx == 0 and grp < 2) else list(range(SEG // 512))

        for bi, nb in enumerate(qk_nbs):
            for tb in tbs_of((nb % 12) // 4):
                for pr in range(2):
                    pairs.append((bi, nb, tb, pr))
        def getw(bi):
            if bi not in wtab and bi < len(qk_nbs):
                wtab[bi] = ws.load(w, 0, 16, qk_nbs[bi] * 512, 512)
            return wtab.get(bi)

        def ctxp(P):
            bi, nb, tb, pr = pairs[P]
            isq = nb < 12
            grp = (nb % 12) // 4
            dl = DILS[grp]
            d = dict(bi=bi, nb=nb, tb=tb, pr=pr, isq=isq, grp=grp, dl=dl, ml=512 // dl,
                     gcol=(gk[:, 0:1] if isq else gk[:, 1:2]), o=ob[bi % 2], o_b=ob_b[bi % 2],
                     tsl=slice(tb * 512, (tb + 1) * 512), st=P % NS, s2=P % 2)
            return d

        def pv(d, ap):
            return ap if d["dl"] == 1 else ap.rearrange("p (m c) -> p c m", c=d["dl"])

        def cv(d, ap):
            return ap if d["dl"] == 1 else ap.rearrange("p (c m) -> p c m", c=d["dl"])

        def ovw(d, part, hd):
            o, tb, ml = d["o"], d["tb"], d["ml"]
            if d["dl"] == 1:
                return o[part, hd, d["tsl"]]
            return o[part, hd, :].rearrange("p (c m) -> p c m", c=d["dl"])[:, :, tb * ml:(tb + 1) * ml]

        allp, lowp = slice(0, 128), slice(0, 32)

        def stage_M(P):
            d = ctxp(P)
            wt, wb = getw(d["bi"])
            for hh in range(2):
                hd = d["pr"] * 2 + hh
                p, p_b = pg[d["st"]][hh], pg_b[d["st"]][hh]
                for kc in range(16):
                    s.op("pe", "matmul", dict(out=p[:, :], lhsT=wt[:, kc, hd * 128:(hd + 1) * 128],
                                              rhs=XT[:, kc, d["tsl"]], start=(kc == 0), stop=(kc == 15)),
                         reads=[wb, XT_b], writes=[p_b])
            for hh in range(2):
                s.op("act", "activation", dict(out=sqb[d["s2"]][:, hh, :], in_=pg[d["st"]][hh][:, :], func=AF.Square),
                     reads=[pg_b[d["st"]][hh]], writes=[sqb_b[d["s2"]][hh]])
            if d["tb"] == tbs_of(d["grp"])[0] and d["pr"] == 0:
                getw(d["bi"] + 1)

        def stage_T1(P):
            d = ctxp(P)
            m_, m_b = ms[d["s2"]], ms_b[d["s2"]]
            for hh in range(2):
                s.op("pe", "matmul", dict(out=ps2[:, :], lhsT=E["onesm"][:, :], rhs=sqb[d["s2"]][:, hh, :],
                                          start=True, stop=False),
                     reads=[E["cm_b"], sqb_b[d["s2"]][hh]], writes=[ps2_b])
                s.op("pe", "matmul", dict(out=ps2[:, :], lhsT=E["epsr"][0:1, 0:128], rhs=E["epsr"][0:1, 128:640],
                                          start=False, stop=True),
                     reads=[E["cm_b"]], writes=[ps2_b])
                s.op("act", "activation", dict(out=m_[:, hh, :], in_=ps2[:, :], func=AF.Sqrt),
                     reads=[ps2_b], writes=[m_b], part=hh)

        def stage_T2(P):
            d = ctxp(P)
            m_, m_b = ms[d["s2"]], ms_b[d["s2"]]
            s.op("dve", "reciprocal", dict(out=m_[:, :, :], in_=m_[:, :, :]), reads=[m_b], writes=[m_b])
            for hh in range(2):
                hd = d["pr"] * 2 + hh
                s.op("dve", "scalar_tensor_tensor", dict(out=ovw(d, allp, hd), in0=pv(d, pg[d["st"]][hh][:, :]),
                                                         scalar=d["gcol"], in1=pv(d, m_[:, hh, :]),
                                                         op0=ALU.mult, op1=ALU.mult),
                     reads=[pg_b[d["st"]][hh], m_b, gk_b], writes=[d["o_b"][hd]],
                     part=(0 if d["tb"] == tbs_of(d["grp"])[0] else 1))

        def stage_R(P):
            d = ctxp(P)
            for hh in range(2):
                hd = d["pr"] * 2 + hh
                s.op("pe", "matmul", dict(out=ps3[0:32, :], lhsT=E["swp"][:, 0:32], rhs=ovw(d, allp, hd),
                                          start=True, stop=True),
                     reads=[E["cm_b"], d["o_b"][hd]], writes=[ps3_b])
                s.op("dve", "tensor_tensor", dict(out=cv(d, t2[hh][:, :]), in0=cv(d, ps3[0:32, :]),
                                                  in1=pv(d, sin32[:, d["tsl"]]), op=ALU.mult),
                     reads=[ps3_b, sin_b], writes=[t2_b[hh]])
                s.op("pool", "tensor_tensor", dict(out=cv(d, t1[hh][:, :]), in0=ovw(d, lowp, hd),
                                                   in1=pv(d, cos32[:, d["tsl"]]), op=ALU.mult),
                     reads=[d["o_b"][hd], cos_b], writes=[t1_b[hh]])
                s.op("pool", "tensor_tensor", dict(out=ovw(d, lowp, hd), in0=cv(d, t1[hh][:, :]),
                                                   in1=cv(d, t2[hh][:, :]), op=ALU.add),
                     reads=[t1_b[hh], t2_b[hh]], writes=[d["o_b"][hd]], part=1)
            if d["tb"] == SEG // 512 - 1 and d["pr"] == 1:
                row0 = ((d["nb"] % 12) % 4) * 512
                dst = (E["aq_s"][d["grp"]] if d["isq"] else E["ak_s"][sidx, d["grp"]])[row0:row0 + 512, :]
                dstv = dst.rearrange("(c p) t -> p c t", p=128)
                srcv = d["o"][:, :, :]
                if len(tbs_of(d["grp"])) == 1:
                    ml, tb, dl = d["ml"], d["tb"], d["dl"]
                    for hd in range(4):
                        dv = dstv[:, hd, :].rearrange("p (c m) -> p c m", c=dl)[:, :, tb * ml:(tb + 1) * ml]
                        sv = srcv[:, hd, :].rearrange("p (c m) -> p c m", c=dl)[:, :, tb * ml:(tb + 1) * ml]
                        s.op("sp", "dma_start", dict(out=dv, in_=sv), reads=[d["o_b"][hd]])
                else:
                    s.op("sp", "dma_start", dict(out=dstv, in_=srcv), reads=d["o_b"])

        NP = len(pairs)
        for i in range(NP + 2):
            if i < NP:
                stage_M(i)
            if 0 <= i - 1 < NP:
                stage_T1(i - 1)
            if 0 <= i - 2 < NP:
                stage_R(i - 2)
            if 0 <= i - 1 < NP:
                stage_T2(i - 1)
        pgv = [pg[a][b_] for a in range(NS) for b_ in range(2)]
        pgv_b = [pg_b[a][b_] for a in range(NS) for b_ in range(2)]
        vc = pk = 0
        for nb in v_nbs:
            wt, wb = ws.load(w, 0, 16, nb * 512, 512)
            col0 = (nb - 24) * 512
            vgrp = (nb - 24) // 4
            for tt in (range(SEG // 128 - 4, SEG // 128) if (sidx == 0 and vgrp < 2) else range(SEG // 128)):
                p, p_b = pgv[pk % 6], pgv_b[pk % 6]
                pk += 1
                for kc in range(16):
                    s.op("pe", "matmul", dict(out=p[:, :], lhsT=XT[:, kc, tt * 128:(tt + 1) * 128],
                                              rhs=wt[:, kc, :], start=(kc == 0), stop=(kc == 15)),
                         reads=[wb, XT_b], writes=[p_b])
                oo, oo_b = ov[vc % 3], ov_b[vc % 3]
                vc += 1
                s.op("act", "activation", dict(out=oo[:, :], in_=p[:, :], func=AF.Copy), reads=[p_b], writes=[oo_b])
                r0 = sidx * SEG + tt * 128
                s.op("sp", "dma_start", dict(out=E["av_s"][r0:r0 + 128, col0:col0 + 512], in_=oo[:, :]),
                     reads=[oo_b])


def phase_D(cx, E, LAG=6):
    s = cx.s
    with cx.phase():
        mask, mask_b = load_const(cx, E["amask"][0], [128, 256], BF16)
        maskh, maskh_b = load_const(cx, E["amask"][1], [128, 256], BF16)
        ones, ones_b = load_const(cx, E["aones"], [128, 128], BF16)
        Q = [[cx.sb([128, SEG], BF16) for _ in range(3)] for _ in range(2)]
        Kt = [[cx.sb([128, 2 * SEG], BF16) for _ in range(3)] for _ in range(2)]
        V = [[cx.sb([128, 2 * SEG // 128, 128], BF16) for _ in range(3)] for _ in range(2)]
        in_b = [[Buf() for _ in range(3)] for _ in range(2)]
        ND = [cx.sb([128, 2, SEG], F32) for _ in range(2)]
        ND_b = [Buf() for _ in range(2)]
        NPS = 8
        psb = [cx.ps([128, 512], F32) for _ in range(NPS // 2)]
        ps = [psb[k % 4][:, (k // 4) * 256:(k // 4) * 256 + 256] for k in range(NPS)]
        _b4 = [Buf() for _ in range(4)]
        ps_b = [_b4[k % 4] for k in range(NPS)]
        NPD = 6
        pndb = [cx.ps([128, 512], F32) for _ in range(NPD // 2)]
        pnd = [pndb[k % 3][:, (k // 3) * 256:(k // 3) * 256 + 256] for k in range(NPD)]
        _b3 = [Buf() for _ in range(3)]
        pnd_b = [_b3[k % 3] for k in range(NPD)]
        NX = LAG + 2
        Ex = [cx.sb([128, 256], BF16) for _ in range(NX)]
        Ex_b = [Buf() for _ in range(NX)]
        PT = [cx.sb([128, 256], BF16) for _ in range(NX)]
        PT_b = [Buf() for _ in range(NX)]
        rec = cx.sb([128, SEG], F32)
        rec_b = Buf()
        ao = [cx.sb([128, SEG], BF16) for _ in range(2)]
        ao_b = [Buf() for _ in range(2)]

        def load_head(h):
            hb = h % 2
            rows = slice(h * 128, (h + 1) * 128)
            for g, dl in enumerate(DILS):
                nbk = (SEG // dl) // 128
                b = in_b[hb][g]
                s.op("sp", "dma_start", dict(out=Q[hb][g][:, :], in_=E["aq_s"][g][rows, :]), writes=[b])
                Lseg = SEG // dl
                ml = 512 // dl
                for sg_ in range(2):
                    col0 = (g * ATT_H + h) * 128
                    src = E["av_s"][sg_ * SEG:(sg_ + 1) * SEG, col0:col0 + 128]
                    if sg_ == 0 and g < 2:
                        s.op("sp", "dma_start", dict(
                            out=Kt[hb][g][:, 0:SEG].rearrange("p (c m) -> p c m", c=dl)[:, :, Lseg - ml:Lseg],
                            in_=E["ak_s"][0, g][rows, :].rearrange("p (c m) -> p c m", c=dl)[:, :, Lseg - ml:Lseg]),
                            writes=[b], part=1)
                        for c in range(dl):
                            if dl == 1:
                                sv = src[SEG - 128:SEG, :]
                            else:
                                sv = src.rearrange("(n k c) d -> c k n d", k=128, c=dl)[c][:, nbk - 1, :]
                            s.op("sp", "dma_start", dict(out=V[hb][g][:, c * nbk + nbk - 1, :], in_=sv),
                                 writes=[b], part=1)
                        continue
                    s.op("sp", "dma_start", dict(out=Kt[hb][g][:, sg_ * SEG:(sg_ + 1) * SEG],
                                                 in_=E["ak_s"][sg_, g][rows, :]), writes=[b], part=1)
                    if dl == 1:
                        s.op("sp", "dma_start", dict(out=V[hb][g][:, sg_ * 16:(sg_ + 1) * 16, :],
                                                     in_=src.rearrange("(n k) d -> k n d", k=128)),
                             writes=[b], part=1)
                    else:
                        for c in range(dl):
                            s.op("sp", "dma_start", dict(
                                out=V[hb][g][:, sg_ * 16 + c * nbk:sg_ * 16 + (c + 1) * nbk, :],
                                in_=src.rearrange("(n k c) d -> c k n d", k=128, c=dl)[c]),
                                writes=[b], part=1)

        blocks = []
        for h in range(ATT_H):
            for g, dl in enumerate(DILS):
                Lseg = SEG // dl
                nbk = Lseg // 128
                for c in range(dl):
                    for n in range(nbk):
                        blocks.append((h, g, dl, c, n, Lseg, nbk))
        NBLK = len(blocks)
        per_head = NBLK // ATT_H

        def front(i):
            h, g, dl, c, n, Lseg, nbk = blocks[i]
            hb = h % 2
            if i % per_head == 0:
                s.op("pool", "memset", dict(ap=ND[hb][:, :, :], constant=0.0), writes=[ND_b[hb]])
            colq = c * Lseg + n * 128
            colk = SEG + colq
            colp = (colk - 128) if n > 0 else (c * Lseg + (nbk - 1) * 128)
            mk_, mk_b = (mask, mask_b) if n > 0 else (maskh, maskh_b)
            j, jx = i % NPS, i % NX
            qs = Q[hb][g][:, colq:colq + 128]
            b = in_b[hb][g]
            s.op("pe", "matmul", dict(out=ps[j][:, 0:128], lhsT=Kt[hb][g][:, colp:colp + 128], rhs=qs,
                                      start=True, stop=True), reads=[b], writes=[ps_b[j]])
            s.op("pe", "matmul", dict(out=ps[j][:, 128:256], lhsT=Kt[hb][g][:, colk:colk + 128], rhs=qs,
                                      start=True, stop=True), reads=[b], writes=[ps_b[j]])
            s.op("act", "activation", dict(out=Ex[jx][:, :], in_=ps[j], func=AF.Exp),
                 reads=[ps_b[j]], writes=[Ex_b[jx]])
            s.op("pool", "tensor_tensor", dict(out=PT[jx][:, :], in0=Ex[jx][:, :], in1=mk_[:, :], op=ALU.mult),
                 reads=[Ex_b[jx], mk_b], writes=[PT_b[jx]])

        def back(i):
            h, g, dl, c, n, Lseg, nbk = blocks[i]
            hb = h % 2
            colq = c * Lseg + n * 128
            colk = SEG + colq
            colp = (colk - 128) if n > 0 else (c * Lseg + (nbk - 1) * 128)
            jx, jd = i % NX, i % NPD
            bo, bp = colk // 128, colp // 128
            b = in_b[hb][g]
            Vg = V[hb][g]
            s.op("pe", "matmul", dict(out=pnd[jd][:, 0:128], lhsT=Vg[:, bp, :], rhs=PT[jx][:, 0:128],
                                      start=True, stop=False), reads=[b, PT_b[jx]], writes=[pnd_b[jd]])
            s.op("pe", "matmul", dict(out=pnd[jd][:, 0:128], lhsT=Vg[:, bo, :], rhs=PT[jx][:, 128:256],
                                      start=False, stop=True), reads=[b, PT_b[jx]], writes=[pnd_b[jd]])
            s.op("pe", "matmul", dict(out=pnd[jd][:, 128:256], lhsT=ones[:, :], rhs=PT[jx][:, 0:128],
                                      start=True, stop=False), reads=[ones_b, PT_b[jx]], writes=[pnd_b[jd]])
            s.op("pe", "matmul", dict(out=pnd[jd][:, 128:256], lhsT=ones[:, :], rhs=PT[jx][:, 128:256],
                                      start=False, stop=True), reads=[ones_b, PT_b[jx]], writes=[pnd_b[jd]])
            t0 = n * 128 * dl + c
            dst = ND[hb][:, :, t0:t0 + 127 * dl + 1:dl]
            s.op("dve", "tensor_tensor", dict(out=dst, in0=dst, in1=pnd[jd].rearrange("p (a t) -> p a t", a=2),
                                              op=ALU.add), reads=[pnd_b[jd], ND_b[hb]], writes=[ND_b[hb]])
            if i % per_head == per_head - 1:
                rows = slice(h * 128, (h + 1) * 128)
                s.op("dve", "reciprocal", dict(out=rec[:, :], in_=ND[hb][:, 1, :]), reads=[ND_b[hb]], writes=[rec_b])
                a, a_b = ao[hb], ao_b[hb]
                s.op("dve", "tensor_tensor", dict(out=a[:, :], in0=ND[hb][:, 0, :], in1=rec[:, :], op=ALU.mult),
                     reads=[ND_b[hb], rec_b], writes=[a_b])
                s.op("sp", "dma_start", dict(out=E["a_s"][rows, :], in_=a[:, :]), reads=[a_b])

        load_head(0)
        load_head(1)
        for i in range(NBLK + LAG):
            if i < NBLK:
                front(i)
            if i - LAG >= 0:
                back(i - LAG)
                hdone = blocks[i - LAG][0]
                if (i - LAG) % per_head == per_head - 1 and hdone + 2 < ATT_H:
                    load_head(hdone + 2)


def build_fused(only=None):
    nc = bass.Bass("TRN2", target_bir_lowering=False)
    cx = Ctx(nc)
    s = cx.s
    E = {}
    E["xw"] = cx.din("xw", [4 * SEG, D], F32)
    E["rtabs"] = cx.din("rtabs", [4, 128, 4 * SEG], F32)
    E["atabs"] = cx.din("atabs", [2, 32, 2 * SEG], F32)
    E["gains"] = cx.din("gains", [4, 128, 16], F32)
    E["qkg"] = cx.din("qkg", [128, 2], F32)
    E["w_ret_in"] = cx.din("w_ret_in", [D, 12288], F32)
    E["w_ret_out"] = cx.din("w_ret_out", [4096, D], F32)
    E["w_att_in"] = cx.din("w_att_in", [D, 3 * NQ], F32)
    E["w_att_out"] = cx.din("w_att_out", [D, D], F32)
    for l in range(2):
        E["w1_%d" % l] = cx.din("w1_%d" % l, [D, DFF], F32)
        E["w2_%d" % l] = cx.din("w2_%d" % l, [DFF, D], F32)
    identd = cx.din("ident", [128, 128], BF16)
    cmat = cx.din("cmat", [2, 128, 128], BF16)
    E["rmask"] = cx.din("rmask", [RET_H, 2, 128, 128], F32)
    E["rcol"] = cx.din("rcol", [RET_H, 128, 2], F32)
    E["rpre"] = cx.din("rpre", [RET_H, 128, 2 * SEG // 128], F32)
    E["amask"] = cx.din("amask", [2, 128, 256], BF16)
    E["aones"] = cx.din("aones", [128, 128], BF16)
    out = cx.dout("out", [SEG, D], F32)
    E["kT_s"] = cx.scratch("kT_s", [2048, 4 * SEG], BF16)
    E["v_s"] = cx.scratch("v_s", [4 * SEG, 4096], BF16)
    E["qT_s"] = cx.scratch("qT_s", [2048, 2 * SEG], BF16)
    E["sg_s"] = cx.scratch("sg_s", [2 * SEG, 4096], BF16)
    E["o_s"] = cx.scratch("o_s", [2 * SEG, 4096], BF16)
    E["h1_s"] = cx.scratch("h1_s", [2 * SEG, D], F32)
    E["aq_s"] = cx.scratch("aq_s", [3, 2048, SEG], BF16)
    E["ak_s"] = cx.scratch("ak_s", [2, 3, 2048, SEG], BF16)
    E["av_s"] = cx.scratch("av_s", [2 * SEG, NQ], BF16)
    E["a_s"] = cx.scratch("a_s", [2048, SEG], BF16)
    E["ident"], E["ident_b"] = load_const(cx, identd, [128, 128], BF16)
    E["onesm"], E["cm_b"] = load_const(cx, cmat[0], [128, 128], BF16)
    E["swp"], b2 = load_const(cx, cmat[1], [128, 128], BF16)
    E["epsr"], b3 = load_const(cx, cx.din("epsr", [1, 640], BF16), [1, 640], BF16)
    s.barrier()
    steps = []
    for seg in range(4):
        steps.append(("A%d" % seg, lambda seg=seg: phase_A(cx, E, seg, full=(seg >= 2))))
    steps.append(("B", lambda: phase_B(cx, E)))
    def c1():
        blks = []
        for blk in range(4):
            r = slice(blk * 1024, (blk + 1) * 1024)
            blks.append((E["xw"][2 * SEG + blk * 1024:2 * SEG + (blk + 1) * 1024, :],
                         (E["o_s"][r, :], E["sg_s"][r, :]), E["h1_s"][r, :]))
        phase_CE(cx, E, "ret", blks, E["w_ret_out"], E["gains"][1], E["w1_0"], E["w2_0"])
    steps.append(("C1", c1))
    for sidx in range(2):
        steps.append(("C2_%d" % sidx, lambda sidx=sidx: phase_C2(cx, E, sidx)))
    steps.append(("D", lambda: phase_D(cx, E)))
    def e1():
        blks = []
        for blk in range(2):
            r = slice(blk * 1024, (blk + 1) * 1024)
            blks.append((E["h1_s"][SEG + blk * 1024:SEG + (blk + 1) * 1024, :], E["a_s"][:, r], out[r, :]))
        phase_CE(cx, E, "att", blks, E["w_att_out"], E["gains"][3], E["w1_1"], E["w2_1"])
    steps.append(("E", e1))
    for name, fn in steps:
        if only is None or name in only:
            fn()
    s.emit(nc)
    return nc


def kernel(x, norm_mix_gain, norm_mlp_gain, ret_w_in, ret_w_out, att_w_in, att_q_gain, att_k_gain,
           att_w_out, mlp_w_in, mlp_w_out, _only_maps=False):
    f = lambda a: np.ascontiguousarray(np.asarray(a, np.float32))
    x = f(x)
    norm_mix_gain, norm_mlp_gain = f(norm_mix_gain), f(norm_mlp_gain)
    R = S // SEG
    gains = np.ascontiguousarray(np.stack([gainT(norm_mix_gain[0]), gainT(norm_mlp_gain[0]),
                                           gainT(norm_mix_gain[1]), gainT(norm_mlp_gain[1])]))
    qkg = np.ascontiguousarray(np.stack([f(att_q_gain[0]), f(att_k_gain[0])], axis=1))
    rc = [ret_consts(h) for h in range(RET_H)]
    rmask = np.ascontiguousarray(np.stack([a for a, _ in rc]))
    rcol = np.ascontiguousarray(np.stack([b for _, b in rc]))
    npre = 2 * SEG // 128
    rpre = np.ascontiguousarray(np.stack([
        (b[:, 0:1].astype(np.float64) * b[0, 1].astype(np.float64) ** (npre - 1 - np.arange(npre))[None, :])
        .astype(np.float32) for _, b in rc]))
    am = att_masks()
    common = {"gains": gains, "qkg": qkg, "w_ret_in": f(ret_w_in[0]), "w_ret_out": f(ret_w_out[0]),
              "w_att_in": f(att_w_in[0]), "w_att_out": f(att_w_out[0]),
              "w1_0": f(mlp_w_in[0]), "w2_0": f(mlp_w_out[0]), "w1_1": f(mlp_w_in[1]), "w2_1": f(mlp_w_out[1]),
              "ident": IDENT, "cmat": att_cmat(), "rmask": rmask, "rcol": rcol, "rpre": rpre,
              "epsr": np.concatenate([np.ones((1, 128), np.float32), np.full((1, 512), EPS, np.float32)], 1).astype(NPBF),
              "aones": np.ones((128, 128), NPBF)}
    in_maps = []
    for c in range(NCORE):
        b, r = c // R, c % R
        lo = (r - 3) * SEG
        xw = np.zeros((4 * SEG, D), np.float32)
        a0 = max(lo, 0)
        xw[a0 - lo:] = x[b, a0:(r + 1) * SEG]
        pos = np.arange(lo, lo + 4 * SEG, dtype=np.float32)
        amask = np.stack([am, am]).copy()
        if r == 0:
            amask[1, :, :128] = 0
        m = dict(common)
        m.update({"xw": xw, "rtabs": ret_tables(pos), "atabs": att_tables(pos[2 * SEG:]),
                  "amask": np.ascontiguousarray(amask)})
        in_maps.append(m)
    if _only_maps:
        return in_maps
    res = _run(build_fused(), in_maps)
    out = np.empty((B, S, D), np.float32)
    for c in range(NCORE):
        out[c // R, (c % R) * SEG:(c % R + 1) * SEG] = res[c]["out"]
    return out
```

```python
import numpy as np
import ml_dtypes
import concourse.bass as bass
import concourse.mybir as mybir
from concourse.bass_utils import run_bass_kernel_spmd

F32 = mybir.dt.float32
BF16 = mybir.dt.bfloat16
AF = mybir.ActivationFunctionType
ALU = mybir.AluOpType
AX = mybir.AxisListType
NPBF = ml_dtypes.bfloat16

D = 2048
S = 8192
B = 2
NCORE = 8
T = 2048
EPS = 1e-6
RET_H, RET_DK, RET_DV = 8, 256, 512
ATT_G, ATT_H, ATT_DH = 3, 16, 128
DILS = (1, 4, 16)
DFF = 8192

ENGS = ("pe", "act", "dve", "pool", "sp")
BLK = {"pe": "tensor", "act": "scalar", "dve": "vector", "pool": "gpsimd", "sp": "sync"}
NDMA = 8
STRICT_SAME_ENGINE = True


class Buf:
    __slots__ = ("w", "re", "rd")

    def __init__(self):
        self.w = []
        self.re = {}
        self.rd = []


class Sched:
    def __init__(self):
        self.streams = {e: [] for e in ENGS}
        self.n_inst = {e: 0 for e in ENGS}
        self.needed = {e: set() for e in ENGS}
        self.seen = {e: {} for e in ENGS}
        self.dma_cnt = {}
        self.dma_rr = {e: 0 for e in ENGS}
        self.dma_last = {}

    def _wait(self, eng, t):
        if t[0] == "e":
            _, src, idx = t
            if src == eng and (eng == "pe" or not STRICT_SAME_ENGINE):
                return
            if self.seen[eng].get(src, -1) >= idx:
                return
            self.seen[eng][src] = idx
            self.needed[src].add(idx)
            self.streams[eng].append(("we", src, idx))
        else:
            _, q, i, val = t
            key = (q, i)
            if self.seen[eng].get(key, 0) >= val:
                return
            self.seen[eng][key] = val
            self.streams[eng].append(("wd", q, i, val))

    def op(self, eng, name, kw, reads=(), writes=(), dma=False, part=0, partial=None):
        fn = (name, kw)
        if name == "dma_start":
            dma = True
        deps = []
        for b in reads:
            deps.extend(b.w)
        for b in writes:
            if not (part and (partial is None or b in partial)):
                deps.extend(b.w)
            for src, idx in b.re.items():
                deps.append(("e", src, idx))
            deps.extend(b.rd)
        if dma:
            i = self.dma_rr[eng]
            self.dma_rr[eng] = (i + 1) % NDMA
            prev = self.dma_last.get((eng, i))
            if prev is not None:
                deps.append(prev)
        for t in deps:
            self._wait(eng, t)
        if dma:
            c = self.dma_cnt.get((eng, i), 0) + 16
            self.dma_cnt[(eng, i)] = c
            tk = ("d", eng, i, c)
            self.dma_last[(eng, i)] = tk
            self.streams[eng].append(("dma", fn, i))
        else:
            idx = self.n_inst[eng]
            self.n_inst[eng] = idx + 1
            tk = ("e", eng, idx)
            self.streams[eng].append(("inst", fn, idx))
        for b in writes:
            if part and (partial is None or b in partial):
                b.w = b.w + [tk]
            else:
                b.w = [tk]
            b.re = {}
            b.rd = []
        for b in reads:
            if tk[0] == "e":
                b.re[tk[1]] = tk[2]
            else:
                b.rd.append(tk)
        return tk

    def barrier(self):
        last = {e: self.n_inst[e] - 1 for e in ENGS if self.n_inst[e] > 0}
        for e in ENGS:
            for e2, idx in last.items():
                if e2 != e:
                    self._wait(e, ("e", e2, idx))
            for key in sorted(self.dma_last):
                self._wait(e, self.dma_last[key])

    def finish(self):
        for (q, i), c in sorted(self.dma_cnt.items()):
            self.streams[q].append(("wd", q, i, c))

    def emit(self, nc):
        from contextlib import ExitStack
        self.finish()
        val = {}
        for e in ENGS:
            val[e] = {idx: r + 1 for r, idx in enumerate(sorted(self.needed[e]))}
        with ExitStack() as st:
            esem = {e: st.enter_context(nc.semaphore("se_" + e)) for e in ENGS}
            dsem = {}
            for (q, i) in sorted(self.dma_cnt):
                dsem[(q, i)] = st.enter_context(nc.semaphore("sd_%s%d" % (q, i)))
            block = st.enter_context(nc.Block())
            for e in ENGS:
                items = self.streams[e]
                if not items:
                    continue

                def body(engine, e=e, items=items):
                    for it in items:
                        k = it[0]
                        if k == "inst":
                            ins = getattr(engine, it[1][0])(**it[1][1])
                            v = val[e].get(it[2])
                            if v is not None:
                                ins.then_inc(esem[e], 1)
                        elif k == "dma":
                            getattr(engine, it[1][0])(**it[1][1]).then_inc(dsem[(e, it[2])], 16)
                        elif k == "we":
                            engine.wait_ge(esem[it[1]], val[it[1]][it[2]])
                        else:
                            engine.wait_ge(dsem[(it[1], it[2])], it[3])

                getattr(block, BLK[e])(body)


class Ctx:
    def __init__(self, nc):
        self.nc = nc
        self.s = Sched()
        self.n = 0
        self.stack = None

    def sb(self, shape, dt, name=None):
        self.n += 1
        nm = name or ("t%d" % self.n)
        if self.stack is not None:
            return self.stack.enter_context(self.nc.sbuf_tensor(nm, list(shape), dt))
        return self.nc.alloc_sbuf_tensor(nm, list(shape), dt)

    def ps(self, shape, dt=F32, name=None):
        self.n += 1
        nm = name or ("p%d" % self.n)
        if self.stack is not None:
            return self.stack.enter_context(self.nc.psum_tensor(nm, list(shape), dt))
        return self.nc.alloc_psum_tensor(nm, list(shape), dt)

    def phase(self):
        return _Phase(self)

    def scratch(self, name, shape, dt):
        return self.nc.dram_tensor(name, list(shape), dt).ap()

    def din(self, name, shape, dt):
        return self.nc.dram_tensor(name, list(shape), dt, kind="ExternalInput").ap()

    def dout(self, name, shape, dt):
        return self.nc.dram_tensor(name, list(shape), dt, kind="ExternalOutput").ap()


class _Phase:
    def __init__(self, cx):
        self.cx = cx

    def __enter__(self):
        from contextlib import ExitStack
        self.prev = self.cx.stack
        self.cx.stack = ExitStack()
        return self

    def __exit__(self, *a):
        self.cx.s.barrier()
        self.cx.stack.close()
        self.cx.stack = self.prev
        return False


def load_const(cx, dram_ap, shape, dt, eng="sp"):
    t = cx.sb(shape, dt)
    b = Buf()
    out = t[:, :] if len(shape) == 2 else t[:, :, :]
    cx.s.op(eng, "dma_start", dict(out=out, in_=dram_ap), writes=[b])
    return t, b


def rsqrt_small(s, dst, src, b, scale, eps):
    s.op("dve", "tensor_scalar", dict(out=dst, in0=src, scalar1=scale, scalar2=eps,
                                      op0=ALU.mult, op1=ALU.add), reads=[b], writes=[b])
    s.op("act", "activation", dict(out=dst, in_=dst, func=AF.Sqrt), reads=[b], writes=[b])
    s.op("dve", "reciprocal", dict(out=dst, in_=dst), reads=[b], writes=[b])


class NormT:
    def __init__(self, cx, gainT_dram, ident, ident_b):
        self.cx = cx
        self.gain, self.gain_b = load_const(cx, gainT_dram, [128, 16], F32)
        self.ident, self.ident_b = ident, ident_b
        self.sq = cx.sb([128, D], BF16)
        self.sq_b = Buf()
        self.st = [cx.sb([128, 2], F32) for _ in range(2)]
        self.st_b = [Buf() for _ in range(2)]
        self.xn = [cx.sb([128, D], BF16) for _ in range(2)]
        self.xn_b = [Buf() for _ in range(2)]
        self.pt = [cx.ps([128, 1024], BF16) for _ in range(2)]
        self.pt_b = [Buf() for _ in range(2)]
        self.k = 0

    def run(self, src, src_b, XT, XT_b, t0):
        s = self.cx.s
        j = self.k % 2
        self.k += 1
        st, xn = self.st[j], self.xn[j]
        s.op("act", "activation", dict(out=self.sq[:, :], in_=src, func=AF.Square, accum_out=st[:, 0:1]),
             reads=[src_b], writes=[self.sq_b, self.st_b[j]])
        rsqrt_small(s, st[:, 1:2], st[:, 0:1], self.st_b[j], 1.0 / D, EPS)
        s.op("act", "activation", dict(out=xn[:, :], in_=src, func=AF.Copy, scale=st[:, 1:2]),
             reads=[src_b, self.st_b[j]], writes=[self.xn_b[j]])
        for half in range(2):
            pt, pt_b = self.pt[half], self.pt_b[half]
            for c in range(8):
                kc = half * 8 + c
                s.op("pe", "transpose", dict(out=pt[:, c * 128:(c + 1) * 128],
                                             in_=xn[:, kc * 128:(kc + 1) * 128], identity=self.ident[:, :]),
                     reads=[self.xn_b[j], self.ident_b], writes=[pt_b])
            s.op("dve", "tensor_tensor", dict(
                out=XT[:, half * 8:(half + 1) * 8, t0:t0 + 128],
                in0=pt[:, :].rearrange("p (c t) -> p c t", t=128),
                in1=self.gain[:, half * 8:(half + 1) * 8].unsqueeze(2).broadcast_to([128, 8, 128]),
                op=ALU.mult), reads=[pt_b, self.gain_b], writes=[XT_b], part=(t0 + half))


class WStream:
    def __init__(self, cx, nbuf=3, elems=8192):
        self.cx = cx
        self.elems = elems
        self.t = [cx.sb([128, elems], BF16) for _ in range(nbuf)]
        self.b = [Buf() for _ in range(nbuf)]
        self.k = 0

    def load(self, w_dram, r0, kc, c0, ncols):
        assert kc * ncols <= self.elems
        j = self.k % len(self.t)
        self.k += 1
        b = self.b[j]
        t = self.t[j][:, 0:kc * ncols].rearrange("p (kc n) -> p kc n", n=ncols)
        src = w_dram[r0:r0 + kc * 128, c0:c0 + ncols].rearrange("(kc p) n -> p kc n", p=128)
        step = max(1, kc // 4)
        for k0 in range(0, kc, step):
            self.cx.s.op("pool", "dma_start", dict(out=t[:, k0:k0 + step, :], in_=src[:, k0:k0 + step, :]),
                         writes=[b], part=k0)
        return t, b


def build_A(Tn=T):
    nc = bass.Bass("TRN2", target_bir_lowering=False)
    cx = Ctx(nc)
    s = cx.s
    x = cx.din("x", [Tn, D], F32)
    gain = cx.din("gain", [128, 16], F32)
    w = cx.din("w", [D, 12288], F32)
    identd = cx.din("ident", [128, 128], BF16)
    tabs = cx.din("tabs", [4, 128, Tn], F32)
    qT = cx.dout("qT", [2048, Tn], BF16)
    kT = cx.dout("kT", [2048, Tn], BF16)
    v = cx.dout("v", [Tn, 4096], BF16)
    sg = cx.dout("sg", [Tn, 4096], BF16)

    ident, ident_b = load_const(cx, identd, [128, 128], BF16)
    tab = [load_const(cx, tabs[i], [128, Tn], F32) for i in range(4)]
    XT = cx.sb([128, 16, Tn], BF16)
    XT_b = Buf()
    nt = NormT(cx, gain, ident, ident_b)
    xin = [cx.sb([128, D], F32) for _ in range(2)]
    xin_b = [Buf() for _ in range(2)]
    for tt in range(Tn // 128):
        j = tt % 2
        s.op("sp", "dma_start", dict(out=xin[j][:, :], in_=x[tt * 128:(tt + 1) * 128, :]), writes=[xin_b[j]])
        nt.run(xin[j][:, :], xin_b[j], XT, XT_b, tt * 128)

    ws = WStream(cx)
    pg = [cx.ps([128, 512], F32) for _ in range(4)]
    pg_b = [Buf() for _ in range(4)]
    tmp = [cx.sb([128, 512], F32) for _ in range(4)]
    tmp_b = [Buf() for _ in range(4)]
    ost = [cx.sb([128, 4, 512], BF16) for _ in range(2)]
    ost_b = [Buf() for _ in range(2)]
    ov = [cx.sb([128, 512], BF16) for _ in range(3)]
    ov_b = [Buf() for _ in range(3)]
    NTB = Tn // 512
    oc = vc = pk = 0
    for nb in range(24):
        wt, wb = ws.load(w, 0, 16, nb * 512, 512)
        if nb < 8:
            dst = qT if nb < 4 else kT
            (cst, cstb), (snt, sntb) = (tab[0], tab[1]) if nb < 4 else (tab[2], tab[3])
            row0 = (nb % 4) * 512
            for tb in range(NTB):
                o, o_b = ost[oc % 2], ost_b[oc % 2]
                oc += 1
                tsl = slice(tb * 512, (tb + 1) * 512)
                for pr in range(2):
                    pp = []
                    for half in range(2):
                        c = pr * 2 + half
                        p, p_b = pg[pk % 4], pg_b[pk % 4]
                        pk += 1
                        pp.append((p, p_b))
                        for kc in range(16):
                            s.op("pe", "matmul", dict(out=p[:, :], lhsT=wt[:, kc, c * 128:(c + 1) * 128],
                                                      rhs=XT[:, kc, tsl], start=(kc == 0), stop=(kc == 15)),
                                 reads=[wb, XT_b], writes=[p_b])
                    (x1, x1b), (x2, x2b) = pp
                    s.op("dve", "tensor_tensor", dict(out=tmp[0][:, :], in0=x1[:, :], in1=cst[:, tsl], op=ALU.mult),
                         reads=[x1b, cstb], writes=[tmp_b[0]])
                    s.op("dve", "tensor_tensor", dict(out=tmp[1][:, :], in0=x2[:, :], in1=snt[:, tsl], op=ALU.mult),
                         reads=[x2b, sntb], writes=[tmp_b[1]])
                    s.op("dve", "tensor_tensor", dict(out=tmp[2][:, :], in0=x2[:, :], in1=cst[:, tsl], op=ALU.mult),
                         reads=[x2b, cstb], writes=[tmp_b[2]])
                    s.op("dve", "tensor_tensor", dict(out=tmp[3][:, :], in0=x1[:, :], in1=snt[:, tsl], op=ALU.mult),
                         reads=[x1b, sntb], writes=[tmp_b[3]])
                    s.op("pool", "tensor_tensor", dict(out=o[:, pr * 2, :], in0=tmp[0][:, :], in1=tmp[1][:, :],
                                                       op=ALU.subtract),
                         reads=[tmp_b[0], tmp_b[1]], writes=[o_b], part=pr)
                    s.op("pool", "tensor_tensor", dict(out=o[:, pr * 2 + 1, :], in0=tmp[2][:, :], in1=tmp[3][:, :],
                                                       op=ALU.add),
                         reads=[tmp_b[2], tmp_b[3]], writes=[o_b], part=1)
                s.op("sp", "dma_start", dict(out=dst[row0:row0 + 512, tsl].rearrange("(c p) t -> p c t", p=128),
                                             in_=o[:, :, :]), reads=[o_b])
        else:
            isv = nb < 16
            dst = v if isv else sg
            col0 = ((nb - 8) % 8) * 512
            for tt in range(Tn // 128):
                p, p_b = pg[pk % 4], pg_b[pk % 4]
                pk += 1
                for kc in range(16):
                    s.op("pe", "matmul", dict(out=p[:, :], lhsT=XT[:, kc, tt * 128:(tt + 1) * 128], rhs=wt[:, kc, :],
                                              start=(kc == 0), stop=(kc == 15)),
                         reads=[wb, XT_b], writes=[p_b])
                o, o_b = ov[vc % 3], ov_b[vc % 3]
                vc += 1
                s.op("act", "activation", dict(out=o[:, :], in_=p[:, :], func=(AF.Copy if isv else AF.Silu)),
                     reads=[p_b], writes=[o_b])
                s.op("sp", "dma_start", dict(out=dst[tt * 128:(tt + 1) * 128, col0:col0 + 512], in_=o[:, :]),
                     reads=[o_b])
    s.emit(nc)
    return nc


def ret_tables(pos):
    inv = (1.0 / (np.float32(10000.0) ** np.linspace(0.0, 1.0, 128, dtype=np.float32))).astype(np.float32)
    ang = (pos[:, None].astype(np.float32) * inv[None, :]).astype(np.float32)
    c = np.cos(ang).astype(np.float32).T
    sn = np.sin(ang).astype(np.float32).T
    sc = np.float32(RET_DK ** -0.5)
    return np.ascontiguousarray(np.stack([c, sn, c * sc, sn * sc]).astype(np.float32))


IDENT = np.eye(128, dtype=np.float32).astype(NPBF)


def gainT(vec):
    return np.ascontiguousarray(np.asarray(vec, np.float32).reshape(16, 128).T)


def build_B(Sn=S, NU=2):
    nc = bass.Bass("TRN2", target_bir_lowering=False)
    cx = Ctx(nc)
    s = cx.s
    NCH = Sn // 128
    G = min(8, NCH)
    qT = cx.din("qT", [NU, 256, Sn], BF16)
    kT = cx.din("kT", [NU, 256, Sn], BF16)
    v = cx.din("v", [NU, Sn, 512], BF16)
    cm = cx.din("cmask", [NU, 2, 128, 128], F32)
    cc = cx.din("ccol", [NU, 128, 2], F32)
    identd = cx.din("ident", [128, 128], BF16)
    o = cx.dout("o", [NU, Sn, 512], BF16)

    ident, ident_b = load_const(cx, identd, [128, 128], BF16)
    U = []
    for u in range(NU):
        d = {}
        d["mask"], d["mask_b"] = load_const(cx, cm[u, 0], [128, 128], F32)
        d["qd"], d["qd_b"] = load_const(cx, cm[u, 1], [128, 128], F32)
        d["col"], d["col_b"] = load_const(cx, cc[u], [128, 2], F32)
        d["S"] = cx.sb([128, 2, 512], F32)
        d["S_b"] = Buf()
        d["Sbf"] = [cx.sb([128, 2, 512], BF16) for _ in range(2)]
        d["Sbf_b"] = [Buf() for _ in range(2)]
        s.op("pool", "memset", dict(ap=d["S"][:, :, :], constant=0.0), writes=[d["S_b"]])
        s.op("pool", "memset", dict(ap=d["Sbf"][0][:, :, :], constant=0.0), writes=[d["Sbf_b"][0]])
        d["q"] = [cx.sb([128, 2, G * 128], BF16) for _ in range(2)]
        d["k"] = [cx.sb([128, 2, G * 128], BF16) for _ in range(2)]
        d["v"] = [cx.sb([128, G, 512], BF16) for _ in range(2)]
        d["in_b"] = [Buf() for _ in range(2)]
        U.append(d)
    ps_s = [cx.ps([128, 512], F32) for _ in range(2)]
    ps_s_b = [Buf() for _ in range(2)]
    ps_o = [cx.ps([128, 512], F32) for _ in range(2)]
    ps_o_b = [Buf() for _ in range(2)]
    ps_k = cx.ps([128, 1024], BF16)
    ps_k_b = Buf()
    ps_st = [cx.ps([128, 512], F32) for _ in range(2)]
    ps_st_b = [Buf() for _ in range(2)]
    PT = [cx.sb([128, 128], BF16) for _ in range(2)]
    PT_b = [Buf() for _ in range(2)]
    qdT = [cx.sb([128, 2, 128], BF16) for _ in range(2)]
    qdT_b = [Buf() for _ in range(2)]
    kd = [cx.sb([128, 256], BF16) for _ in range(2)]
    kd_b = [Buf() for _ in range(2)]
    sq = cx.sb([128, 512], BF16)
    sq_b = Buf()
    st = [cx.sb([128, 2], F32) for _ in range(2)]
    st_b = [Buf() for _ in range(2)]
    ob = [cx.sb([128, 512], BF16) for _ in range(3)]
    ob_b = [Buf() for _ in range(3)]
    it = 0
    for n in range(NCH):
        for u in range(NU):
            d = U[u]
            g, gi = n // G, n % G
            gb = g % 2
            if gi == 0:
                tsl = slice(g * G * 128, (g + 1) * G * 128)
                s.op("sp", "dma_start", dict(out=d["q"][gb][:, :, :],
                                             in_=qT[u][:, tsl].rearrange("(c p) t -> p c t", p=128)),
                     writes=[d["in_b"][gb]])
                s.op("sp", "dma_start", dict(out=d["k"][gb][:, :, :],
                                             in_=kT[u][:, tsl].rearrange("(c p) t -> p c t", p=128)),
                     writes=[d["in_b"][gb]], part=1)
                s.op("sp", "dma_start", dict(out=d["v"][gb][:, :, :],
                                             in_=v[u][tsl, :].rearrange("(g p) f -> p g f", p=128)),
                     writes=[d["in_b"][gb]], part=1)
            j = it % 2
            it += 1
            inb = d["in_b"][gb]
            csl = slice(gi * 128, (gi + 1) * 128)
            qc, kc_, vc = d["q"][gb], d["k"][gb], d["v"][gb]
            sb_old, sb_old_b = d["Sbf"][n % 2], d["Sbf_b"][n % 2]
            sb_new, sb_new_b = d["Sbf"][(n + 1) % 2], d["Sbf_b"][(n + 1) % 2]
            for dc in range(2):
                s.op("pe", "matmul", dict(out=ps_s[j][:, 0:128], lhsT=kc_[:, dc, csl], rhs=qc[:, dc, csl],
                                          start=(dc == 0), stop=(dc == 1)), reads=[inb], writes=[ps_s_b[j]])
            s.op("dve", "tensor_tensor", dict(out=PT[j][:, :], in0=ps_s[j][:, 0:128], in1=d["mask"][:, :],
                                              op=ALU.mult), reads=[ps_s_b[j], d["mask_b"]], writes=[PT_b[j]])
            s.op("pool", "tensor_tensor", dict(out=qdT[j][:, :, :], in0=qc[:, :, csl],
                                               in1=d["qd"][:, :].unsqueeze(1).broadcast_to([128, 2, 128]),
                                               op=ALU.mult), reads=[inb, d["qd_b"]], writes=[qdT_b[j]])
            s.op("pe", "matmul", dict(out=ps_o[j][:, :], lhsT=PT[j][:, :], rhs=vc[:, gi, :], start=True, stop=False),
                 reads=[PT_b[j], inb], writes=[ps_o_b[j]])
            for dc in range(2):
                s.op("pe", "matmul", dict(out=ps_o[j][:, :], lhsT=qdT[j][:, dc, :], rhs=sb_old[:, dc, :],
                                          start=False, stop=(dc == 1)),
                     reads=[qdT_b[j], sb_old_b], writes=[ps_o_b[j]])
            s.op("act", "activation", dict(out=sq[:, :], in_=ps_o[j][:, :], func=AF.Square, accum_out=st[j][:, 0:1]),
                 reads=[ps_o_b[j]], writes=[sq_b, st_b[j]])
            rsqrt_small(s, st[j][:, 1:2], st[j][:, 0:1], st_b[j], 1.0 / RET_DV, EPS)
            oo, oo_b = ob[it % 3], ob_b[it % 3]
            s.op("act", "activation", dict(out=oo[:, :], in_=ps_o[j][:, :], func=AF.Copy, scale=st[j][:, 1:2]),
                 reads=[ps_o_b[j], st_b[j]], writes=[oo_b])
            s.op("sp", "dma_start", dict(out=o[u][n * 128:(n + 1) * 128, :], in_=oo[:, :]), reads=[oo_b])
            if n == NCH - 1:
                continue
            for dc in range(2):
                s.op("pe", "transpose", dict(out=ps_k[:, dc * 128:(dc + 1) * 128], in_=kc_[:, dc, csl],
                                             identity=ident[:, :]), reads=[inb, ident_b], writes=[ps_k_b])
            s.op("dve", "tensor_scalar", dict(out=kd[j][:, :], in0=ps_k[:, 0:256], scalar1=d["col"][:, 0:1],
                                              scalar2=None, op0=ALU.mult), reads=[ps_k_b, d["col_b"]],
                 writes=[kd_b[j]])
            for dc in range(2):
                s.op("pe", "matmul", dict(out=ps_st[dc][:, :], lhsT=kd[j][:, dc * 128:(dc + 1) * 128],
                                          rhs=vc[:, gi, :], start=True, stop=True),
                     reads=[kd_b[j], inb], writes=[ps_st_b[dc]])
                s.op("dve", "scalar_tensor_tensor", dict(out=d["S"][:, dc, :], in0=d["S"][:, dc, :],
                                                         scalar=d["col"][:, 1:2], in1=ps_st[dc][:, :],
                                                         op0=ALU.mult, op1=ALU.add),
                     reads=[ps_st_b[dc], d["col_b"], d["S_b"]], writes=[d["S_b"]])
                s.op("act", "copy", dict(out=sb_new[:, dc, :], in_=d["S"][:, dc, :]),
                     reads=[d["S_b"]], writes=[sb_new_b], part=dc)
    s.emit(nc)
    return nc


def ret_consts(h):
    lg = np.log(np.float32(1.0) - np.float32(2.0) ** (np.float32(-5.0) - np.float32(h))).astype(np.float32)
    idx = np.arange(128, dtype=np.float32)
    diff = idx[:, None] - idx[None, :]
    inner = np.where(diff >= 0, np.exp(lg * np.maximum(diff, 0.0)), 0.0).astype(np.float32)
    maskT = np.ascontiguousarray(inner.T)
    qd = np.exp(lg * (idx + 1.0)).astype(np.float32)
    qdb = np.broadcast_to(qd[None, :], (128, 128))
    kdec = np.exp(lg * (127.0 - idx)).astype(np.float32)
    cd = np.full(128, np.exp(lg * 128.0), np.float32)
    return (np.ascontiguousarray(np.stack([maskT, qdb]).astype(np.float32)),
            np.ascontiguousarray(np.stack([kdec, cd], axis=1).astype(np.float32)))


def build_CE(mode, Tn=T, TB=1024):
    nc = bass.Bass("TRN2", target_bir_lowering=False)
    cx = Ctx(nc)
    s = cx.s
    KM = 4096 if mode == "ret" else 2048
    resid = cx.din("resid", [Tn, D], F32)
    if mode == "ret":
        o_in = cx.din("o", [Tn, KM], BF16)
        sg_in = cx.din("sg", [Tn, KM], BF16)
    else:
        aT = cx.din("aT", [KM, Tn], BF16)
    wo_d = cx.din("wo", [KM, D], F32)
    gain = cx.din("gain", [128, 16], F32)
    w1_d = cx.din("w1", [D, DFF], F32)
    w2_d = cx.din("w2", [DFF, D], F32)
    identd = cx.din("ident", [128, 128], BF16)
    out = cx.dout("out", [Tn, D], F32)

    NT = TB // 128
    ident, ident_b = load_const(cx, identd, [128, 128], BF16)
    Y = cx.sb([128, NT, D], F32)
    Y_b = [Buf() for _ in range(NT)]
    XT = cx.sb([128, 16, TB], BF16)
    XT_b = Buf()
    nt = NormT(cx, gain, ident, ident_b)
    ws = WStream(cx)
    GT = [cx.sb([128, 4, TB], BF16) for _ in range(2)]
    GT_b = [Buf() for _ in range(2)]
    pg = [cx.ps([128, 512], F32) for _ in range(4)]
    pg_b = [Buf() for _ in range(4)]
    pk = 0
    gk = 0
    if mode == "ret":
        oin = [cx.sb([128, 512], BF16) for _ in range(2)]
        sgin = [cx.sb([128, 512], BF16) for _ in range(2)]
        in_b = [Buf() for _ in range(2)]
        prod = [cx.sb([128, 512], BF16) for _ in range(2)]
        prod_b = [Buf() for _ in range(2)]
        ptr = [cx.ps([128, 1024], BF16) for _ in range(2)]
        ptr_b = [Buf() for _ in range(2)]
        ik = 0
    rl = [cx.sb([128, 512], F32) for _ in range(2)]
    rl_b = [Buf() for _ in range(2)]
    rk = 0

    def acc_gemm(A, A_b, W, W_b, nkc):
        nonlocal pk
        for tt in range(NT):
            for nb in range(4):
                p, p_b = pg[pk % 4], pg_b[pk % 4]
                pk += 1
                for kc in range(nkc):
                    s.op("pe", "matmul", dict(out=p[:, :], lhsT=A[:, kc, tt * 128:(tt + 1) * 128],
                                              rhs=W[:, kc, nb * 512:(nb + 1) * 512],
                                              start=(kc == 0), stop=(kc == nkc - 1)),
                         reads=[A_b, W_b], writes=[p_b])
                ysl = Y[:, tt, nb * 512:(nb + 1) * 512]
                s.op("dve", "tensor_tensor", dict(out=ysl, in0=ysl, in1=p[:, :], op=ALU.add),
                     reads=[p_b, Y_b[tt]], writes=[Y_b[tt]])

    for blk in range(Tn // TB):
        tok0 = blk * TB
        for tt in range(NT):
            s.op("sp", "dma_start", dict(out=Y[:, tt, :], in_=resid[tok0 + tt * 128:tok0 + (tt + 1) * 128, :]),
                 writes=[Y_b[tt]])
        for kb in range(KM // 512):
            W, W_b = ws.load(wo_d, kb * 512, 4, 0, D)
            g, g_b = GT[gk % 2], GT_b[gk % 2]
            gk += 1
            if mode == "att":
                s.op("sp", "dma_start", dict(out=g[:, :, :], in_=aT[kb * 512:(kb + 1) * 512, tok0:tok0 + TB]
                                             .rearrange("(c p) t -> p c t", p=128)), writes=[g_b])
            else:
                for tt in range(NT):
                    j = ik % 2
                    ik += 1
                    rows = slice(tok0 + tt * 128, tok0 + (tt + 1) * 128)
                    cols = slice(kb * 512, (kb + 1) * 512)
                    s.op("sp", "dma_start", dict(out=oin[j][:, :], in_=o_in[rows, cols]), writes=[in_b[j]])
                    s.op("sp", "dma_start", dict(out=sgin[j][:, :], in_=sg_in[rows, cols]), writes=[in_b[j]], part=1)
                    s.op("pool", "tensor_tensor", dict(out=prod[j][:, :], in0=oin[j][:, :], in1=sgin[j][:, :],
                                                       op=ALU.mult), reads=[in_b[j]], writes=[prod_b[j]])
                    for c in range(4):
                        s.op("pe", "transpose", dict(out=ptr[j][:, c * 128:(c + 1) * 128],
                                                     in_=prod[j][:, c * 128:(c + 1) * 128], identity=ident[:, :]),
                             reads=[prod_b[j], ident_b], writes=[ptr_b[j]])
                    s.op("act", "copy", dict(out=g[:, :, tt * 128:(tt + 1) * 128],
                                             in_=ptr[j][:, 0:512].rearrange("p (c t) -> p c t", t=128)),
                         reads=[ptr_b[j]], writes=[g_b], part=tt)
            acc_gemm(g, g_b, W, W_b, 4)
        for tt in range(NT):
            nt.run(Y[:, tt, :], Y_b[tt], XT, XT_b, tt * 128)
        for hb in range(DFF // 512):
            W1, W1_b = ws.load(w1_d, 0, 16, hb * 512, 512)
            W2, W2_b = ws.load(w2_d, hb * 512, 4, 0, D)
            u, u_b = GT[gk % 2], GT_b[gk % 2]
            gk += 1
            first = True
            for hc in range(4):
                for t2 in range(TB // 512):
                    p, p_b = pg[pk % 4], pg_b[pk % 4]
                    pk += 1
                    for kc in range(16):
                        s.op("pe", "matmul", dict(out=p[:, :], lhsT=W1[:, kc, hc * 128:(hc + 1) * 128],
                                                  rhs=XT[:, kc, t2 * 512:(t2 + 1) * 512],
                                                  start=(kc == 0), stop=(kc == 15)),
                             reads=[W1_b, XT_b], writes=[p_b])
                    r, r_b = rl[rk % 2], rl_b[rk % 2]
                    rk += 1
                    s.op("act", "activation", dict(out=r[:, :], in_=p[:, :], func=AF.Relu), reads=[p_b], writes=[r_b])
                    s.op("pool", "tensor_tensor", dict(out=u[:, hc, t2 * 512:(t2 + 1) * 512], in0=r[:, :], in1=r[:, :],
                                                       op=ALU.mult), reads=[r_b], writes=[u_b], part=(0 if first else 1))
                    first = False
            acc_gemm(u, u_b, W2, W2_b, 4)
        for tt in range(NT):
            s.op("sp", "dma_start", dict(out=out[tok0 + tt * 128:tok0 + (tt + 1) * 128, :], in_=Y[:, tt, :]),
                 reads=[Y_b[tt]])
    s.emit(nc)
    return nc


def build_C2(Tn=T):
    nc = bass.Bass("TRN2", target_bir_lowering=False)
    cx = Ctx(nc)
    s = cx.s
    NQ = ATT_G * ATT_H * ATT_DH
    x = cx.din("x", [Tn, D], F32)
    gain = cx.din("gain", [128, 16], F32)
    w = cx.din("w", [D, 3 * NQ], F32)
    qkg = cx.din("qkg", [128, 2], F32)
    identd = cx.din("ident", [128, 128], BF16)
    cm = cx.din("cmat", [2, 128, 128], BF16)
    tabs = cx.din("tabs", [2, 32, Tn], F32)
    qT = cx.dout("qT", [NQ, Tn], BF16)
    kT = cx.dout("kT", [NQ, Tn], BF16)
    v = cx.dout("v", [Tn, NQ], BF16)

    ident, ident_b = load_const(cx, identd, [128, 128], BF16)
    onesm, onesm_b = load_const(cx, cm[0], [128, 128], BF16)
    swp, swp_b = load_const(cx, cm[1], [128, 128], BF16)
    cos32, cos_b = load_const(cx, tabs[0], [32, Tn], F32)
    sin32, sin_b = load_const(cx, tabs[1], [32, Tn], F32)
    gk, gk_b = load_const(cx, qkg, [128, 2], F32)
    s.op("dve", "tensor_scalar", dict(out=gk[:, 0:1], in0=gk[:, 0:1], scalar1=float(ATT_DH ** -0.5), scalar2=None,
                                      op0=ALU.mult), reads=[gk_b], writes=[gk_b])
    XT = cx.sb([128, 16, Tn], BF16)
    XT_b = Buf()
    nt = NormT(cx, gain, ident, ident_b)
    xin = [cx.sb([128, D], F32) for _ in range(2)]
    xin_b = [Buf() for _ in range(2)]
    for tt in range(Tn // 128):
        j = tt % 2
        s.op("sp", "dma_start", dict(out=xin[j][:, :], in_=x[tt * 128:(tt + 1) * 128, :]), writes=[xin_b[j]])
        nt.run(xin[j][:, :], xin_b[j], XT, XT_b, tt * 128)

    ws = WStream(cx)
    pg = [cx.ps([128, 512], F32) for _ in range(4)]
    pg_b = [Buf() for _ in range(4)]
    ps2 = cx.ps([128, 512], F32)
    ps2_b = Buf()
    ps3 = cx.ps([128, 512], F32)
    ps3_b = Buf()
    sqb = cx.sb([128, 4, 512], BF16)
    sqb_b = [Buf() for _ in range(4)]
    ms = cx.sb([128, 4, 512], F32)
    ms_b = Buf()
    ob = [cx.sb([128, 4, 512], BF16) for _ in range(2)]
    ob_b = [[Buf() for _ in range(4)] for _ in range(2)]
    t1 = cx.sb([32, 512], F32)
    t1_b = Buf()
    t2 = cx.sb([32, 512], F32)
    t2_b = Buf()
    ov = [cx.sb([128, 512], BF16) for _ in range(3)]
    ov_b = [Buf() for _ in range(3)]
    oc = vc = pk = 0
    for nb in range(3 * NQ // 512):
        wt, wb = ws.load(w, 0, 16, nb * 512, 512)
        if nb < 24:
            isq = nb < 12
            dst = qT if isq else kT
            gcol = gk[:, 0:1] if isq else gk[:, 1:2]
            row0 = (nb % 12) * 512
            for tb in range(Tn // 512):
                tsl = slice(tb * 512, (tb + 1) * 512)
                o, o_b = ob[oc % 2], ob_b[oc % 2]
                oc += 1
                for hd in range(4):
                    for kc in range(16):
                        s.op("pe", "matmul", dict(out=pg[hd][:, :], lhsT=wt[:, kc, hd * 128:(hd + 1) * 128],
                                                  rhs=XT[:, kc, tsl], start=(kc == 0), stop=(kc == 15)),
                             reads=[wb, XT_b], writes=[pg_b[hd]])
                for hd in range(4):
                    s.op("act", "activation", dict(out=sqb[:, hd, :], in_=pg[hd][:, :], func=AF.Square),
                         reads=[pg_b[hd]], writes=[sqb_b[hd]])
                for hd in range(4):
                    s.op("pe", "matmul", dict(out=ps2[:, :], lhsT=onesm[:, :], rhs=sqb[:, hd, :], start=True, stop=True),
                         reads=[onesm_b, sqb_b[hd]], writes=[ps2_b])
                    s.op("dve", "tensor_scalar", dict(out=ms[:, hd, :], in0=ps2[:, :], scalar1=EPS, scalar2=None,
                                                      op0=ALU.add), reads=[ps2_b], writes=[ms_b], part=hd)
                s.op("act", "activation", dict(out=ms[:, :, :], in_=ms[:, :, :], func=AF.Sqrt), reads=[ms_b], writes=[ms_b])
                s.op("dve", "reciprocal", dict(out=ms[:, :, :], in_=ms[:, :, :]), reads=[ms_b], writes=[ms_b])
                for hd in range(4):
                    s.op("dve", "scalar_tensor_tensor", dict(out=o[:, hd, :], in0=pg[hd][:, :], scalar=gcol,
                                                             in1=ms[:, hd, :], op0=ALU.mult, op1=ALU.mult),
                         reads=[pg_b[hd], ms_b, gk_b], writes=[o_b[hd]])
                for hd in range(4):
                    s.op("pe", "matmul", dict(out=ps3[0:32, :], lhsT=swp[:, 0:32], rhs=o[:, hd, :], start=True, stop=True),
                         reads=[swp_b, o_b[hd]], writes=[ps3_b])
                    s.op("dve", "tensor_tensor", dict(out=t2[:, :], in0=ps3[0:32, :], in1=sin32[:, tsl], op=ALU.mult),
                         reads=[ps3_b, sin_b], writes=[t2_b])
                    s.op("pool", "tensor_tensor", dict(out=t1[:, :], in0=o[0:32, hd, :], in1=cos32[:, tsl], op=ALU.mult),
                         reads=[o_b[hd], cos_b], writes=[t1_b])
                    s.op("pool", "tensor_tensor", dict(out=o[0:32, hd, :], in0=t1[:, :], in1=t2[:, :], op=ALU.add),
                         reads=[t1_b, t2_b], writes=[o_b[hd]])
                s.op("sp", "dma_start", dict(out=dst[row0:row0 + 512, tsl].rearrange("(c p) t -> p c t", p=128),
                                             in_=o[:, :, :]), reads=o_b)
        else:
            col0 = (nb - 24) * 512
            for tt in range(Tn // 128):
                p, p_b = pg[pk % 4], pg_b[pk % 4]
                pk += 1
                for kc in range(16):
                    s.op("pe", "matmul", dict(out=p[:, :], lhsT=XT[:, kc, tt * 128:(tt + 1) * 128], rhs=wt[:, kc, :],
                                              start=(kc == 0), stop=(kc == 15)),
                         reads=[wb, XT_b], writes=[p_b])
                oo, oo_b = ov[vc % 3], ov_b[vc % 3]
                vc += 1
                s.op("act", "activation", dict(out=oo[:, :], in_=p[:, :], func=AF.Copy), reads=[p_b], writes=[oo_b])
                s.op("sp", "dma_start", dict(out=v[tt * 128:(tt + 1) * 128, col0:col0 + 512], in_=oo[:, :]),
                     reads=[oo_b])
    s.emit(nc)
    return nc


def att_tables(pos):
    inv = (np.float32(500000.0) ** (-np.arange(0, 32, 2, dtype=np.float32) / np.float32(32))).astype(np.float32)
    ang = (pos[:, None].astype(np.float32) * inv[None, :]).astype(np.float32)
    c = np.cos(ang).astype(np.float32).T
    sn = np.sin(ang).astype(np.float32).T
    return np.ascontiguousarray(np.stack([np.concatenate([c, c]), np.concatenate([-sn, sn])]).astype(np.float32))


def att_cmat():
    ones = np.full((128, 128), 1.0 / 128.0, np.float32)
    sw = np.zeros((128, 128), np.float32)
    for i in range(16):
        sw[i + 16, i] = 1.0
        sw[i, i + 16] = 1.0
    return np.stack([ones, sw]).astype(NPBF)


def build_D(Sn=S, NU=4):
    nc = bass.Bass("TRN2", target_bir_lowering=False)
    cx = Ctx(nc)
    s = cx.s
    SB = 2048
    qp = cx.din("qp", [NU, 3, 128, Sn], BF16)
    kp = cx.din("kp", [NU, 3, 128, Sn], BF16)
    vp = cx.din("vp", [NU, 3, Sn, 128], BF16)
    mk = cx.din("mask", [128, 256], BF16)
    on = cx.din("ones", [128, 128], BF16)
    aT = cx.dout("aT", [NU, 128, Sn], BF16)

    mask, mask_b = load_const(cx, mk, [128, 256], BF16)
    ones, ones_b = load_const(cx, on, [128, 128], BF16)
    Q = [cx.sb([128, Sn], BF16) for _ in range(3)]
    Kt = [cx.sb([128, Sn], BF16) for _ in range(3)]
    V = [cx.sb([128, Sn // 128, 128], BF16) for _ in range(3)]
    in_b = [Buf() for _ in range(3)]
    ND = cx.sb([128, 2, SB], F32)
    ND_b = Buf()
    ps = [cx.ps([128, 512], F32) for _ in range(3)]
    ps_b = [Buf() for _ in range(3)]
    pnd = [cx.ps([128, 512], F32) for _ in range(3)]
    pnd_b = [Buf() for _ in range(3)]
    E = [cx.sb([128, 256], BF16) for _ in range(3)]
    E_b = [Buf() for _ in range(3)]
    PT = [cx.sb([128, 256], BF16) for _ in range(3)]
    PT_b = [Buf() for _ in range(3)]
    rec = cx.sb([128, SB], F32)
    rec_b = Buf()
    ao = [cx.sb([128, SB], BF16) for _ in range(2)]
    ao_b = [Buf() for _ in range(2)]
    it = 0
    ak = 0
    for u in range(NU):
        for g in range(3):
            s.op("sp", "dma_start", dict(out=Q[g][:, :], in_=qp[u, g]), writes=[in_b[g]])
            s.op("sp", "dma_start", dict(out=Kt[g][:, :], in_=kp[u, g]), writes=[in_b[g]], part=1)
            s.op("sp", "dma_start", dict(out=V[g][:, :, :], in_=vp[u, g].rearrange("(n p) d -> p n d", p=128)),
                 writes=[in_b[g]], part=1)
        for sb in range(Sn // SB):
            s.op("pool", "memset", dict(ap=ND[:, :, :], constant=0.0), writes=[ND_b])
            for g, dl in enumerate(DILS):
                L = Sn // dl
                nbk = (SB // dl) // 128
                for c in range(dl):
                    for n in range(nbk):
                        m0 = sb * (SB // dl) + n * 128
                        col = c * L + m0
                        has_prev = m0 >= 128
                        j = it % 3
                        it += 1
                        qs = Q[g][:, col:col + 128]
                        if has_prev:
                            s.op("pe", "matmul", dict(out=ps[j][:, 0:128], lhsT=Kt[g][:, col - 128:col], rhs=qs,
                                                      start=True, stop=True), reads=[in_b[g]], writes=[ps_b[j]])
                        s.op("pe", "matmul", dict(out=ps[j][:, 128:256], lhsT=Kt[g][:, col:col + 128], rhs=qs,
                                                  start=True, stop=True), reads=[in_b[g]], writes=[ps_b[j]])
                        lo = 0 if has_prev else 128
                        s.op("act", "activation", dict(out=E[j][:, lo:256], in_=ps[j][:, lo:256], func=AF.Exp),
                             reads=[ps_b[j]], writes=[E_b[j]])
                        s.op("pool", "tensor_tensor", dict(out=PT[j][:, lo:256], in0=E[j][:, lo:256],
                                                           in1=mask[:, lo:256], op=ALU.mult),
                             reads=[E_b[j], mask_b], writes=[PT_b[j]])
                        bi = col // 128
                        if has_prev:
                            s.op("pe", "matmul", dict(out=pnd[j][:, 0:128], lhsT=V[g][:, bi - 1, :], rhs=PT[j][:, 0:128],
                                                      start=True, stop=False), reads=[in_b[g], PT_b[j]],
                                 writes=[pnd_b[j]])
                        s.op("pe", "matmul", dict(out=pnd[j][:, 0:128], lhsT=V[g][:, bi, :], rhs=PT[j][:, 128:256],
                                                  start=(not has_prev), stop=True), reads=[in_b[g], PT_b[j]],
                             writes=[pnd_b[j]])
                        if has_prev:
                            s.op("pe", "matmul", dict(out=pnd[j][:, 128:256], lhsT=ones[:, :], rhs=PT[j][:, 0:128],
                                                      start=True, stop=False), reads=[ones_b, PT_b[j]],
                                 writes=[pnd_b[j]])
                        s.op("pe", "matmul", dict(out=pnd[j][:, 128:256], lhsT=ones[:, :], rhs=PT[j][:, 128:256],
                                                  start=(not has_prev), stop=True), reads=[ones_b, PT_b[j]],
                             writes=[pnd_b[j]])
                        t0 = n * 128 * dl + c
                        dst = ND[:, :, t0:t0 + 127 * dl + 1:dl]
                        s.op("dve", "tensor_tensor", dict(out=dst, in0=dst,
                                                          in1=pnd[j][:, 0:256].rearrange("p (a t) -> p a t", a=2),
                                                          op=ALU.add), reads=[pnd_b[j], ND_b], writes=[ND_b])
            s.op("dve", "reciprocal", dict(out=rec[:, :], in_=ND[:, 1, :]), reads=[ND_b], writes=[rec_b])
            a, a_b = ao[ak % 2], ao_b[ak % 2]
            ak += 1
            s.op("dve", "tensor_tensor", dict(out=a[:, :], in0=ND[:, 0, :], in1=rec[:, :], op=ALU.mult),
                 reads=[ND_b, rec_b], writes=[a_b])
            s.op("sp", "dma_start", dict(out=aT[u][:, sb * SB:(sb + 1) * SB], in_=a[:, :]), reads=[a_b])
    s.emit(nc)
    return nc


def att_masks():
    kk = np.arange(128)[:, None]
    qi = np.arange(128)[None, :]
    prev = (kk >= qi).astype(np.float32)
    own = (kk <= qi).astype(np.float32)
    return np.ascontiguousarray(np.concatenate([prev, own], axis=1).astype(NPBF))


MASK_NEG = -30000.0


def att_masks_add():
    m = att_masks().astype(np.float32)
    return np.ascontiguousarray(((1.0 - m) * MASK_NEG).astype(NPBF))


def to_subseq(a, dl, axis):
    a = np.moveaxis(a, axis, 0)
    n = a.shape[0]
    a = a.reshape(n // dl, dl, *a.shape[1:]).swapaxes(0, 1).reshape(n, *a.shape[1:])
    return np.ascontiguousarray(np.moveaxis(a, 0, axis))


def _run(nc, in_maps):
    res = run_bass_kernel_spmd(nc, in_maps, core_ids=list(range(NCORE)))
    return res.results


def kernel_unfused(x, norm_mix_gain, norm_mlp_gain, ret_w_in, ret_w_out, att_w_in, att_q_gain, att_k_gain,
           att_w_out, mlp_w_in, mlp_w_out):
    f = lambda a: np.ascontiguousarray(np.asarray(a, np.float32))
    x = f(x)
    norm_mix_gain, norm_mlp_gain = f(norm_mix_gain), f(norm_mlp_gain)
    cores = list(range(NCORE))
    R = S // T
    xs = [np.ascontiguousarray(x[c // R, (c % R) * T:(c % R + 1) * T]) for c in cores]
    pos = [np.arange((c % R) * T, (c % R + 1) * T, dtype=np.float32) for c in cores]

    wA = f(ret_w_in[0])
    gA = gainT(norm_mix_gain[0])
    rA = _run(build_A(), [{"x": xs[c], "gain": gA, "w": wA, "ident": IDENT, "tabs": ret_tables(pos[c])}
                          for c in cores])
    qTf = [np.concatenate([rA[b * R + r]["qT"] for r in range(R)], axis=1) for b in range(B)]
    kTf = [np.concatenate([rA[b * R + r]["kT"] for r in range(R)], axis=1) for b in range(B)]
    vf = [np.concatenate([rA[b * R + r]["v"] for r in range(R)], axis=0) for b in range(B)]

    insB = []
    for c in cores:
        us = [(2 * c + i) for i in range(2)]
        bh = [(u // RET_H, u % RET_H) for u in us]
        cms, ccs = zip(*[ret_consts(h) for _, h in bh])
        insB.append({
            "qT": np.ascontiguousarray(np.stack([qTf[b][h * 256:(h + 1) * 256] for b, h in bh])),
            "kT": np.ascontiguousarray(np.stack([kTf[b][h * 256:(h + 1) * 256] for b, h in bh])),
            "v": np.ascontiguousarray(np.stack([vf[b][:, h * 512:(h + 1) * 512] for b, h in bh])),
            "cmask": np.stack(cms), "ccol": np.stack(ccs), "ident": IDENT})
    rB = _run(build_B(), insB)

    insC = []
    for c in cores:
        b, r = c // R, c % R
        o = np.concatenate([rB[(b * RET_H + h) // 2]["o"][(b * RET_H + h) % 2][r * T:(r + 1) * T]
                            for h in range(RET_H)], axis=1)
        insC.append({"resid": xs[c], "o": np.ascontiguousarray(o), "sg": rA[c]["sg"], "wo": f(ret_w_out[0]),
                     "gain": gainT(norm_mlp_gain[0]), "w1": f(mlp_w_in[0]), "w2": f(mlp_w_out[0]), "ident": IDENT})
    rC = _run(build_CE("ret"), insC)
    h1 = [rC[c]["out"] for c in cores]

    qkg = np.ascontiguousarray(np.stack([f(att_q_gain[0]), f(att_k_gain[0])], axis=1))
    wC2 = f(att_w_in[0])
    cmat = att_cmat()
    rC2 = _run(build_C2(), [{"x": h1[c], "gain": gainT(norm_mix_gain[1]), "w": wC2, "qkg": qkg, "ident": IDENT,
                             "cmat": cmat, "tabs": att_tables(pos[c])} for c in cores])
    aq = [np.concatenate([rC2[b * R + r]["qT"] for r in range(R)], axis=1) for b in range(B)]
    ak = [np.concatenate([rC2[b * R + r]["kT"] for r in range(R)], axis=1) for b in range(B)]
    av = [np.concatenate([rC2[b * R + r]["v"] for r in range(R)], axis=0) for b in range(B)]

    insD = []
    masks = att_masks()
    ones = np.ones((128, 128), NPBF)
    for c in cores:
        b = c // R
        hs = [4 * (c % R) + i for i in range(4)]
        qp = np.stack([np.stack([to_subseq(aq[b][(g * ATT_H + h) * 128:(g * ATT_H + h + 1) * 128], dl, 1)
                                 for g, dl in enumerate(DILS)]) for h in hs])
        kp = np.stack([np.stack([to_subseq(ak[b][(g * ATT_H + h) * 128:(g * ATT_H + h + 1) * 128], dl, 1)
                                 for g, dl in enumerate(DILS)]) for h in hs])
        vp = np.stack([np.stack([to_subseq(av[b][:, (g * ATT_H + h) * 128:(g * ATT_H + h + 1) * 128], dl, 0)
                                 for g, dl in enumerate(DILS)]) for h in hs])
        insD.append({"qp": np.ascontiguousarray(qp), "kp": np.ascontiguousarray(kp), "vp": np.ascontiguousarray(vp),
                     "mask": masks, "ones": ones})
    rD = _run(build_D(), insD)

    insE = []
    for c in cores:
        b, r = c // R, c % R
        a = np.concatenate([rD[b * R + h // 4]["aT"][h % 4][:, r * T:(r + 1) * T] for h in range(ATT_H)], axis=0)
        insE.append({"resid": h1[c], "aT": np.ascontiguousarray(a), "wo": f(att_w_out[0]),
                     "gain": gainT(norm_mlp_gain[1]), "w1": f(mlp_w_in[1]), "w2": f(mlp_w_out[1]), "ident": IDENT})
    rE = _run(build_CE("att"), insE)
    out = np.empty((B, S, D), np.float32)
    for c in cores:
        out[c // R, (c % R) * T:(c % R + 1) * T] = rE[c]["out"]
    return out


SEG = 2048
NQ = ATT_G * ATT_H * ATT_DH
DBG = {}


def phase_A(cx, E, seg, full):
    s = cx.s
    w0 = seg * SEG
    with cx.phase():
        tab = [load_const(cx, E["rtabs"][i][:, w0:w0 + SEG], [128, SEG], F32) for i in range(4)]
        XT = cx.sb([128, 16, SEG], BF16)
        XT_b = Buf()
        ws = WStream(cx)
        w = E["w_ret_in"]
        nb_list = list(range(24) if full else range(4, 16))
        pre_w = ws.load(w, 0, 16, nb_list[0] * 512, 512)
        with cx.phase():
            nt = NormT(cx, E["gains"][0], E["ident"], E["ident_b"])
            xin = [cx.sb([128, D], F32) for _ in range(4)]
            xin_b = [Buf() for _ in range(4)]
            for tt in range(SEG // 128):
                j = tt % 4
                s.op("sp", "dma_start", dict(out=xin[j][:, :], in_=E["xw"][w0 + tt * 128:w0 + (tt + 1) * 128, :]),
                     writes=[xin_b[j]])
                nt.run(xin[j][:, :], xin_b[j], XT, XT_b, tt * 128)
        pg = [cx.ps([128, 512], F32) for _ in range(4)]
        pg_b = [Buf() for _ in range(4)]
        tmp8 = [cx.sb([128, 512], F32) for _ in range(8)]
        tmp8_b = [Buf() for _ in range(8)]
        tq = 0
        ost = [cx.sb([128, 4, 512], BF16) for _ in range(2)]
        ost_b = [Buf() for _ in range(2)]
        ov = [cx.sb([128, 512], BF16) for _ in range(3)]
        ov_b = [Buf() for _ in range(3)]
        oc = vc = pk = 0
        for nb in nb_list:
            wt, wb = pre_w if nb == nb_list[0] else ws.load(w, 0, 16, nb * 512, 512)
            if nb < 8:
                isq = nb < 4
                (cst, cstb), (snt, sntb) = (tab[0], tab[1]) if isq else (tab[2], tab[3])
                row0 = (nb % 4) * 512
                for tb in range(SEG // 512):
                    o, o_b = ost[oc % 2], ost_b[oc % 2]
                    oc += 1
                    tsl = slice(tb * 512, (tb + 1) * 512)
                    for pr in range(2):
                        pp = []
                        for half in range(2):
                            c = pr * 2 + half
                            p, p_b = pg[pk % 4], pg_b[pk % 4]
                            pk += 1
                            pp.append((p, p_b))
                            for kc in range(16):
                                s.op("pe", "matmul", dict(out=p[:, :], lhsT=wt[:, kc, c * 128:(c + 1) * 128],
                                                          rhs=XT[:, kc, tsl], start=(kc == 0), stop=(kc == 15)),
                                     reads=[wb, XT_b], writes=[p_b])
                        (x1, x1b), (x2, x2b) = pp
                        tmp = tmp8[(tq % 2) * 4:(tq % 2) * 4 + 4]
                        tmp_b = tmp8_b[(tq % 2) * 4:(tq % 2) * 4 + 4]
                        tq += 1
                        s.op("dve", "tensor_tensor", dict(out=tmp[0][:, :], in0=x1[:, :], in1=cst[:, tsl], op=ALU.mult),
                             reads=[x1b, cstb], writes=[tmp_b[0]])
                        s.op("dve", "tensor_tensor", dict(out=tmp[1][:, :], in0=x2[:, :], in1=snt[:, tsl], op=ALU.mult),
                             reads=[x2b, sntb], writes=[tmp_b[1]])
                        s.op("dve", "tensor_tensor", dict(out=tmp[2][:, :], in0=x2[:, :], in1=cst[:, tsl], op=ALU.mult),
                             reads=[x2b, cstb], writes=[tmp_b[2]])
                        s.op("dve", "tensor_tensor", dict(out=tmp[3][:, :], in0=x1[:, :], in1=snt[:, tsl], op=ALU.mult),
                             reads=[x1b, sntb], writes=[tmp_b[3]])
                        s.op("dve", "tensor_tensor", dict(out=o[:, pr * 2, :], in0=tmp[0][:, :], in1=tmp[1][:, :],
                                                          op=ALU.subtract),
                             reads=[tmp_b[0], tmp_b[1]], writes=[o_b], part=pr)
                        s.op("dve", "tensor_tensor", dict(out=o[:, pr * 2 + 1, :], in0=tmp[2][:, :], in1=tmp[3][:, :],
                                                          op=ALU.add),
                             reads=[tmp_b[2], tmp_b[3]], writes=[o_b], part=1)
                    if isq:
                        dst = E["qT_s"][row0:row0 + 512, (seg - 2) * SEG + tb * 512:(seg - 2) * SEG + (tb + 1) * 512]
                    else:
                        dst = E["kT_s"][row0:row0 + 512, w0 + tb * 512:w0 + (tb + 1) * 512]
                    s.op("sp", "dma_start", dict(out=dst.rearrange("(c p) t -> p c t", p=128), in_=o[:, :, :]),
                         reads=[o_b])
            else:
                isv = nb < 16
                col0 = ((nb - 8) % 8) * 512
                for tt in range(SEG // 128):
                    p, p_b = pg[pk % 4], pg_b[pk % 4]
                    pk += 1
                    for kc in range(16):
                        s.op("pe", "matmul", dict(out=p[:, :], lhsT=XT[:, kc, tt * 128:(tt + 1) * 128],
                                                  rhs=wt[:, kc, :], start=(kc == 0), stop=(kc == 15)),
                             reads=[wb, XT_b], writes=[p_b])
                    o, o_b = ov[vc % 3], ov_b[vc % 3]
                    vc += 1
                    s.op("act", "activation", dict(out=o[:, :], in_=p[:, :], func=(AF.Copy if isv else AF.Silu)),
                         reads=[p_b], writes=[o_b])
                    if isv:
                        dst = E["v_s"][w0 + tt * 128:w0 + (tt + 1) * 128, col0:col0 + 512]
                    else:
                        r0 = (seg - 2) * SEG + tt * 128
                        dst = E["sg_s"][r0:r0 + 128, col0:col0 + 512]
                    s.op("sp", "dma_start", dict(out=dst, in_=o[:, :]), reads=[o_b])


def phase_B(cx, E, NH=4, G=4):
    s = cx.s
    NCH = 4 * SEG // 128
    NPRE = 2 * SEG // 128
    with cx.phase():
        S_ = [cx.sb([128, 2, 512], F32) for _ in range(RET_H)]
        S_b = [Buf() for _ in range(RET_H)]
        Sbf = [[cx.sb([128, 2, 512], BF16) for _ in range(2)] for _ in range(RET_H)]
        Sbf_b = [[Buf() for _ in range(2)] for _ in range(RET_H)]
        with cx.phase():
            GP = 8
            if DBG.get("skip_pre"):
                raise_skip = True
            else:
                raise_skip = False
            kp = [cx.sb([128, 2, GP * 128], BF16) for _ in range(2)]
            vp = [cx.sb([128, GP, 512], BF16) for _ in range(2)]
            in_b = [Buf() for _ in range(2)]
            pre = [cx.sb([128, NPRE], F32) for _ in range(2)]
            pre_b = [Buf() for _ in range(2)]
            psS = [[cx.ps([128, 512], F32) for _ in range(2)] for _ in range(2)]
            psS_b = [[Buf() for _ in range(2)] for _ in range(2)]
            psk = [cx.ps([128, 1024], BF16) for _ in range(2)]
            psk_b = [Buf() for _ in range(2)]
            kd = [cx.sb([128, 256], BF16) for _ in range(3)]
            kd_b = [Buf() for _ in range(3)]
            steps = [(h, n) for h in range(RET_H) for n in range(NPRE)]
            lk = 0

            def pre_front(i):
                nonlocal lk
                h, n = steps[i]
                g, gi = n // GP, n % GP
                if n == 0:
                    s.op("sp", "dma_start", dict(out=pre[h % 2][:, :], in_=E["rpre"][h]), writes=[pre_b[h % 2]])
                if gi == 0:
                    gb = lk % 2
                    lk += 1
                    tsl = slice(g * GP * 128, (g + 1) * GP * 128)
                    s.op("sp", "dma_start", dict(out=kp[gb][:, :, :], in_=E["kT_s"][h * 256:(h + 1) * 256, tsl]
                                                 .rearrange("(c p) t -> p c t", p=128)), writes=[in_b[gb]])
                    s.op("sp", "dma_start", dict(out=vp[gb][:, :, :], in_=E["v_s"][tsl, h * 512:(h + 1) * 512]
                                                 .rearrange("(g p) f -> p g f", p=128)), writes=[in_b[gb]], part=1)
                gb = (h * (NPRE // GP) + g) % 2
                csl = slice(gi * 128, (gi + 1) * 128)
                for dc in range(2):
                    s.op("pe", "transpose", dict(out=psk[i % 2][:, dc * 128:(dc + 1) * 128], in_=kp[gb][:, dc, csl],
                                                 identity=E["ident"][:, :]),
                         reads=[in_b[gb], E["ident_b"]], writes=[psk_b[i % 2]])
                s.op("dve", "tensor_scalar", dict(out=kd[i % 3][:, :], in0=psk[i % 2][:, 0:256],
                                                  scalar1=(pre[h % 2][:, 0:1] if DBG.get("pre_col0") else pre[h % 2][:, n:n + 1]),
                                                  scalar2=None, op0=ALU.mult),
                     reads=[psk_b[i % 2], pre_b[h % 2]], writes=[kd_b[i % 3]])

            def pre_back(i):
                h, n = steps[i]
                g, gi = n // GP, n % GP
                gb = (h * (NPRE // GP) + g) % 2
                for dc in range(0 if DBG.get("pre_nomm") else 2):
                    s.op("pe", "matmul", dict(out=psS[h % 2][dc][:, :], lhsT=kd[i % 3][:, dc * 128:(dc + 1) * 128],
                                              rhs=vp[gb][:, gi, :], start=(n == 0), stop=(n == NPRE - 1)),
                         reads=[kd_b[i % 3], in_b[gb]], writes=[psS_b[h % 2][dc]])
                if n == NPRE - 1 and not DBG.get("pre_noevac"):
                    for dc in range(2):
                        if not DBG.get("evac_noact"):
                            s.op("act", "copy", dict(out=S_[h][:, dc, :], in_=psS[h % 2][dc][:, :]),
                                 reads=[psS_b[h % 2][dc]], writes=[S_b[h]], part=dc)
                        s.op("pool", "tensor_copy", dict(out=Sbf[h][0][:, dc, :], in_=S_[h][:, dc, :]),
                             reads=[S_b[h]], writes=[Sbf_b[h][0]], part=dc)

            for i in range(0 if raise_skip else len(steps) + 1):
                if i < len(steps):
                    pre_front(i)
                if i >= 1:
                    pre_back(i - 1)
        with cx.phase():
            _ps_s = cx.ps([128, 512], F32)
            ps_s = [_ps_s, _ps_s]
            _ps_s_b = Buf()
            ps_s_b = [_ps_s_b, _ps_s_b]
            ps_o = [cx.ps([128, 512], F32) for _ in range(NH)]
            ps_o_b = [Buf() for _ in range(NH)]
            _ps_k = cx.ps([128, 1024], BF16)
            ps_k = [_ps_k, _ps_k]
            _ps_k_b = Buf()
            ps_k_b = [_ps_k_b, _ps_k_b]
            ps_st = [cx.ps([128, 512], F32) for _ in range(2)]
            ps_st_b = [Buf() for _ in range(2)]
            stA = [cx.sb([128, 2 * NH], F32) for _ in range(3)]
            stA_b = [Buf() for _ in range(3)]
            po_sb = [[cx.sb([128, 512], F32) for _ in range(NH)] for _ in range(4)]
            po_sb_b = [[Buf() for _ in range(NH)] for _ in range(4)]
            PT = [[cx.sb([128, 128], BF16) for _ in range(NH)] for _ in range(2)]
            PT_b = [[Buf() for _ in range(NH)] for _ in range(2)]
            qdT = [[cx.sb([128, 2, 128], BF16) for _ in range(NH)] for _ in range(2)]
            qdT_b = [[Buf() for _ in range(NH)] for _ in range(2)]
            kd = [[cx.sb([128, 256], BF16) for _ in range(NH)] for _ in range(2)]
            kd_b = [[Buf() for _ in range(NH)] for _ in range(2)]
            sq = cx.sb([128, 512], BF16)
            sq_b = Buf()
            st = [cx.sb([128, 2], F32) for _ in range(NH)]
            st_b = [Buf() for _ in range(NH)]
            ob = [cx.sb([128, 512], BF16) for _ in range(4)]
            ob_b = [Buf() for _ in range(4)]
            C = []
            for h in range(RET_H):
                c = {}
                c["mask"] = cx.sb([128, 128], F32)
                c["qd"] = cx.sb([128, 128], F32)
                c["col"] = cx.sb([128, 2], F32)
                c["b"] = Buf()
                s.op("sp", "dma_start", dict(out=c["mask"][:, :], in_=E["rmask"][h, 0]), writes=[c["b"]])
                s.op("sp", "dma_start", dict(out=c["qd"][:, :], in_=E["rmask"][h, 1]), writes=[c["b"]], part=1)
                s.op("sp", "dma_start", dict(out=c["col"][:, :], in_=E["rcol"][h]), writes=[c["b"]], part=1)
                C.append(c)
            U = []
            for u in range(NH):
                d = {}
                d["q"] = [cx.sb([128, 2, G * 128], BF16) for _ in range(2)]
                d["k"] = [cx.sb([128, 2, G * 128], BF16) for _ in range(2)]
                d["v"] = [cx.sb([128, G, 512], BF16) for _ in range(2)]
                d["in_b"] = [Buf() for _ in range(2)]
                U.append(d)
            cnt = dict(oc=0, sk=0, ok=0)
            stepl = [(hp, n) for hp in range(0 if DBG.get("skip_full") else RET_H // NH) for n in range(NPRE, NCH)]

            def geom(k):
                hp, n = stepl[k]
                m = n - NPRE
                g, gi = m // G, m % G
                return hp, n, m, g, gi, g % 2, slice(gi * 128, (gi + 1) * 128), n == NCH - 1

            def stage_ab(k):
                hp, n, m, g, gi, gb, csl, last = geom(k)
                kk = k % 2
                if gi == 0:
                    tsl = slice(n * 128, (n + G) * 128)
                    qsl = slice(m * 128, (m + G) * 128)
                    for u in range(NH):
                        d = U[u]
                        h = hp * NH + u
                        s.op("sp", "dma_start", dict(out=d["k"][gb][:, :, :],
                                                     in_=E["kT_s"][h * 256:(h + 1) * 256, tsl]
                                                     .rearrange("(c p) t -> p c t", p=128)), writes=[d["in_b"][gb]])
                        s.op("sp", "dma_start", dict(out=d["v"][gb][:, :, :],
                                                     in_=E["v_s"][tsl, h * 512:(h + 1) * 512]
                                                     .rearrange("(g p) f -> p g f", p=128)),
                             writes=[d["in_b"][gb]], part=1)
                        s.op("sp", "dma_start", dict(out=d["q"][gb][:, :, :],
                                                     in_=E["qT_s"][h * 256:(h + 1) * 256, qsl]
                                                     .rearrange("(c p) t -> p c t", p=128)),
                             writes=[d["in_b"][gb]], part=1)
                for u in range(NH):
                    d = U[u]
                    inb = d["in_b"][gb]
                    qc, kc_ = d["q"][gb], d["k"][gb]
                    for dc in range(2):
                        s.op("pe", "matmul", dict(out=ps_s[kk][:, u * 128:(u + 1) * 128], lhsT=kc_[:, dc, csl],
                                                  rhs=qc[:, dc, csl], start=(dc == 0), stop=(dc == 1)),
                             reads=[inb], writes=[ps_s_b[kk]])
                    if not last:
                        for dc in range(2):
                            s.op("pe", "transpose", dict(
                                out=ps_k[kk][:, u * 256 + dc * 128:u * 256 + (dc + 1) * 128],
                                in_=kc_[:, dc, csl], identity=E["ident"][:, :]),
                                reads=[inb, E["ident_b"]], writes=[ps_k_b[kk]])
                for u in range(NH):
                    d = U[u]
                    c = C[hp * NH + u]
                    inb = d["in_b"][gb]
                    qc = d["q"][gb]
                    s.op("dve", "tensor_tensor", dict(out=PT[kk][u][:, :], in0=ps_s[kk][:, u * 128:(u + 1) * 128],
                                                      in1=c["mask"][:, :], op=ALU.mult),
                         reads=[ps_s_b[kk], c["b"]], writes=[PT_b[kk][u]])
                    s.op("pool", "tensor_tensor", dict(out=qdT[kk][u][:, :, :], in0=qc[:, :, csl],
                                                       in1=c["qd"][:, :].unsqueeze(1).broadcast_to([128, 2, 128]),
                                                       op=ALU.mult), reads=[inb, c["b"]], writes=[qdT_b[kk][u]])
                    if not last:
                        s.op("dve", "tensor_scalar", dict(out=kd[kk][u][:, :], in0=ps_k[kk][:, u * 256:(u + 1) * 256],
                                                          scalar1=c["col"][:, 0:1], scalar2=None, op0=ALU.mult),
                             reads=[ps_k_b[kk], c["b"]], writes=[kd_b[kk][u]])

            def stage_cd(k):
                hp, n, m, g, gi, gb, csl, last = geom(k)
                kk = k % 2
                for u in range(NH):
                    d = U[u]
                    h = hp * NH + u
                    c = C[h]
                    inb = d["in_b"][gb]
                    vc = d["v"][gb]
                    sb_old, sb_old_b = Sbf[h][m % 2], Sbf_b[h][m % 2]
                    sb_new, sb_new_b = Sbf[h][(m + 1) % 2], Sbf_b[h][(m + 1) % 2]
                    po, po_b = ps_o[u], ps_o_b[u]
                    s.op("pe", "matmul", dict(out=po[:, :], lhsT=PT[kk][u][:, :], rhs=vc[:, gi, :],
                                              start=True, stop=False), reads=[PT_b[kk][u], inb], writes=[po_b])
                    for dc in range(2):
                        s.op("pe", "matmul", dict(out=po[:, :], lhsT=qdT[kk][u][:, dc, :], rhs=sb_old[:, dc, :],
                                                  start=False, stop=(dc == 1)),
                             reads=[qdT_b[kk][u], sb_old_b], writes=[po_b])
                    s.op("act", "copy", dict(out=po_sb[k % 4][u][:, :], in_=po[:, :]), reads=[po_b],
                         writes=[po_sb_b[k % 4][u]])
                    if not last:
                        for dc in range(2):
                            pst, pst_b = ps_st[cnt["sk"] % 2], ps_st_b[cnt["sk"] % 2]
                            cnt["sk"] += 1
                            s.op("pe", "matmul", dict(out=pst[:, :], lhsT=kd[kk][u][:, dc * 128:(dc + 1) * 128],
                                                      rhs=vc[:, gi, :], start=True, stop=True),
                                 reads=[kd_b[kk][u], inb], writes=[pst_b])
                            s.op("dve", "scalar_tensor_tensor", dict(out=S_[h][:, dc, :], in0=S_[h][:, dc, :],
                                                                     scalar=c["col"][:, 1:2], in1=pst[:, :],
                                                                     op0=ALU.mult, op1=ALU.add),
                                 reads=[pst_b, c["b"], S_b[h]], writes=[S_b[h]])
                            s.op("act", "copy", dict(out=sb_new[:, dc, :], in_=S_[h][:, dc, :]),
                                 reads=[S_b[h]], writes=[sb_new_b], part=dc)
            def rms_act(k):
                for u in range(NH):
                    s.op("act", "activation", dict(out=sq[:, :], in_=po_sb[k % 4][u][:, :], func=AF.Square,
                                                   accum_out=stA[k % 3][:, u:u + 1]),
                         reads=[po_sb_b[k % 4][u]], writes=[sq_b, stA_b[k % 3]], part=u, partial=[stA_b[k % 3]])

            def rms_fin(k):
                hp, n, m, g, gi, gb, csl, last = geom(k)
                a, a_b = stA[k % 3], stA_b[k % 3]
                rsqrt_small(s, a[:, NH:2 * NH], a[:, 0:NH], a_b, 1.0 / RET_DV, EPS)
                for u in range(NH):
                    h = hp * NH + u
                    oo, oo_b = ob[cnt["oc"] % 4], ob_b[cnt["oc"] % 4]
                    cnt["oc"] += 1
                    s.op("dve", "tensor_scalar", dict(out=oo[:, :], in0=po_sb[k % 4][u][:, :],
                                                      scalar1=a[:, NH + u:NH + u + 1], scalar2=None, op0=ALU.mult),
                         reads=[po_sb_b[k % 4][u], a_b], writes=[oo_b])
                    s.op("sp", "dma_start", dict(out=E["o_s"][m * 128:(m + 1) * 128, h * 512:(h + 1) * 512],
                                                 in_=oo[:, :]), reads=[oo_b])

            NS_ = len(stepl)
            for k in range(NS_ + 3):
                if k < NS_:
                    stage_ab(k)
                if 0 <= k - 1 < NS_:
                    stage_cd(k - 1)
                if 0 <= k - 3 < NS_:
                    rms_fin(k - 3)
                if 0 <= k - 2 < NS_:
                    rms_act(k - 2)


def phase_CE(cx, E, mode, blocks, wo_d, gain, w1_d, w2_d, TB=1024):
    s = cx.s
    KM = 4096 if mode == "ret" else 2048
    NT = TB // 128
    with cx.phase():
        Y = cx.sb([128, NT, D], F32)
        Y_b = [Buf() for _ in range(NT)]
        XT = cx.sb([128, 16, TB], BF16)
        XT_b = Buf()
        nt = NormT(cx, gain, E["ident"], E["ident_b"])
        ws = WStream(cx)
        GT = [cx.sb([128, 4, TB], BF16) for _ in range(2)]
        GT_b = [Buf() for _ in range(2)]
        pg = [cx.ps([128, 512], F32) for _ in range(4)]
        pg_b = [Buf() for _ in range(4)]
        pk = [0]
        gk = 0
        if mode == "ret":
            oin = [cx.sb([128, 512], BF16) for _ in range(2)]
            sgin = [cx.sb([128, 512], BF16) for _ in range(2)]
            in_b = [Buf() for _ in range(2)]
            prod = [cx.sb([128, 512], BF16) for _ in range(2)]
            prod_b = [Buf() for _ in range(2)]
            ptr = [cx.ps([128, 1024], BF16) for _ in range(2)]
            ptr_b = [Buf() for _ in range(2)]
            ik = 0
        rl = [cx.sb([128, 512], F32) for _ in range(2)]
        rl_b = [Buf() for _ in range(2)]
        rk = 0

        def acc_gemm(A, A_b, W, W_b, nkc):
            for tt in range(NT):
                for nb in range(4):
                    p, p_b = pg[pk[0] % 4], pg_b[pk[0] % 4]
                    pk[0] += 1
                    for kc in range(nkc):
                        s.op("pe", "matmul", dict(out=p[:, :], lhsT=A[:, kc, tt * 128:(tt + 1) * 128],
                                                  rhs=W[:, kc, nb * 512:(nb + 1) * 512],
                                                  start=(kc == 0), stop=(kc == nkc - 1)),
                             reads=[A_b, W_b], writes=[p_b])
                    ysl = Y[:, tt, nb * 512:(nb + 1) * 512]
                    s.op("dve", "tensor_tensor", dict(out=ysl, in0=ysl, in1=p[:, :], op=ALU.add),
                         reads=[p_b, Y_b[tt]], writes=[Y_b[tt]])

        for resid, mix, out in blocks:
            for tt in range(NT):
                s.op("sp", "dma_start", dict(out=Y[:, tt, :], in_=resid[tt * 128:(tt + 1) * 128, :]), writes=[Y_b[tt]])
            for kb in range(KM // 512):
                W, W_b = ws.load(wo_d, kb * 512, 4, 0, D)
                g, g_b = GT[gk % 2], GT_b[gk % 2]
                gk += 1
                if mode == "att":
                    s.op("sp", "dma_start", dict(out=g[:, :, :], in_=mix[kb * 512:(kb + 1) * 512, :]
                                                 .rearrange("(c p) t -> p c t", p=128)), writes=[g_b])
                else:
                    o_in, sg_in = mix
                    for tt in range(NT):
                        j = ik % 2
                        ik += 1
                        rows = slice(tt * 128, (tt + 1) * 128)
                        cols = slice(kb * 512, (kb + 1) * 512)
                        s.op("sp", "dma_start", dict(out=oin[j][:, :], in_=o_in[rows, cols]), writes=[in_b[j]])
                        s.op("sp", "dma_start", dict(out=sgin[j][:, :], in_=sg_in[rows, cols]), writes=[in_b[j]], part=1)
                        s.op("pool", "tensor_tensor", dict(out=prod[j][:, :], in0=oin[j][:, :], in1=sgin[j][:, :],
                                                           op=ALU.mult), reads=[in_b[j]], writes=[prod_b[j]])
                        for c in range(4):
                            s.op("pe", "transpose", dict(out=ptr[j][:, c * 128:(c + 1) * 128],
                                                         in_=prod[j][:, c * 128:(c + 1) * 128], identity=E["ident"][:, :]),
                                 reads=[prod_b[j], E["ident_b"]], writes=[ptr_b[j]])
                        s.op("act", "copy", dict(out=g[:, :, tt * 128:(tt + 1) * 128],
                                                 in_=ptr[j][:, 0:512].rearrange("p (c t) -> p c t", t=128)),
                             reads=[ptr_b[j]], writes=[g_b], part=tt)
                acc_gemm(g, g_b, W, W_b, 4)
            for tt in range(NT):
                nt.run(Y[:, tt, :], Y_b[tt], XT, XT_b, tt * 128)
            for hb in range(DFF // 512):
                W1, W1_b = ws.load(w1_d, 0, 16, hb * 512, 512)
                W2, W2_b = ws.load(w2_d, hb * 512, 4, 0, D)
                u, u_b = GT[gk % 2], GT_b[gk % 2]
                gk += 1
                first = True
                for hc in range(4):
                    for t2 in range(TB // 512):
                        p, p_b = pg[pk[0] % 4], pg_b[pk[0] % 4]
                        pk[0] += 1
                        for kc in range(16):
                            s.op("pe", "matmul", dict(out=p[:, :], lhsT=W1[:, kc, hc * 128:(hc + 1) * 128],
                                                      rhs=XT[:, kc, t2 * 512:(t2 + 1) * 512],
                                                      start=(kc == 0), stop=(kc == 15)),
                                 reads=[W1_b, XT_b], writes=[p_b])
                        r, r_b = rl[rk % 2], rl_b[rk % 2]
                        rk += 1
                        s.op("act", "activation", dict(out=r[:, :], in_=p[:, :], func=AF.Relu), reads=[p_b], writes=[r_b])
                        s.op("pool", "tensor_tensor", dict(out=u[:, hc, t2 * 512:(t2 + 1) * 512], in0=r[:, :], in1=r[:, :],
                                                           op=ALU.mult), reads=[r_b], writes=[u_b],
                             part=(0 if first else 1))
                        first = False
                acc_gemm(u, u_b, W2, W2_b, 4)
            for tt in range(NT):
                s.op("sp", "dma_start", dict(out=out[tt * 128:(tt + 1) * 128, :], in_=Y[:, tt, :]), reads=[Y_b[tt]])


def phase_C2(cx, E, sidx):
    s = cx.s
    with cx.phase():
        cos32, cos_b = load_const(cx, E["atabs"][0][:, sidx * SEG:(sidx + 1) * SEG], [32, SEG], F32)
        sin32, sin_b = load_const(cx, E["atabs"][1][:, sidx * SEG:(sidx + 1) * SEG], [32, SEG], F32)
        gk, gk_b = load_const(cx, E["qkg"], [128, 2], F32)
        s.op("dve", "tensor_scalar", dict(out=gk[:, 0:1], in0=gk[:, 0:1], scalar1=float(ATT_DH ** -0.5),
                                          scalar2=None, op0=ALU.mult), reads=[gk_b], writes=[gk_b])
        XT = cx.sb([128, 16, SEG], BF16)
        XT_b = Buf()
        ws = WStream(cx, nbuf=2)
        w = E["w_att_in"]
        nbs = list(range(36) if sidx == 1 else range(12, 36))
        qk_nbs = [nb for nb in nbs if nb < 24]
        v_nbs = [nb for nb in nbs if nb >= 24]
        wtab = {0: ws.load(w, 0, 16, qk_nbs[0] * 512, 512)}
        with cx.phase():
            nt = NormT(cx, E["gains"][2], E["ident"], E["ident_b"])
            xin = [cx.sb([128, D], F32) for _ in range(4)]
            xin_b = [Buf() for _ in range(4)]
            for tt in range(SEG // 128):
                j = tt % 4
                r0 = sidx * SEG + tt * 128
                s.op("sp", "dma_start", dict(out=xin[j][:, :], in_=E["h1_s"][r0:r0 + 128, :]), writes=[xin_b[j]])
                nt.run(xin[j][:, :], xin_b[j], XT, XT_b, tt * 128)
        NS = 3
        pg = [[cx.ps([128, 512], F32) for _ in range(2)] for _ in range(NS)]
        pg_b = [[Buf() for _ in range(2)] for _ in range(NS)]
        ps2 = cx.ps([128, 512], F32)
        ps2_b = Buf()
        ps3 = cx.ps([128, 512], F32)
        ps3_b = Buf()
        sqb = [cx.sb([128, 2, 512], BF16) for _ in range(2)]
        sqb_b = [[Buf() for _ in range(2)] for _ in range(2)]
        ms = [cx.sb([128, 2, 512], F32) for _ in range(2)]
        ms_b = [Buf() for _ in range(2)]
        ob = [cx.sb([128, 4, SEG], BF16) for _ in range(2)]
        ob_b = [[Buf() for _ in range(4)] for _ in range(2)]
        t1 = [cx.sb([32, 512], F32) for _ in range(2)]
        t1_b = [Buf() for _ in range(2)]
        t2 = [cx.sb([32, 512], F32) for _ in range(2)]
        t2_b = [Buf() for _ in range(2)]
        ov = [cx.sb([128, 512], BF16) for _ in range(3)]
        ov_b = [Buf() for _ in range(3)]
        pairs = []
        def tbs_of(grp):
            return [SEG // 512 - 1] if (sidx == 0 and grp < 2) else list(range(SEG // 512))

        for bi, nb in enumerate(qk_nbs):
            for tb in tbs_of((nb % 12) // 4):
                for pr in range(2):
                    pairs.append((bi, nb, tb, pr))
        def getw(bi):
            if bi not in wtab and bi < len(qk_nbs):
                wtab[bi] = ws.load(w, 0, 16, qk_nbs[bi] * 512, 512)
            return wtab.get(bi)

        def ctxp(P):
            bi, nb, tb, pr = pairs[P]
            isq = nb < 12
            grp = (nb % 12) // 4
            dl = DILS[grp]
            d = dict(bi=bi, nb=nb, tb=tb, pr=pr, isq=isq, grp=grp, dl=dl, ml=512 // dl,
                     gcol=(gk[:, 0:1] if isq else gk[:, 1:2]), o=ob[bi % 2], o_b=ob_b[bi % 2],
                     tsl=slice(tb * 512, (tb + 1) * 512), st=P % NS, s2=P % 2)
            return d

        def pv(d, ap):
            return ap if d["dl"] == 1 else ap.rearrange("p (m c) -> p c m", c=d["dl"])

        def cv(d, ap):
            return ap if d["dl"] == 1 else ap.rearrange("p (c m) -> p c m", c=d["dl"])

        def ovw(d, part, hd):
            o, tb, ml = d["o"], d["tb"], d["ml"]
            if d["dl"] == 1:
                return o[part, hd, d["tsl"]]
            return o[part, hd, :].rearrange("p (c m) -> p c m", c=d["dl"])[:, :, tb * ml:(tb + 1) * ml]

        allp, lowp = slice(0, 128), slice(0, 32)

        def stage_M(P):
            d = ctxp(P)
            wt, wb = getw(d["bi"])
            for hh in range(2):
                hd = d["pr"] * 2 + hh
                p, p_b = pg[d["st"]][hh], pg_b[d["st"]][hh]
                for kc in range(16):
                    s.op("pe", "matmul", dict(out=p[:, :], lhsT=wt[:, kc, hd * 128:(hd + 1) * 128],
                                              rhs=XT[:, kc, d["tsl"]], start=(kc == 0), stop=(kc == 15)),
                         reads=[wb, XT_b], writes=[p_b])
            for hh in range(2):
                s.op("act", "activation", dict(out=sqb[d["s2"]][:, hh, :], in_=pg[d["st"]][hh][:, :], func=AF.Square),
                     reads=[pg_b[d["st"]][hh]], writes=[sqb_b[d["s2"]][hh]])
            if d["tb"] == tbs_of(d["grp"])[0] and d["pr"] == 0:
                getw(d["bi"] + 1)

        def stage_T1(P):
            d = ctxp(P)
            m_, m_b = ms[d["s2"]], ms_b[d["s2"]]
            for hh in range(2):
                s.op("pe", "matmul", dict(out=ps2[:, :], lhsT=E["onesm"][:, :], rhs=sqb[d["s2"]][:, hh, :],
                                          start=True, stop=False),
                     reads=[E["cm_b"], sqb_b[d["s2"]][hh]], writes=[ps2_b])
                s.op("pe", "matmul", dict(out=ps2[:, :], lhsT=E["epsr"][0:1, 0:128], rhs=E["epsr"][0:1, 128:640],
                                          start=False, stop=True),
                     reads=[E["cm_b"]], writes=[ps2_b])
                s.op("act", "activation", dict(out=m_[:, hh, :], in_=ps2[:, :], func=AF.Sqrt),
                     reads=[ps2_b], writes=[m_b], part=hh)

        def stage_T2(P):
            d = ctxp(P)
            m_, m_b = ms[d["s2"]], ms_b[d["s2"]]
            s.op("dve", "reciprocal", dict(out=m_[:, :, :], in_=m_[:, :, :]), reads=[m_b], writes=[m_b])
            for hh in range(2):
                hd = d["pr"] * 2 + hh
                s.op("dve", "scalar_tensor_tensor", dict(out=ovw(d, allp, hd), in0=pv(d, pg[d["st"]][hh][:, :]),
                                                         scalar=d["gcol"], in1=pv(d, m_[:, hh, :]),
                                                         op0=ALU.mult, op1=ALU.mult),
                     reads=[pg_b[d["st"]][hh], m_b, gk_b], writes=[d["o_b"][hd]],
                     part=(0 if d["tb"] == tbs_of(d["grp"])[0] else 1))

        def stage_R(P):
            d = ctxp(P)
            for hh in range(2):
                hd = d["pr"] * 2 + hh
                s.op("pe", "matmul", dict(out=ps3[0:32, :], lhsT=E["swp"][:, 0:32], rhs=ovw(d, allp, hd),
                                          start=True, stop=True),
                     reads=[E["cm_b"], d["o_b"][hd]], writes=[ps3_b])
                s.op("dve", "tensor_tensor", dict(out=cv(d, t2[hh][:, :]), in0=cv(d, ps3[0:32, :]),
                                                  in1=pv(d, sin32[:, d["tsl"]]), op=ALU.mult),
                     reads=[ps3_b, sin_b], writes=[t2_b[hh]])
                s.op("pool", "tensor_tensor", dict(out=cv(d, t1[hh][:, :]), in0=ovw(d, lowp, hd),
                                                   in1=pv(d, cos32[:, d["tsl"]]), op=ALU.mult),
                     reads=[d["o_b"][hd], cos_b], writes=[t1_b[hh]])
                s.op("pool", "tensor_tensor", dict(out=ovw(d, lowp, hd), in0=cv(d, t1[hh][:, :]),
                                                   in1=cv(d, t2[hh][:, :]), op=ALU.add),
                     reads=[t1_b[hh], t2_b[hh]], writes=[d["o_b"][hd]], part=1)
            if d["tb"] == SEG // 512 - 1 and d["pr"] == 1:
                row0 = ((d["nb"] % 12) % 4) * 512
                dst = (E["aq_s"][d["grp"]] if d["isq"] else E["ak_s"][sidx, d["grp"]])[row0:row0 + 512, :]
                dstv = dst.rearrange("(c p) t -> p c t", p=128)
                srcv = d["o"][:, :, :]
                if len(tbs_of(d["grp"])) == 1:
                    ml, tb, dl = d["ml"], d["tb"], d["dl"]
                    for hd in range(4):
                        dv = dstv[:, hd, :].rearrange("p (c m) -> p c m", c=dl)[:, :, tb * ml:(tb + 1) * ml]
                        sv = srcv[:, hd, :].rearrange("p (c m) -> p c m", c=dl)[:, :, tb * ml:(tb + 1) * ml]
                        s.op("sp", "dma_start", dict(out=dv, in_=sv), reads=[d["o_b"][hd]])
                else:
                    s.op("sp", "dma_start", dict(out=dstv, in_=srcv), reads=d["o_b"])

        NP = len(pairs)
        for i in range(NP + 2):
            if i < NP:
                stage_M(i)
            if 0 <= i - 1 < NP:
                stage_T1(i - 1)
            if 0 <= i - 2 < NP:
                stage_R(i - 2)
            if 0 <= i - 1 < NP:
                stage_T2(i - 1)
        pgv = [pg[a][b_] for a in range(NS) for b_ in range(2)]
        pgv_b = [pg_b[a][b_] for a in range(NS) for b_ in range(2)]
        vc = pk = 0
        for nb in v_nbs:
            wt, wb = ws.load(w, 0, 16, nb * 512, 512)
            col0 = (nb - 24) * 512
            vgrp = (nb - 24) // 4
            for tt in (range(SEG // 128 - 4, SEG // 128) if (sidx == 0 and vgrp < 2) else range(SEG // 128)):
                p, p_b = pgv[pk % 6], pgv_b[pk % 6]
                pk += 1
                for kc in range(16):
                    s.op("pe", "matmul", dict(out=p[:, :], lhsT=XT[:, kc, tt * 128:(tt + 1) * 128],
                                              rhs=wt[:, kc, :], start=(kc == 0), stop=(kc == 15)),
                         reads=[wb, XT_b], writes=[p_b])
                oo, oo_b = ov[vc % 3], ov_b[vc % 3]
                vc += 1
                s.op("act", "activation", dict(out=oo[:, :], in_=p[:, :], func=AF.Copy), reads=[p_b], writes=[oo_b])
                r0 = sidx * SEG + tt * 128
                s.op("sp", "dma_start", dict(out=E["av_s"][r0:r0 + 128, col0:col0 + 512], in_=oo[:, :]),
                     reads=[oo_b])


def phase_D(cx, E, LAG=6):
    s = cx.s
    with cx.phase():
        mask, mask_b = load_const(cx, E["amask"][0], [128, 256], BF16)
        maskh, maskh_b = load_const(cx, E["amask"][1], [128, 256], BF16)
        ones, ones_b = load_const(cx, E["aones"], [128, 128], BF16)
        Q = [[cx.sb([128, SEG], BF16) for _ in range(3)] for _ in range(2)]
        Kt = [[cx.sb([128, 2 * SEG], BF16) for _ in range(3)] for _ in range(2)]
        V = [[cx.sb([128, 2 * SEG // 128, 128], BF16) for _ in range(3)] for _ in range(2)]
        in_b = [[Buf() for _ in range(3)] for _ in range(2)]
        ND = [cx.sb([128, 2, SEG], F32) for _ in range(2)]
        ND_b = [Buf() for _ in range(2)]
        NPS = 8
        psb = [cx.ps([128, 512], F32) for _ in range(NPS // 2)]
        ps = [psb[k % 4][:, (k // 4) * 256:(k // 4) * 256 + 256] for k in range(NPS)]
        _b4 = [Buf() for _ in range(4)]
        ps_b = [_b4[k % 4] for k in range(NPS)]
        NPD = 6
        pndb = [cx.ps([128, 512], F32) for _ in range(NPD // 2)]
        pnd = [pndb[k % 3][:, (k // 3) * 256:(k // 3) * 256 + 256] for k in range(NPD)]
        _b3 = [Buf() for _ in range(3)]
        pnd_b = [_b3[k % 3] for k in range(NPD)]
        NX = LAG + 2
        Ex = [cx.sb([128, 256], BF16) for _ in range(NX)]
        Ex_b = [Buf() for _ in range(NX)]
        PT = [cx.sb([128, 256], BF16) for _ in range(NX)]
        PT_b = [Buf() for _ in range(NX)]
        rec = cx.sb([128, SEG], F32)
        rec_b = Buf()
        ao = [cx.sb([128, SEG], BF16) for _ in range(2)]
        ao_b = [Buf() for _ in range(2)]

        def load_head(h):
            hb = h % 2
            rows = slice(h * 128, (h + 1) * 128)
            for g, dl in enumerate(DILS):
                nbk = (SEG // dl) // 128
                b = in_b[hb][g]
                s.op("sp", "dma_start", dict(out=Q[hb][g][:, :], in_=E["aq_s"][g][rows, :]), writes=[b])
                Lseg = SEG // dl
                ml = 512 // dl
                for sg_ in range(2):
                    col0 = (g * ATT_H + h) * 128
                    src = E["av_s"][sg_ * SEG:(sg_ + 1) * SEG, col0:col0 + 128]
                    if sg_ == 0 and g < 2:
                        s.op("sp", "dma_start", dict(
                            out=Kt[hb][g][:, 0:SEG].rearrange("p (c m) -> p c m", c=dl)[:, :, Lseg - ml:Lseg],
                            in_=E["ak_s"][0, g][rows, :].rearrange("p (c m) -> p c m", c=dl)[:, :, Lseg - ml:Lseg]),
                            writes=[b], part=1)
                        for c in range(dl):
                            if dl == 1:
                                sv = src[SEG - 128:SEG, :]
                            else:
                                sv = src.rearrange("(n k c) d -> c k n d", k=128, c=dl)[c][:, nbk - 1, :]
                            s.op("sp", "dma_start", dict(out=V[hb][g][:, c * nbk + nbk - 1, :], in_=sv),
                                 writes=[b], part=1)
                        continue
                    s.op("sp", "dma_start", dict(out=Kt[hb][g][:, sg_ * SEG:(sg_ + 1) * SEG],
                                                 in_=E["ak_s"][sg_, g][rows, :]), writes=[b], part=1)
                    if dl == 1:
                        s.op("sp", "dma_start", dict(out=V[hb][g][:, sg_ * 16:(sg_ + 1) * 16, :],
                                                     in_=src.rearrange("(n k) d -> k n d", k=128)),
                             writes=[b], part=1)
                    else:
                        for c in range(dl):
                            s.op("sp", "dma_start", dict(
                                out=V[hb][g][:, sg_ * 16 + c * nbk:sg_ * 16 + (c + 1) * nbk, :],
                                in_=src.rearrange("(n k c) d -> c k n d", k=128, c=dl)[c]),
                                writes=[b], part=1)

        blocks = []
        for h in range(ATT_H):
            for g, dl in enumerate(DILS):
                Lseg = SEG // dl
                nbk = Lseg // 128
                for c in range(dl):
                    for n in range(nbk):
                        blocks.append((h, g, dl, c, n, Lseg, nbk))
        NBLK = len(blocks)
        per_head = NBLK // ATT_H

        def front(i):
            h, g, dl, c, n, Lseg, nbk = blocks[i]
            hb = h % 2
            if i % per_head == 0:
                s.op("pool", "memset", dict(ap=ND[hb][:, :, :], constant=0.0), writes=[ND_b[hb]])
            colq = c * Lseg + n * 128
            colk = SEG + colq
            colp = (colk - 128) if n > 0 else (c * Lseg + (nbk - 1) * 128)
            mk_, mk_b = (mask, mask_b) if n > 0 else (maskh, maskh_b)
            j, jx = i % NPS, i % NX
            qs = Q[hb][g][:, colq:colq + 128]
            b = in_b[hb][g]
            s.op("pe", "matmul", dict(out=ps[j][:, 0:128], lhsT=Kt[hb][g][:, colp:colp + 128], rhs=qs,
                                      start=True, stop=False), reads=[b], writes=[ps_b[j]])
            s.op("pe", "matmul", dict(out=ps[j][:, 0:128], lhsT=E["ident"][:, :], rhs=mk_[:, 0:128],
                                      start=False, stop=True), reads=[E["ident_b"], mk_b], writes=[ps_b[j]])
            s.op("pe", "matmul", dict(out=ps[j][:, 128:256], lhsT=Kt[hb][g][:, colk:colk + 128], rhs=qs,
                                      start=True, stop=False), reads=[b], writes=[ps_b[j]])
            s.op("pe", "matmul", dict(out=ps[j][:, 128:256], lhsT=E["ident"][:, :], rhs=mk_[:, 128:256],
                                      start=False, stop=True), reads=[E["ident_b"], mk_b], writes=[ps_b[j]])
            s.op("act", "activation", dict(out=PT[jx][:, :], in_=ps[j], func=AF.Exp),
                 reads=[ps_b[j]], writes=[PT_b[jx]])

        def back(i):
            h, g, dl, c, n, Lseg, nbk = blocks[i]
            hb = h % 2
            colq = c * Lseg + n * 128
            colk = SEG + colq
            colp = (colk - 128) if n > 0 else (c * Lseg + (nbk - 1) * 128)
            jx, jd = i % NX, i % NPD
            bo, bp = colk // 128, colp // 128
            b = in_b[hb][g]
            Vg = V[hb][g]
            s.op("pe", "matmul", dict(out=pnd[jd][:, 0:128], lhsT=Vg[:, bp, :], rhs=PT[jx][:, 0:128],
                                      start=True, stop=False), reads=[b, PT_b[jx]], writes=[pnd_b[jd]])
            s.op("pe", "matmul", dict(out=pnd[jd][:, 0:128], lhsT=Vg[:, bo, :], rhs=PT[jx][:, 128:256],
                                      start=False, stop=True), reads=[b, PT_b[jx]], writes=[pnd_b[jd]])
            s.op("pe", "matmul", dict(out=pnd[jd][:, 128:256], lhsT=ones[:, :], rhs=PT[jx][:, 0:128],
                                      start=True, stop=False), reads=[ones_b, PT_b[jx]], writes=[pnd_b[jd]])
            s.op("pe", "matmul", dict(out=pnd[jd][:, 128:256], lhsT=ones[:, :], rhs=PT[jx][:, 128:256],
                                      start=False, stop=True), reads=[ones_b, PT_b[jx]], writes=[pnd_b[jd]])
            t0 = n * 128 * dl + c
            dst = ND[hb][:, :, t0:t0 + 127 * dl + 1:dl]
            s.op("dve", "tensor_tensor", dict(out=dst, in0=dst, in1=pnd[jd].rearrange("p (a t) -> p a t", a=2),
                                              op=ALU.add), reads=[pnd_b[jd], ND_b[hb]], writes=[ND_b[hb]])
            if i % per_head == per_head - 1:
                rows = slice(h * 128, (h + 1) * 128)
                s.op("dve", "reciprocal", dict(out=rec[:, :], in_=ND[hb][:, 1, :]), reads=[ND_b[hb]], writes=[rec_b])
                a, a_b = ao[hb], ao_b[hb]
                s.op("dve", "tensor_tensor", dict(out=a[:, :], in0=ND[hb][:, 0, :], in1=rec[:, :], op=ALU.mult),
                     reads=[ND_b[hb], rec_b], writes=[a_b])
                s.op("sp", "dma_start", dict(out=E["a_s"][rows, :], in_=a[:, :]), reads=[a_b])

        load_head(0)
        load_head(1)
        for i in range(NBLK + LAG):
            if i < NBLK:
                front(i)
            if i - LAG >= 0:
                back(i - LAG)
                hdone = blocks[i - LAG][0]
                if (i - LAG) % per_head == per_head - 1 and hdone + 2 < ATT_H:
                    load_head(hdone + 2)


def build_fused(only=None):
    nc = bass.Bass("TRN2", target_bir_lowering=False)
    cx = Ctx(nc)
    s = cx.s
    E = {}
    E["xw"] = cx.din("xw", [4 * SEG, D], F32)
    E["rtabs"] = cx.din("rtabs", [4, 128, 4 * SEG], F32)
    E["atabs"] = cx.din("atabs", [2, 32, 2 * SEG], F32)
    E["gains"] = cx.din("gains", [4, 128, 16], F32)
    E["qkg"] = cx.din("qkg", [128, 2], F32)
    E["w_ret_in"] = cx.din("w_ret_in", [D, 12288], F32)
    E["w_ret_out"] = cx.din("w_ret_out", [4096, D], F32)
    E["w_att_in"] = cx.din("w_att_in", [D, 3 * NQ], F32)
    E["w_att_out"] = cx.din("w_att_out", [D, D], F32)
    for l in range(2):
        E["w1_%d" % l] = cx.din("w1_%d" % l, [D, DFF], F32)
        E["w2_%d" % l] = cx.din("w2_%d" % l, [DFF, D], F32)
    identd = cx.din("ident", [128, 128], BF16)
    cmat = cx.din("cmat", [2, 128, 128], BF16)
    E["rmask"] = cx.din("rmask", [RET_H, 2, 128, 128], F32)
    E["rcol"] = cx.din("rcol", [RET_H, 128, 2], F32)
    E["rpre"] = cx.din("rpre", [RET_H, 128, 2 * SEG // 128], F32)
    E["amask"] = cx.din("amask", [2, 128, 256], BF16)
    E["aones"] = cx.din("aones", [128, 128], BF16)
    out = cx.dout("out", [SEG, D], F32)
    E["kT_s"] = cx.scratch("kT_s", [2048, 4 * SEG], BF16)
    E["v_s"] = cx.scratch("v_s", [4 * SEG, 4096], BF16)
    E["qT_s"] = cx.scratch("qT_s", [2048, 2 * SEG], BF16)
    E["sg_s"] = cx.scratch("sg_s", [2 * SEG, 4096], BF16)
    E["o_s"] = cx.scratch("o_s", [2 * SEG, 4096], BF16)
    E["h1_s"] = cx.scratch("h1_s", [2 * SEG, D], F32)
    E["aq_s"] = cx.scratch("aq_s", [3, 2048, SEG], BF16)
    E["ak_s"] = cx.scratch("ak_s", [2, 3, 2048, SEG], BF16)
    E["av_s"] = cx.scratch("av_s", [2 * SEG, NQ], BF16)
    E["a_s"] = cx.scratch("a_s", [2048, SEG], BF16)
    E["ident"], E["ident_b"] = load_const(cx, identd, [128, 128], BF16)
    E["onesm"], E["cm_b"] = load_const(cx, cmat[0], [128, 128], BF16)
    E["swp"], b2 = load_const(cx, cmat[1], [128, 128], BF16)
    E["epsr"], b3 = load_const(cx, cx.din("epsr", [1, 640], BF16), [1, 640], BF16)
    s.barrier()
    steps = []
    for seg in range(4):
        steps.append(("A%d" % seg, lambda seg=seg: phase_A(cx, E, seg, full=(seg >= 2))))
    steps.append(("B", lambda: phase_B(cx, E)))
    def c1():
        blks = []
        for blk in range(4):
            r = slice(blk * 1024, (blk + 1) * 1024)
            blks.append((E["xw"][2 * SEG + blk * 1024:2 * SEG + (blk + 1) * 1024, :],
                         (E["o_s"][r, :], E["sg_s"][r, :]), E["h1_s"][r, :]))
        phase_CE(cx, E, "ret", blks, E["w_ret_out"], E["gains"][1], E["w1_0"], E["w2_0"])
    steps.append(("C1", c1))
    for sidx in range(2):
        steps.append(("C2_%d" % sidx, lambda sidx=sidx: phase_C2(cx, E, sidx)))
    steps.append(("D", lambda: phase_D(cx, E)))
    def e1():
        blks = []
        for blk in range(2):
            r = slice(blk * 1024, (blk + 1) * 1024)
            blks.append((E["h1_s"][SEG + blk * 1024:SEG + (blk + 1) * 1024, :], E["a_s"][:, r], out[r, :]))
        phase_CE(cx, E, "att", blks, E["w_att_out"], E["gains"][3], E["w1_1"], E["w2_1"])
    steps.append(("E", e1))
    for name, fn in steps:
        if only is None or name in only:
            fn()
    s.emit(nc)
    return nc


def kernel(x, norm_mix_gain, norm_mlp_gain, ret_w_in, ret_w_out, att_w_in, att_q_gain, att_k_gain,
           att_w_out, mlp_w_in, mlp_w_out, _only_maps=False):
    f = lambda a: np.ascontiguousarray(np.asarray(a, np.float32))
    x = f(x)
    norm_mix_gain, norm_mlp_gain = f(norm_mix_gain), f(norm_mlp_gain)
    R = S // SEG
    gains = np.ascontiguousarray(np.stack([gainT(norm_mix_gain[0]), gainT(norm_mlp_gain[0]),
                                           gainT(norm_mix_gain[1]), gainT(norm_mlp_gain[1])]))
    qkg = np.ascontiguousarray(np.stack([f(att_q_gain[0]), f(att_k_gain[0])], axis=1))
    rc = [ret_consts(h) for h in range(RET_H)]
    rmask = np.ascontiguousarray(np.stack([a for a, _ in rc]))
    rcol = np.ascontiguousarray(np.stack([b for _, b in rc]))
    npre = 2 * SEG // 128
    rpre = np.ascontiguousarray(np.stack([
        (b[:, 0:1].astype(np.float64) * b[0, 1].astype(np.float64) ** (npre - 1 - np.arange(npre))[None, :])
        .astype(np.float32) for _, b in rc]))
    am = att_masks_add()
    common = {"gains": gains, "qkg": qkg, "w_ret_in": f(ret_w_in[0]), "w_ret_out": f(ret_w_out[0]),
              "w_att_in": f(att_w_in[0]), "w_att_out": f(att_w_out[0]),
              "w1_0": f(mlp_w_in[0]), "w2_0": f(mlp_w_out[0]), "w1_1": f(mlp_w_in[1]), "w2_1": f(mlp_w_out[1]),
              "ident": IDENT, "cmat": att_cmat(), "rmask": rmask, "rcol": rcol, "rpre": rpre,
              "epsr": np.concatenate([np.ones((1, 128), np.float32), np.full((1, 512), EPS, np.float32)], 1).astype(NPBF),
              "aones": np.ones((128, 128), NPBF)}
    in_maps = []
    for c in range(NCORE):
        b, r = c // R, c % R
        lo = (r - 3) * SEG
        xw = np.zeros((4 * SEG, D), np.float32)
        a0 = max(lo, 0)
        xw[a0 - lo:] = x[b, a0:(r + 1) * SEG]
        pos = np.arange(lo, lo + 4 * SEG, dtype=np.float32)
        amask = np.stack([am, am]).copy()
        if r == 0:
            amask[1, :, :128] = MASK_NEG
        m = dict(common)
        m.update({"xw": xw, "rtabs": ret_tables(pos), "atabs": att_tables(pos[2 * SEG:]),
                  "amask": np.ascontiguousarray(amask)})
        in_maps.append(m)
    if _only_maps:
        return in_maps
    res = _run(build_fused(), in_maps)
    out = np.empty((B, S, D), np.float32)
    for c in range(NCORE):
        out[c // R, (c % R) * SEG:(c % R + 1) * SEG] = res[c]["out"]
    return out
```
